# Optimizing a Trainium2 kernel written in Bass

```python
import math
import jax, jax.numpy as jnp
from jax import lax
import numpy as np

D_MODEL = 2048
BATCH = 8
SEQ = 2048
DEPTH = 4
DEC_BATCH = 32
DEC_SEQ = 64
PAST_LEN = 4096

CHUNK = 64
N_EVEN = (DEPTH + 1) // 2
N_ODD = DEPTH // 2
MLA_HEADS = 8
Q_RANK = 512
KV_RANK = 512
NOPE_DIM = 128
ROPE_DIM = 64
V_DIM = 128
ROPE_THETA = 10000.0
MLA_SCALE = (NOPE_DIM + ROPE_DIM) ** -0.5
MLA_QBLK = 128
MASK_NEG = -1e30
POOL_WINDOWS = (2, 4, 8, 16)
POOL_GROUPS = len(POOL_WINDOWS)
POOL_WIDTH = D_MODEL - MLA_HEADS * V_DIM
POOL_GROUP_DIM = POOL_WIDTH // POOL_GROUPS
POOL_KEEP = max(POOL_WINDOWS) - 1
IN_EVEN = Q_RANK + KV_RANK + ROPE_DIM + POOL_WIDTH
C_HEADS = 16
HEAD_F = 128
HEAD_I = D_MODEL // C_HEADS
C_FDIM = C_HEADS * HEAD_F
C_IDIM = C_HEADS * HEAD_I
IN_ODD = 2 * C_FDIM + 2 * C_IDIM
HGRN_BLK = 32
F_MIN = 1e-30
D_FF = 5632
CONV_W = 3
EPS = 1e-6

kernel_name = "mla_pool_hgrn2_convffn_stream_step"

F32 = jnp.float32


def rmsnorm(x, g):
    xf = x.astype(F32)
    y = xf * lax.rsqrt(jnp.mean(xf * xf, axis=-1, keepdims=True) + EPS)
    return (y * g.astype(F32)).astype(x.dtype)


def rope_angles(pos):
    inv = ROPE_THETA ** (-jnp.arange(0, ROPE_DIM, 2, dtype=F32) / ROPE_DIM)
    ang = pos.astype(F32)[:, None] * inv[None, :]
    return jnp.cos(ang), jnp.sin(ang)


def apply_rope(x, cos, sin):
    xf = x.astype(F32)
    x1, x2 = jnp.split(xf, 2, axis=-1)
    return jnp.concatenate([x1 * cos - x2 * sin, x1 * sin + x2 * cos], axis=-1).astype(x.dtype)


def mla_attend(q_lat, q_pe, lat, k_pe, q_pos, k_pos):
    s = (jnp.einsum('bqhr,bkr->bhqk', q_lat, lat) +
         jnp.einsum('bqhe,bke->bhqk', q_pe, k_pe)).astype(F32) * MLA_SCALE
    visible = (k_pos[None, :] // CHUNK) <= (q_pos[:, None] // CHUNK)
    s = jnp.where(visible[None, None], s, MASK_NEG)
    p = jax.nn.softmax(s, axis=-1).astype(lat.dtype)
    return jnp.einsum('bhqk,bkr->bqhr', p, lat)


def pool_mix(z, z_past, w_pool, pool_scale):
    B, L, C = z.shape
    ext = z if z_past is None else jnp.concatenate([z_past.astype(z.dtype), z], axis=1)
    P = ext.shape[1] - L
    cs = jnp.concatenate([jnp.zeros((B, 1, C), F32), jnp.cumsum(ext.astype(F32), axis=1)], axis=1)
    hi = np.arange(P + 1, P + L + 1)
    outs = []
    for gi, w in enumerate(POOL_WINDOWS):
        lo = np.maximum(hi - w, 0)
        sl = slice(gi * POOL_GROUP_DIM, (gi + 1) * POOL_GROUP_DIM)
        cnt = jnp.asarray(hi - lo, F32)[None, :, None]
        mean = (cs[:, hi, sl] - cs[:, lo, sl]) / cnt
        outs.append(mean - z[..., sl].astype(F32))
    p = jnp.stack(outs, axis=2).astype(z.dtype)
    y = jnp.einsum('blgc,gcd->blgd', p, w_pool).reshape(B, L, C) * pool_scale
    return y, ext


def even_mixer(u, pos, lat_past, kpe_past, pool_past, w_in, g_qa, w_qb, g_kva, w_uk, w_uv,
               w_pool, pool_scale, w_out):
    B, L, _ = u.shape
    h = u @ w_in
    o1, o2, o3 = Q_RANK, Q_RANK + KV_RANK, Q_RANK + KV_RANK + ROPE_DIM
    c_q, c_kv, k_pe, z = h[..., :o1], h[..., o1:o2], h[..., o2:o3], h[..., o3:]
    q = (rmsnorm(c_q, g_qa) @ w_qb).reshape(B, L, MLA_HEADS, NOPE_DIM + ROPE_DIM)
    cos, sin = rope_angles(pos)
    q_pe = apply_rope(q[..., NOPE_DIM:], cos[:, None], sin[:, None])
    q_lat = jnp.einsum('blhd,rhd->blhr', q[..., :NOPE_DIM], w_uk)
    lat = rmsnorm(c_kv, g_kva)
    k_pe = apply_rope(k_pe, cos, sin)
    if lat_past is None:
        keys_lat, keys_pe, k_pos = lat, k_pe, pos
    else:
        keys_lat = jnp.concatenate([lat_past.astype(lat.dtype), lat], axis=1)
        keys_pe = jnp.concatenate([kpe_past.astype(k_pe.dtype), k_pe], axis=1)
        k_pos = jnp.arange(lat_past.shape[1] + L)
    if L > MLA_QBLK:
        nb = L // MLA_QBLK
        def blk(a):
            return a.reshape((B, nb, MLA_QBLK) + a.shape[2:]).swapaxes(0, 1)
        o = lax.map(lambda t: mla_attend(t[0], t[1], keys_lat, keys_pe, t[2], k_pos),
                    (blk(q_lat), blk(q_pe), pos.reshape(nb, MLA_QBLK)))
        o_lat = o.swapaxes(0, 1).reshape(B, L, MLA_HEADS, KV_RANK)
    else:
        o_lat = mla_attend(q_lat, q_pe, keys_lat, keys_pe, pos, k_pos)
    y_mla = jnp.einsum('blhr,rhd->blhd', o_lat, w_uv).reshape(B, L, MLA_HEADS * V_DIM)
    y_pool, z_ext = pool_mix(z, pool_past, w_pool, pool_scale)
    y = jnp.concatenate([y_mla, y_pool], axis=-1) @ w_out
    return y, lat, k_pe, z_ext[:, -POOL_KEEP:]


def gla_chunkwise(q, k, v, log_f, S0):
    B, L, H, Dk = q.shape
    Dv = v.shape[-1]
    blk = math.gcd(L, HGRN_BLK)
    n = L // blk
    causal = np.tril(np.ones((blk, blk), dtype=bool))

    def blocks(a):
        return a.reshape(B, n, blk, H, a.shape[-1]).swapaxes(0, 1)

    def step(S, inp):
        qb, kb, vb, gb = inp
        b = jnp.cumsum(gb, axis=1)
        o_inter = jnp.einsum('bthk,bhkv->bthv', qb * jnp.exp(b), S)
        diff = b[:, :, None] - b[:, None, :]
        decay = jnp.exp(jnp.where(causal[None, :, :, None, None], diff, MASK_NEG))
        A = jnp.einsum('bthk,btshk,bshk->bhts', qb, decay, kb)
        o_intra = jnp.einsum('bhts,bshv->bthv', A, vb)
        b_last = b[:, -1]
        S_new = jnp.exp(b_last)[..., None] * S + jnp.einsum(
            'bshk,bshv->bhkv', kb * jnp.exp(b_last[:, None] - b), vb)
        return S_new, o_inter + o_intra

    S, o = lax.scan(step, S0, (blocks(q), blocks(k), blocks(v), blocks(log_f)))
    return o.swapaxes(0, 1).reshape(B, L, H, Dv), S


def odd_mixer(u, S0, lb, w_in, g_onorm, w_out):
    B, L, _ = u.shape
    h = u @ w_in
    q = jax.nn.silu(h[..., :C_FDIM]).astype(F32)
    fz = h[..., C_FDIM:2 * C_FDIM].astype(F32)
    v = h[..., 2 * C_FDIM:2 * C_FDIM + C_IDIM].astype(F32)
    g = h[..., 2 * C_FDIM + C_IDIM:]
    f = lb + (1.0 - lb) * jax.nn.sigmoid(fz)
    log_f = jnp.log(jnp.maximum(f, F_MIN))
    k = 1.0 - f
    o, S = gla_chunkwise(q.reshape(B, L, C_HEADS, HEAD_F), k.reshape(B, L, C_HEADS, HEAD_F),
                         v.reshape(B, L, C_HEADS, HEAD_I), log_f.reshape(B, L, C_HEADS, HEAD_F),
                         S0.astype(F32))
    o = rmsnorm(o, g_onorm).reshape(B, L, C_IDIM).astype(u.dtype) * jax.nn.silu(g)
    return o @ w_out, S


def conv_ffn(u, conv_past, w_up, conv_w, conv_b, w_down):
    L = u.shape[1]
    h = u @ w_up
    ext = jnp.concatenate([conv_past.astype(h.dtype), h], axis=1)
    c = conv_b + sum(ext[:, j:j + L] * conv_w[j] for j in range(CONV_W))
    a, b = jnp.split(c, 2, axis=-1)
    return (jax.nn.silu(a) * b) @ w_down, ext[:, -(CONV_W - 1):]


def setup_inputs(seed: int = 0) -> dict:
    key = jax.random.key(seed)
    ks = jax.random.split(key, 32)
    nrm = lambda k, shape, s: jax.random.normal(k, shape, F32) * s
    return {
        "x_prompt": nrm(ks[0], (BATCH, SEQ, D_MODEL), 1.0),
        "x_sample": nrm(ks[1], (DEC_BATCH, DEC_SEQ, D_MODEL), 1.0),
        "cache_mla_latent": nrm(ks[2], (N_EVEN, DEC_BATCH, PAST_LEN, KV_RANK), 1.0),
        "cache_mla_krope": nrm(ks[3], (N_EVEN, DEC_BATCH, PAST_LEN, ROPE_DIM), 1.0),
        "state_pool": nrm(ks[4], (N_EVEN, DEC_BATCH, POOL_KEEP, POOL_WIDTH), 1.0),
        "state_hgrn": nrm(ks[5], (N_ODD, DEC_BATCH, C_HEADS, HEAD_F, HEAD_I), 0.5),
        "state_ffn_conv": nrm(ks[6], (DEPTH, DEC_BATCH, CONV_W - 1, 2 * D_FF), 1.0),
        "g_mix": 1.0 + nrm(ks[7], (DEPTH, D_MODEL), 0.05),
        "g_ffn": 1.0 + nrm(ks[8], (DEPTH, D_MODEL), 0.05),
        "g_final": 1.0 + nrm(ks[9], (D_MODEL,), 0.05),
        "w_in_a": nrm(ks[10], (N_EVEN, D_MODEL, IN_EVEN), D_MODEL ** -0.5),
        "g_qa": 1.0 + nrm(ks[11], (N_EVEN, Q_RANK), 0.05),
        "w_qb": nrm(ks[12], (N_EVEN, Q_RANK, MLA_HEADS * (NOPE_DIM + ROPE_DIM)), Q_RANK ** -0.5),
        "g_kva": 1.0 + nrm(ks[13], (N_EVEN, KV_RANK), 0.05),
        "w_uk": nrm(ks[14], (N_EVEN, KV_RANK, MLA_HEADS, NOPE_DIM), KV_RANK ** -0.5),
        "w_uv": nrm(ks[15], (N_EVEN, KV_RANK, MLA_HEADS, V_DIM), KV_RANK ** -0.5),
        "w_pool": nrm(ks[16], (N_EVEN, POOL_GROUPS, POOL_GROUP_DIM, POOL_GROUP_DIM), POOL_GROUP_DIM ** -0.5),
        "pool_scale": 1.0 + nrm(ks[17], (N_EVEN, POOL_WIDTH), 0.1),
        "w_out_a": nrm(ks[18], (N_EVEN, MLA_HEADS * V_DIM + POOL_WIDTH, D_MODEL), D_MODEL ** -0.5),
        "w_in_c": nrm(ks[19], (N_ODD, D_MODEL, IN_ODD), D_MODEL ** -0.5),
        "lb_param": nrm(ks[20], (N_ODD, C_FDIM), 0.5),
        "g_onorm": 1.0 + nrm(ks[21], (N_ODD, HEAD_I), 0.05),
        "w_out_c": nrm(ks[22], (N_ODD, C_IDIM, D_MODEL), C_IDIM ** -0.5),
        "w_up": nrm(ks[23], (DEPTH, D_MODEL, 2 * D_FF), D_MODEL ** -0.5),
        "conv_w": nrm(ks[24], (DEPTH, CONV_W, 2 * D_FF), CONV_W ** -0.5),
        "conv_b": nrm(ks[25], (DEPTH, 2 * D_FF), 0.02),
        "w_down": nrm(ks[26], (DEPTH, D_FF, D_MODEL), D_FF ** -0.5),
    }


def reference(x_prompt, x_sample, cache_mla_latent, cache_mla_krope, state_pool, state_hgrn,
              state_ffn_conv, g_mix, g_ffn, g_final, w_in_a, g_qa, w_qb, g_kva, w_uk, w_uv,
              w_pool, pool_scale, w_out_a, w_in_c, lb_param, g_onorm, w_out_c, w_up, conv_w,
              conv_b, w_down):
    Bp, Lp, _ = x_prompt.shape
    Bs, Ls, _ = x_sample.shape
    past = cache_mla_latent.shape[2]
    pos_p = jnp.arange(Lp)
    pos_s = past + jnp.arange(Ls)
    sm = jax.nn.softmax(lb_param.astype(F32), axis=0)
    lbs = jnp.clip(jnp.cumsum(sm, axis=0) - sm[0:1], 0.0, 1.0)

    xp, xs = x_prompt, x_sample
    lat_p, kpe_p, pool_p, hg_p, cv_p = [], [], [], [], []
    lat_s, kpe_s, pool_s, hg_s, cv_s = [], [], [], [], []
    for layer in range(DEPTH):
        i = layer // 2
        up, us = rmsnorm(xp, g_mix[layer]), rmsnorm(xs, g_mix[layer])
        if layer % 2 == 0:
            wa = (w_in_a[i], g_qa[i], w_qb[i], g_kva[i], w_uk[i], w_uv[i], w_pool[i], pool_scale[i], w_out_a[i])
            yp, a, b, c = even_mixer(up, pos_p, None, None, None, *wa)
            lat_p.append(a); kpe_p.append(b); pool_p.append(c)
            ys, a, b, c = even_mixer(us, pos_s, cache_mla_latent[i], cache_mla_krope[i], state_pool[i], *wa)
            lat_s.append(a); kpe_s.append(b); pool_s.append(c)
        else:
            wc = (lbs[i], w_in_c[i], g_onorm[i], w_out_c[i])
            S0 = jnp.zeros((Bp, C_HEADS, HEAD_F, HEAD_I), F32)
            yp, Sp = odd_mixer(up, S0, *wc)
            ys, Ss = odd_mixer(us, state_hgrn[i], *wc)
            hg_p.append(Sp); hg_s.append(Ss)
        xp = xp + yp
        xs = xs + ys
        wf = (w_up[layer], conv_w[layer], conv_b[layer], w_down[layer])
        fp, cp = conv_ffn(rmsnorm(xp, g_ffn[layer]), jnp.zeros((Bp, CONV_W - 1, 2 * D_FF), xp.dtype), *wf)
        fs, cs = conv_ffn(rmsnorm(xs, g_ffn[layer]), state_ffn_conv[layer], *wf)
        cv_p.append(cp); cv_s.append(cs)
        xp = xp + fp
        xs = xs + fs

    y_prompt = rmsnorm(xp, g_final)
    y_sample = rmsnorm(xs, g_final)
    return (y_prompt, y_sample,
            jnp.stack(lat_p), jnp.stack(kpe_p), jnp.stack(pool_p), jnp.stack(hg_p), jnp.stack(cv_p),
            jnp.stack(lat_s), jnp.stack(kpe_s), jnp.stack(pool_s), jnp.stack(hg_s), jnp.stack(cv_s))
```

```python
import numpy as np
import concourse.bass as bass
import concourse.mybir as mybir
from concourse.bass_utils import run_bass_kernel_spmd

F32 = mybir.dt.float32
BF16 = mybir.dt.bfloat16
AF = mybir.ActivationFunctionType
ALU = mybir.AluOpType
AX = mybir.AxisListType


class Res:
    __slots__ = ("name", "lw", "rd")

    def __init__(self, name):
        self.name = name
        self.lw = None
        self.rd = {}


class Chan:
    __slots__ = ("sem", "val")

    def __init__(self, nc, name):
        self.sem = nc.alloc_semaphore(name)
        self.val = 0


class Arena:
    def __init__(self, nc, nbytes):
        self.t = nc.alloc_sbuf_tensor("arena", [128, nbytes // 4], F32)
        self.tb = self.t.bitcast(BF16)
        self.nbytes = nbytes
        self.regs = []

    def view(self, name, off, cols, dtype):
        esz = 4 if dtype == F32 else 2
        nb = cols * esz
        assert off % 4 == 0 and off + nb <= self.nbytes, (name, off, nb, self.nbytes)
        res = Res(name)
        keep = []
        for (s, e, r) in self.regs:
            if s < off + nb and off < e:
                if r.lw is not None:
                    res.rd[("lw", id(r))] = r.lw
                for k, v in r.rd.items():
                    res.rd[(k, id(r))] = v
                if s < off:
                    keep.append((s, off, r))
                if e > off + nb:
                    keep.append((off + nb, e, r))
            else:
                keep.append((s, e, r))
        keep.append((off, off + nb, res))
        self.regs = keep
        base = self.t if dtype == F32 else self.tb
        o = off // esz
        return base[:, o:o + cols], res


class Sched:
    ENGS = ("pe", "act", "dve", "pool", "sp")

    def __init__(self, nc, sync_same_engine=False):
        self.nc = nc
        self.ops = {e: [] for e in self.ENGS}
        self.sync_same = sync_same_engine
        self.same_dist = 3
        self.out_res = []

    def _deps(self, eng, reads, writes):
        deps = []
        for r in reads:
            if r.lw is not None:
                deps.append(r.lw)
        for w in writes:
            if w.lw is not None:
                deps.append(w.lw)
            deps.extend(w.rd.values())
        out = []
        cur = len(self.ops[eng])
        for d in deps:
            if d[0] == "e" and d[1] == eng:
                if eng == "pe" or cur - d[2] > self.same_dist:
                    continue
            out.append(d)
        return out

    def _mark(self, me, reads, writes):
        key = me[1] if me[0] == "e" else id(me[1])
        for r in reads:
            r.rd[key] = me
        for w in writes:
            w.lw = me
            w.rd = {}

    def op(self, eng, fn, reads=(), writes=()):
        deps = self._deps(eng, reads, writes)
        idx = len(self.ops[eng])
        self.ops[eng].append(dict(fn=fn, deps=deps, dma=None))
        self._mark(("e", eng, idx), reads, writes)

    def dma(self, eng, dst, src, chan, reads=(), writes=()):
        deps = self._deps(eng, reads, writes)
        if chan.val > 0:
            deps.append(("d", chan, chan.val))
        chan.val += 16
        me = ("d", chan, chan.val)
        self.ops[eng].append(dict(fn=lambda e: e.dma_start(out=dst, in_=src), deps=deps, dma=chan))
        self._mark(me, reads, writes)

    def emit(self):
        nc = self.nc
        mil = {e: set() for e in self.ENGS}
        for e in self.ENGS:
            for o in self.ops[e]:
                for d in o["deps"]:
                    if d[0] == "e":
                        mil[d[1]].add(d[2])
        milidx = {}
        for e in self.ENGS:
            for k, i in enumerate(sorted(mil[e])):
                milidx[(e, i)] = k + 1
        sems = {e: nc.alloc_semaphore("c_" + e) for e in self.ENGS}
        ops = self.ops
        out_res = self.out_res

        def run(ename, eng):
            waited = {}
            for i, o in enumerate(ops[ename]):
                need = {}
                for d in o["deps"]:
                    if d[0] == "e":
                        k = ("e", d[1])
                        v = milidx[(d[1], d[2])]
                        s = sems[d[1]]
                    else:
                        k = ("d", id(d[1]))
                        v = d[2]
                        s = d[1].sem
                    if waited.get(k, 0) >= v:
                        continue
                    if k not in need or need[k][1] < v:
                        need[k] = (s, v)
                for k, (s, v) in need.items():
                    eng.wait_ge(s, v)
                    waited[k] = v
                inst = o["fn"](eng)
                if o["dma"] is not None:
                    inst.then_inc(o["dma"].sem, 16)
                elif (ename, i) in milidx:
                    inst.then_inc(sems[ename], 1)
            if ename == "sp":
                for r in out_res:
                    if r.val > 0:
                        eng.wait_ge(r.sem, r.val)

        with nc.Block() as block:
            @block.tensor
            def _(e):
                run("pe", e)

            @block.scalar
            def _(e):
                run("act", e)

            @block.vector
            def _(e):
                run("dve", e)

            @block.gpsimd
            def _(e):
                run("pool", e)

            @block.sync
            def _(e):
                run("sp", e)


D = 2048
DC = 16
DFF = 5632
FC = 44
NL = 4
T_P = 512
N_PT = 4
T_S = 256
EPS = 1e-6
MLA_SCALE = 192 ** -0.5
KBMAX = 2112
ARENA_BYTES = 91392

V_GMIX, V_GFFN, V_GFIN, V_GQA, V_GKVA, V_PSC, V_LBP, V_GON, NV = 0, 64, 128, 144, 152, 160, 176, 208, 210


class _Stop(Exception):
    pass


STOP = None
SCRATCH = True


def build_program(n_layers=NL, tile_sel=None):
    nc = bass.Bass("TRN2", target_bir_lowering=False)
    S = Sched(nc)

    def din(name, shape):
        return nc.dram_tensor(name, list(shape), F32, kind="ExternalInput").ap()

    def dout(name, shape):
        return nc.dram_tensor(name, list(shape), F32, kind="ExternalOutput").ap()

    xp = din("xp", [2048, D]); xs = din("xs", [256, D])
    c_lat = din("c_lat", [2, 4, 4096, 512]); c_kr = din("c_kr", [2, 4, 4096, 64])
    s_pool = din("s_pool", [2, 4, 15, 1024]); s_hg = din("s_hg", [2, 4, 16, 128, 128])
    s_cv = din("s_cv", [4, 4, 2, 2 * DFF])
    vec_d = din("vec", [128, NV]); cwb_d = din("cwb", [4, 128, 4 * 88])
    ident_d = din("ident", [128, 128]); rope_d = din("rope", [2, 64, 2048 + 256])
    rc_d = din("rc", [128, 64]); scanm_d = din("scanm", [128, 512]); glam_d = din("glam", [32, 512])
    w_in_a = din("w_in_a", [2, D, 2112]); w_qb = din("w_qb", [2, 512, 1536])
    w_uk = din("w_uk", [2, 512, 1024]); w_uv = din("w_uv", [2, 512, 1024])
    w_pool = din("w_pool", [2, 4, 256, 256]); w_out_a = din("w_out_a", [2, 4, 128, 8192])
    w_in_c = din("w_in_c", [2, 16, 128, 8192]); w_out_c = din("w_out_c", [2, 4, 128, 8192])
    w_up = din("w_up", [4, 22, 128, 8192]); w_down = din("w_down", [4, 16, 128, FC * 128])
    w_in_a4 = din("w_in_a4", [2, 4, 128, 8192])

    y_p = dout("y_p", [2048, D]); y_s = dout("y_s", [256, D])
    lat_p = dout("lat_p", [2, 2048, 512]); kr_p = dout("kr_p", [2, 2048, 64])
    pool_p = dout("pool_p", [2, 15, 1024]); hg_p = dout("hg_p", [2, 16, 128, 128])
    cv_p = dout("cv_p", [4, 2, 2 * DFF])
    lat_s = dout("lat_s", [2, 256, 512]); kr_s = dout("kr_s", [2, 256, 64])
    pool_s = dout("pool_s", [2, 4, 15, 1024]); hg_s = dout("hg_s", [2, 4, 16, 128, 128])
    cv_s = dout("cv_s", [4, 4, 2, 2 * DFF])
    dbg_d = dout("dbg", [128, 8192]) if STOP is not None else None

    def sb(name, shape, dt):
        return nc.alloc_sbuf_tensor(name, list(shape), dt), Res(name)

    xT, r_xT = sb("xT", [128, DC, 512], F32)
    xn, _r_xn_unused = sb("xn", [128, DC, 512], BF16)
    r_xn = [Res("xn%d" % c) for c in range(DC)]
    vec, r_vec = sb("vecs", [128, NV], F32)
    cwb, r_cwb = sb("cwbs", [128, 4 * 88], F32)
    ident_f, r_idf = sb("ident_f", [128, 128], F32)
    ident_b, r_idb = sb("ident_b", [128, 128], BF16)
    ones_b, r_ones = sb("ones_b", [128, 128], BF16)
    ropeT, r_rope = sb("ropeT", [64, 2, 512], F32)
    scanm, r_scanm = sb("scanms", [128, 512], F32)
    glam, r_glam = sb("glams", [32, 512], F32)
    rcT, r_rc = sb("rcT", [128, 64], F32)
    lbv, r_lbv = sb("lbv", [128, 2, 2, 16], F32)
    sqb = [sb(f"sqb{i}", [128, 512], BF16) for i in range(2)]
    rstd, r_rstd = sb("rstd", [128, 512], F32)
    hcv_p, r_hcvp = sb("hcv_p", [128, 4, 88, 2], F32)
    hcv_s, r_hcvs = sb("hcv_s", [128, 88, 4, 2], F32)
    hpool_p, r_hpool = sb("hpool_p", [128, 2, 8, 15], F32)
    Sst, r_Sst = sb("Sst", [128, 2, 16, 128], F32)
    stat, r_stat = sb("stat", [128, 64], F32)
    Wsl = [sb(f"wslot{i}", [128, 8192], BF16) for i in range(2)]
    arena = Arena(nc, ARENA_BYTES)

    PS = []
    for i in range(8):
        t = nc.alloc_psum_tensor(f"ps{i}", [128, 512], F32)
        PS.append((t, t.bitcast(BF16), Res(f"ps{i}")))
    st = dict(ps=0, w=0, ld=0, ldp=0, stc=0, sq=0)

    def nps():
        st["ps"] = (st["ps"] + 1) % st.get("psn", 6)
        return PS[st.get("psb", 0) + st["ps"]]

    wch = [[Chan(nc, f"w{s}_{p}") for p in range(4)] for s in range(2)]
    ldch = [Chan(nc, f"ld{i}") for i in range(6)]
    ldpch = [Chan(nc, f"ldp{i}") for i in range(4)]
    stch = [Chan(nc, f"st{i}") for i in range(8)]
    S.out_res = stch

    def ld(dst, src, reads=(), writes=()):
        st["ld"] = (st["ld"] + 1) % len(ldch)
        S.dma("sp", dst, src, ldch[st["ld"]], reads, writes)

    def ldp(dst, src, reads=(), writes=()):
        st["ldp"] = (st["ldp"] + 1) % len(ldpch)
        S.dma("pool", dst, src, ldpch[st["ldp"]], reads, writes)

    def store(dst, src, reads=(), writes=()):
        st["stc"] = (st["stc"] + 1) % len(stch)
        S.dma("sp", dst, src, stch[st["stc"]], reads, writes)

    def wslot():
        st["w"] = (st["w"] + 1) % 2
        return st["w"]

    wscr = {}
    scr_ch = [Chan(nc, f"scr{i}") for i in range(4)]

    def wload(parts, key=None, n_used=8192):
        s = wslot()
        t, r = Wsl[s]
        if key is not None and key in wscr:
            scr, rscr = wscr[key]
            S.dma("pool", t[:, 0:n_used], scr.ap(), wch[s][0], [rscr], [r])
            return t, r
        for i, (dv, src) in enumerate(parts):
            S.dma("pool", dv(t), src, wch[s][i % 4], (), [r])
        if key is not None and SCRATCH:
            scr = nc.dram_tensor("scr_" + key, [128, n_used], BF16, kind="Internal")
            rscr = Res("scr_" + key)
            wscr[key] = (scr, rscr)
            st["scr"] = (st.get("scr", 0) + 1) % len(scr_ch)
            S.dma("sp", scr.ap(), t[:, 0:n_used], scr_ch[st["scr"]], [r], [rscr])
        return t, r

    def mm(out, lhsT, rhs, start, stop, reads, writes):
        S.op("pe", lambda e: e.matmul(out, lhsT, rhs, start=start, stop=stop), reads, writes)

    def tr(out, in_, idn, reads, writes):
        S.op("pe", lambda e: e.transpose(out, in_, idn), reads, writes)

    def act(out, in_, func, reads, writes, bias=None, scale=None, accum=None):
        kw = {}
        if bias is not None:
            kw["bias"] = bias
        if scale is not None:
            kw["scale"] = scale
        if accum is not None:
            kw["accum_out"] = accum
        S.op("act", lambda e: e.activation(out, in_, func, **kw), reads, writes)

    def tt(out, a, b, op, reads, writes, eng="dve"):
        S.op(eng, lambda e: e.tensor_tensor(out, a, b, op), reads, writes)

    def ts(out, a, s1, s2, op0, op1, reads, writes, eng="dve"):
        if op1 is None:
            S.op(eng, lambda e: e.tensor_scalar(out, a, s1, None, op0), reads, writes)
        else:
            S.op(eng, lambda e: e.tensor_scalar(out, a, s1, s2, op0, op1), reads, writes)

    def stt(out, a, sc, b, op0, op1, reads, writes):
        S.op("dve", lambda e: e.scalar_tensor_tensor(out, a, sc, b, op0, op1), reads, writes)

    def cp(out, in_, reads, writes, eng="dve"):
        if eng == "act":
            S.op("act", lambda e: e.copy(out, in_), reads, writes)
        else:
            S.op(eng, lambda e: e.tensor_copy(out, in_), reads, writes)

    def rmax(out, in_, reads, writes):
        S.op("dve", lambda e: e.reduce_max(out, in_, AX.X), reads, writes)

    def recip(out, in_, reads, writes):
        S.op("dve", lambda e: e.reciprocal(out, in_), reads, writes)

    def memset(ap, val, writes, eng="dve"):
        S.op(eng, lambda e: e.memset(ap, val), (), writes)

    ld(vec[:], vec_d, (), [r_vec])
    ld(ident_f[:], ident_d, (), [r_idf])
    ld(scanm[:], scanm_d, (), [r_scanm])
    ld(glam[:], glam_d, (), [r_glam])
    ld(rcT[:], rc_d, (), [r_rc])
    cp(ident_b[:], ident_f[:], [r_idf], [r_idb])
    memset(ones_b[:], 1.0, [r_ones])
    p0 = vec[:, V_LBP:V_LBP + 16]; p1 = vec[:, V_LBP + 16:V_LBP + 32]
    sm0 = lbv[:, 0, 1, :]; sm1 = lbv[:, 1, 1, :]
    tt(sm0, p0, p1, ALU.subtract, [r_vec], [r_lbv])
    tt(sm1, p1, p0, ALU.subtract, [r_vec], [r_lbv])
    act(sm0, sm0, AF.Sigmoid, [r_lbv], [r_lbv])
    act(sm1, sm1, AF.Sigmoid, [r_lbv], [r_lbv])
    tt(lbv[:, 0, 0, :], sm0, sm0, ALU.subtract, [r_lbv], [r_lbv])
    tt(lbv[:, 1, 0, :], sm0, sm1, ALU.add, [r_lbv], [r_lbv])
    tt(lbv[:, 1, 0, :], lbv[:, 1, 0, :], sm0, ALU.subtract, [r_lbv], [r_lbv])
    for i in range(2):
        ts(lbv[:, i, 0, :], lbv[:, i, 0, :], 0.0, 1.0, ALU.max, ALU.min, [r_lbv], [r_lbv])
        ts(lbv[:, i, 1, :], lbv[:, i, 0, :], -1.0, 1.0, ALU.mult, ALU.add, [r_lbv], [r_lbv])
    memset(Sst[:], 0.0, [r_Sst])
    memset(hcv_p[:], 0.0, [r_hcvp])
    memset(hpool_p[:], 0.0, [r_hpool])

    epsT, r_eps = sb("epsT", [128, 1], F32)
    memset(epsT[:], EPS, [r_eps])

    def chk(tag, src=None, res=None, n=8192):
        if STOP in (tag, "%s@%d" % (tag, st.get("tile", -1))):
            if src is not None:
                dv, dr = arena.view("dbgbuf", 0, n, F32)
                cp(dv, src, res if isinstance(res, list) else [res], [dr])
                store(dbg_d[:, 0:n], dv, [dr], ())
            raise _Stop()

    def mk_alloc(base=0):
        off = [base]

        def A(name, cols, dt):
            v, r = arena.view(name, off[0], cols, dt)
            off[0] += (cols * (4 if dt == F32 else 2) + 31) // 32 * 32
            return v, r
        A.off = off
        return A

    def rms_stats(src_fn, nch, T, reads, nfeat):
        pt, _, pr = nps()
        for c in range(nch):
            q, qr = sqb[c % 2]
            if c % 2 == 0:
                act(q[:, :T], src_fn(c), AF.Square, reads, [qr])
            else:
                tt(q[:, :T], src_fn(c), src_fn(c), ALU.mult, reads, [qr])
            mm(pt[:, :T], ones_b[:], q[:, :T], c == 0, c == nch - 1, [qr, r_ones], [pr])
        act(rstd[:, :T], pt[:, :T], AF.Ln, [pr, r_eps], [r_rstd], bias=epsT[:, 0:1], scale=1.0 / nfeat)
        act(rstd[:, :T], rstd[:, :T], AF.Exp, [r_rstd], [r_rstd], scale=-0.5)

    def rmsnorm(src_fn, r_src, nch, nfeat, T, g_fn, dst_fn, r_dst):
        rms_stats(src_fn, nch, T, [r_src], nfeat)
        for c in range(nch):
            rd = r_dst[c] if isinstance(r_dst, list) else r_dst
            stt(dst_fn(c), src_fn(c), g_fn(c), rstd[:, :T], ALU.mult, ALU.mult, [r_src, r_vec, r_rstd], [rd])

    def load_x(tile):
        st["psn"] = 8
        st["psb"] = 0
        T = tile["T"]; nb = T // 128
        A = mk_alloc()
        stg, r_stg = A("xstage", nb * D, F32)
        stg = stg.rearrange("p (b d) -> p b d", d=D)
        src = (xp if tile["kind"] == "p" else xs)[tile["tok0"]:tile["tok0"] + T, :]
        ld(stg, src.rearrange("(b p) d -> p b d", p=128), (), [r_stg])
        for dc in range(DC):
            pt, _, pr = nps()
            for b in range(nb):
                tr(pt[:, b * 128:(b + 1) * 128], stg[:, b, dc * 128:(dc + 1) * 128], ident_f[:], [r_stg, r_idf], [pr])
            cp(xT[:, dc, :T], pt[:, :T], [pr], [r_xT], eng="act" if dc % 2 else "dve")
        ld(ropeT[:, 0, :T], rope_d[0, :, tile["ropecol"]:tile["ropecol"] + T], (), [r_rope])
        ld(ropeT[:, 1, :T], rope_d[1, :, tile["ropecol"]:tile["ropecol"] + T], (), [r_rope])

    def store_y(tile):
        T = tile["T"]; nb = T // 128
        rms_stats(lambda c: xT[:, c, :T], DC, T, [r_xT], D)
        for dc in range(DC):
            stt(xT[:, dc, :T], xT[:, dc, :T], vec[:, V_GFIN + dc:V_GFIN + dc + 1], rstd[:, :T], ALU.mult, ALU.mult,
                [r_xT, r_vec, r_rstd], [r_xT])
        A = mk_alloc()
        stg, r_stg = A("ystage", nb * D, F32)
        stg = stg.rearrange("p (b d) -> p b d", d=D)
        k = 0
        for b in range(nb):
            for d4 in range(4):
                pt, _, pr = nps()
                for j in range(4):
                    dc = d4 * 4 + j
                    tr(pt[:, j * 128:(j + 1) * 128], xT[:, dc, b * 128:(b + 1) * 128], ident_f[:], [r_xT, r_idf], [pr])
                cp(stg[:, b, d4 * 512:(d4 + 1) * 512], pt[:, :], [pr], [r_stg], eng="act" if k % 2 else "dve")
                k += 1
        dst = (y_p if tile["kind"] == "p" else y_s)[tile["tok0"]:tile["tok0"] + T, :]
        store(dst.rearrange("(b p) d -> p b d", p=128), stg, [r_stg], ())

    def ffn(tile, l):
        st["psn"] = 8
        st["psb"] = 0
        T = tile["T"]; segs = tile["segs"]; nseg = len(segs); L = segs[0]["L"]; E = L + 2
        kind = tile["kind"]
        A = mk_alloc()
        gT, r_gT = A("gT", FC * T, BF16)
        gT = gT.rearrange("p (c t) -> p c t", t=T)
        hx = [[A(f"hx{ab}{i}", nseg * E, F32) for i in range(2)] for ab in range(2)]
        t1 = [[A(f"t1{ab}{i}", T, F32) for i in range(2)] for ab in range(2)]
        sa = [A(f"sa{i}", T, F32) for i in range(2)]
        cst, r_cst = A("cvstage", 8 * 128, F32)
        cst = cst.rearrange("p (g f) -> p g f", f=128)
        rmsnorm(lambda c: xT[:, c, :T], r_xT, DC, D, T, lambda c: vec[:, V_GFFN + 16 * l + c:V_GFFN + 16 * l + c + 1],
                lambda c: xn[:, c, :T], r_xn)
        ld(cwb[:], cwb_d[l], (), [r_cwb])
        if kind == "s":
            ld(cst[:88, :, :], s_cv[l].rearrange("s t (c f) -> c (s t) f", f=128), (), [r_cst])
            for h2 in range(2):
                pt, _, pr = nps()
                for q in range(4):
                    g = h2 * 4 + q
                    tr(pt[:, q * 88:(q + 1) * 88], cst[:88, g, :], ident_f[:88, :88], [r_cst, r_idf], [pr])
                cp(hcv_s[:, :, 2 * h2:2 * h2 + 2, :].rearrange("p c s t -> p (s t) c") if False else
                   hcv_s[:, :, 2 * h2:2 * h2 + 2, :],
                   pt[:, :4 * 88].rearrange("p (s t c) -> p c s t", s=2, t=2), [pr], [r_hcvs])

        def hist_ap(ch):
            return hcv_p[:, l, ch:ch + 1, :] if kind == "p" else hcv_s[:, ch, :, :]

        r_hist = r_hcvp if kind == "p" else r_hcvs
        it = 0
        for ps_i in range(FC // 2):
            def dva(t):
                return t[:, :].rearrange("p (k n) -> p k n", n=512)[:, :, 0:256]

            def dvb(t):
                return t[:, :].rearrange("p (k n) -> p k n", n=512)[:, :, 256:512]
            wt, wr = wload([(lambda t: t[:, 0:8192], w_up[l, ps_i])], key=f"up{l}_{ps_i}")
            wv = wt[:, :].rearrange("p (k n) -> p k n", n=512)
            for jj in range(2):
                j = 2 * ps_i + jj
                buf = it % 2
                it += 1
                tv = []
                for ab in range(2):
                    ch = j + ab * FC
                    pt, _, pr = nps()
                    for kc in range(DC):
                        mm(pt[:, :T], wv[:, kc, ab * 256 + jj * 128:ab * 256 + (jj + 1) * 128], xn[:, kc, :T],
                           kc == 0, kc == DC - 1, [wr, r_xn[kc]], [pr])
                    hxt, hxr = hx[ab][buf]
                    hxv = hxt.rearrange("p (g e) -> p g e", e=E)
                    act(hxv[:, :, 2:E], pt[:, :T].rearrange("p (g l) -> p g l", l=L), AF.Copy, [pr], [hxr])
                    cp(hxv[:, :, 0:2], hist_ap(ch), [r_hist], [hxr])
                    cp(hist_ap(ch), hxv[:, :, L:L + 2], [hxr], [r_hist])
                    tt_, tr_ = t1[ab][buf]
                    t3 = tt_.rearrange("p (g l) -> p g l", l=L)
                    w0 = cwb[:, ch:ch + 1]; w1 = cwb[:, 88 + ch:88 + ch + 1]; w2 = cwb[:, 176 + ch:176 + ch + 1]
                    bb = cwb[:, 264 + ch:264 + ch + 1]
                    ts(t3, hxv[:, :, 2:E], w2, bb, ALU.mult, ALU.add, [hxr, r_cwb], [tr_])
                    stt(t3, hxv[:, :, 1:E - 1], w1, t3, ALU.mult, ALU.add, [hxr, r_cwb, tr_], [tr_])
                    stt(t3, hxv[:, :, 0:L], w0, t3, ALU.mult, ALU.add, [hxr, r_cwb, tr_], [tr_])
                    tv.append((tt_, tr_))
                sat, sar = sa[buf]
                act(sat[:, :T], tv[0][0][:, :T], AF.Silu, [tv[0][1]], [sar])
                tt(gT[:, j, :], sat[:, :T], tv[1][0][:, :T], ALU.mult, [sar, tv[1][1]], [r_gT])
        chk("f1")
        if kind == "s" or tile["last"]:
            ng = 8 if kind == "s" else 2
            for h2 in range((ng + 3) // 4):
                pt, _, pr = nps()
                nq = min(4, ng - 4 * h2)
                for q in range(nq):
                    g = h2 * 4 + q
                    src = hcv_s[:, :, g // 2, g % 2] if kind == "s" else hcv_p[:, l, :, g]
                    tr(pt[:88, q * 128:(q + 1) * 128], src, ident_f[:], [r_hist, r_idf], [pr])
                cp(cst[:88, 4 * h2:4 * h2 + nq, :], pt[:88, :nq * 128].rearrange("p (g f) -> p g f", f=128), [pr], [r_cst])
            if kind == "s":
                store(cv_s[l].rearrange("s t (c f) -> c (s t) f", f=128), cst[:88, :, :], [r_cst], ())
            else:
                store(cv_p[l].rearrange("t (c f) -> c t f", f=128), cst[:88, 0:2, :], [r_cst], ())
        for dm in range(DC):
            def dv0(t):
                return t[:, :FC * 128].rearrange("p (c n) -> p c n", n=128)[:, 0:22, :]

            def dv1(t):
                return t[:, :FC * 128].rearrange("p (c n) -> p c n", n=128)[:, 22:44, :]
            wt, wr = wload([(lambda t: t[:, 0:FC * 128], w_down[l, dm])], key=f"dn{l}_{dm}", n_used=FC * 128)
            wv = wt[:, :FC * 128].rearrange("p (c n) -> p c n", n=128)
            pt, _, pr = nps()
            for fc in range(FC):
                mm(pt[:, :T], wv[:, fc, :], gT[:, fc, :], fc == 0, fc == FC - 1, [wr, r_gT], [pr])
            tt(xT[:, dm, :T], xT[:, dm, :T], pt[:, :T], ALU.add, [r_xT, pr], [r_xT])

    def run_tile(tile):
        st["tile"] = st.get("tile", -1) + 1
        load_x(tile)
        for l in range(n_layers):
            T = tile["T"]
            rmsnorm(lambda c: xT[:, c, :T], r_xT, DC, D, T, lambda c: vec[:, V_GMIX + 16 * l + c:V_GMIX + 16 * l + c + 1],
                    lambda c: xn[:, c, :T], r_xn)
            if l % 2 == 0:
                even_mixer(tile, l // 2)
            else:
                odd_mixer(tile, l // 2)
            chk("mix%d" % l, xT[:, :, :].rearrange("p c t -> p (c t)"), r_xT)
            ffn(tile, l)
            chk("ffn%d" % l, xT[:, :, :].rearrange("p c t -> p (c t)"), r_xT)
        store_y(tile)

    r_latout = [Res("latout0"), Res("latout1")]
    r_krout = [Res("krout0"), Res("krout1")]

    def even_mixer(tile, i):
        st["psn"] = 8
        st["psb"] = 0
        T = tile["T"]; segs = tile["segs"]; nseg = len(segs); L = segs[0]["L"]; kind = tile["kind"]
        E = 15 + L
        nblk = T // 128 if kind == "p" else nseg
        bn = 128 if kind == "p" else 64
        nkt = 1536 if kind == "p" else 2048
        A = mk_alloc()
        qn, r_qn = A("qn", 8 * T, BF16); qn = qn.rearrange("p (h t) -> p h t", t=T)
        qr, r_qr = A("qr", 8 * T, BF16); qr = qr.rearrange("p (h t) -> p h t", t=T)
        KT, r_KT = A("KT", 5 * nkt, BF16); KT = KT.rearrange("p (c k) -> p c k", k=nkt)
        KVx, r_KV = A("KVx", (nkt // 128) * 576, BF16); KVx = KVx.rearrange("p (b r) -> p b r", r=576)
        nKT, r_nKT = A("nKT", 5 * T, BF16); nKT = nKT.rearrange("p (c t) -> p c t", t=T)
        nKV, r_nKV = A("nKV", nblk * 512, BF16); nKV = nKV.rearrange("p (b r) -> p b r", r=512)
        pbase = A.off[0]

        def load_keys(src_lat, src_kr, n, reads):
            nb = n // 128
            ldp(KVx[:, 0:nb, 0:512], src_lat.rearrange("(b p) r -> p b r", p=128), reads, [r_KV])
            ldp(KVx[:, 0:nb, 512:576], src_kr.rearrange("(b p) r -> p b r", p=128), reads, [r_KV])
            for j in range(nb):
                _, ptb, pr = nps()
                for rc in range(4):
                    tr(ptb[:, rc * 128:(rc + 1) * 128], KVx[:, j, rc * 128:(rc + 1) * 128], ident_b[:], [r_KV, r_idb], [pr])
                tr(ptb[:64, 512:640], KVx[:, j, 512:576], ident_b[:], [r_KV, r_idb], [pr])
                cp(KT[:, 0:4, j * 128:(j + 1) * 128], ptb[:, 0:512].rearrange("p (c k) -> p c k", k=128), [pr], [r_KT],
                   eng="act" if j % 2 else "dve")
                cp(KT[:64, 4, j * 128:(j + 1) * 128], ptb[:64, 512:640], [pr], [r_KT], eng="act" if j % 2 else "dve")

        if kind == "p" and segs[0]["n_prev"] > 0:
            n = segs[0]["n_prev"]
            load_keys(lat_p[i][0:n, :], kr_p[i][0:n, :], n, [r_latout[i], r_krout[i]])
            chk("lkV", KVx[:, :, :].rearrange("p b r -> p (b r)"), r_KV, n=12 * 576)
            chk("lkT", KT[:, :, :].rearrange("p c k -> p (c k)"), r_KT, n=5 * 1536)

        B = mk_alloc(pbase)
        cq, r_cq = B("cq", 4 * T, F32); cq = cq.rearrange("p (c t) -> p c t", t=T)
        cqn, r_cqn = B("cqn", 4 * T, BF16); cqn = cqn.rearrange("p (c t) -> p c t", t=T)
        kpe, r_kpe = B("kpe", T, F32)
        tk1, r_tk1 = B("tk1", T, F32)
        tk2, r_tk2 = B("tk2", T, F32)
        stl, r_stl = B("stl", nblk * 512, F32); stl = stl.rearrange("p (b r) -> p b r", r=512)
        stk, r_stk = B("stk", nblk * 64, F32); stk = stk.rearrange("p (b r) -> p b r", r=64)
        wsrc = w_in_a[i].rearrange("(k p) n -> p k n", p=128)

        def v512(t):
            return t[:, :].rearrange("p (k n) -> p k n", n=512)

        def panel512(c0):
            pi = {0: 0, 512: 1, 1088: 2, 1600: 3}[c0]
            return wload([(lambda t: t[:, 0:8192], w_in_a4[i, pi])], key=f"ina{i}_{c0}")

        def proj4(wv, wr, dst, r_dst):
            for oc in range(4):
                pt, _, pr = nps()
                for kc in range(DC):
                    mm(pt[:, :T], wv[:, kc, oc * 128:(oc + 1) * 128], xn[:, kc, :T], kc == 0, kc == DC - 1, [wr, r_xn[kc]], [pr])
                cp(dst(oc), pt[:, :T], [pr], [r_dst], eng="act" if oc % 2 else "dve")

        def rope(ps_r, pr_r, ps_s, pr_s, out, r_out):
            tt(tk1[:64, :T], ps_r[:64, :T], ropeT[:, 0, :T], ALU.mult, [pr_r, r_rope], [r_tk1])
            tt(tk2[:64, :T], ps_s[:64, :T], ropeT[:, 1, :T], ALU.mult, [pr_s, r_rope], [r_tk2])
            tt(out, tk1[:64, :T], tk2[:64, :T], ALU.add, [r_tk1, r_tk2], [r_out])

        wt, wr = panel512(0)
        proj4(v512(wt), wr, lambda oc: cq[:, oc, :], r_cq)
        rmsnorm(lambda c: cq[:, c, :], r_cq, 4, 512, T, lambda c: vec[:, V_GQA + 4 * i + c:V_GQA + 4 * i + c + 1],
                lambda c: cqn[:, c, :], r_cqn)
        qsrc = w_qb[i].rearrange("(k p) n -> p k n", p=128)
        qsrc4 = w_qb[i].rearrange("(k p) (h e) -> p k h e", p=128, e=192)

        def vq(t):
            return t[:, 0:6144].rearrange("p (k n) -> p k n", n=1536)

        def vqs(t):
            return t[:, 6144:8192].rearrange("p (k h e) -> p k h e", h=8, e=64)
        parts = [(vq, qsrc)]
        for kq in range(4):
            parts.append((lambda t, kq=kq: vqs(t)[:, kq, :, 0:32], qsrc4[:, kq, :, 160:192]))
            parts.append((lambda t, kq=kq: vqs(t)[:, kq, :, 32:64], qsrc4[:, kq, :, 128:160]))
        wt, wr = wload(parts, key=f"qb{i}")
        wq = vq(wt); wqs = vqs(wt)
        for h in range(8):
            pt, _, pr = nps()
            for k in range(4):
                mm(pt[:, :T], wq[:, k, h * 192:h * 192 + 128], cqn[:, k, :], k == 0, k == 3, [wr, r_cqn], [pr])
            cp(qn[:, h, :], pt[:, :T], [pr], [r_qn], eng="act")
            p1, _, pr1 = nps()
            for k in range(4):
                mm(p1[:64, :T], wq[:, k, h * 192 + 128:h * 192 + 192], cqn[:, k, :], k == 0, k == 3, [wr, r_cqn], [pr1])
            p2, _, pr2 = nps()
            for k in range(4):
                mm(p2[:64, :T], wqs[:, k, h, :], cqn[:, k, :], k == 0, k == 3, [wr, r_cqn], [pr2])
            rope(p1, pr1, p2, pr2, qr[:64, h, :], r_qr)
        chk("p1a")
        wt, wr = panel512(512)
        proj4(v512(wt), wr, lambda oc: cq[:, oc, :], r_cq)
        rms_stats(lambda c: cq[:, c, :], 4, T, [r_cq], 512)
        for c in range(4):
            stt(cq[:, c, :], cq[:, c, :], vec[:, V_GKVA + 4 * i + c:V_GKVA + 4 * i + c + 1], rstd[:, :T], ALU.mult, ALU.mult,
                [r_cq, r_vec, r_rstd], [r_cq])
        chk("k1")
        cp(nKT[:, 0:4, :], cq[:, :, :], [r_cq], [r_nKT], eng="act")
        chk("k2")
        for b in range(nblk):
            pt, _, pr = nps()
            for rc in range(4):
                tr(pt[:bn, rc * 128:(rc + 1) * 128], cq[:, rc, b * bn:(b + 1) * bn], ident_f[:], [r_cq, r_idf], [pr])
            cp(stl[:bn, b, :], pt[:bn, :], [pr], [r_stl], eng="act")
            cp(nKV[:bn, b, :], stl[:bn, b, :], [r_stl], [r_nKV], eng="dve")
        chk("k3")
        if kind == "p":
            store(lat_p[i][tile["tok0"]:tile["tok0"] + T, :].rearrange("(b p) r -> p b r", p=128), stl[:, :, :],
                  [r_stl], [r_latout[i]])
        else:
            store(lat_s[i].rearrange("(b p) r -> p b r", p=64), stl[:64, :, :], [r_stl], ())
        chk("p1b")
        ksrc = wsrc

        def vk(t):
            return t[:, 0:1024].rearrange("p (k n) -> p k n", n=64)

        def vks(t):
            return t[:, 1024:2048].rearrange("p (k n) -> p k n", n=64)
        wt, wr = wload([(vk, ksrc[:, :, 1024:1088]), (lambda t: vks(t)[:, :, 0:32], ksrc[:, :, 1056:1088]),
                        (lambda t: vks(t)[:, :, 32:64], ksrc[:, :, 1024:1056])], key=f"kpe{i}", n_used=2048)
        p1, _, pr1 = nps()
        for kc in range(DC):
            mm(p1[:64, :T], vk(wt)[:, kc, :], xn[:, kc, :T], kc == 0, kc == DC - 1, [wr, r_xn[kc]], [pr1])
        p2, _, pr2 = nps()
        for kc in range(DC):
            mm(p2[:64, :T], vks(wt)[:, kc, :], xn[:, kc, :T], kc == 0, kc == DC - 1, [wr, r_xn[kc]], [pr2])
        rope(p1, pr1, p2, pr2, kpe[:64, :T], r_kpe)
        cp(nKT[:64, 4, :], kpe[:64, :T], [r_kpe], [r_nKT], eng="act")
        pt, _, pr = nps()
        for b in range(nblk):
            tr(pt[:bn, b * 64:(b + 1) * 64], kpe[:64, b * bn:(b + 1) * bn], ident_f[:64, :64], [r_kpe, r_idf], [pr])
        cp(stk[:bn, :, :], pt[:bn, :nblk * 64].rearrange("p (b e) -> p b e", e=64), [pr], [r_stk])
        if kind == "p":
            store(kr_p[i][tile["tok0"]:tile["tok0"] + T, :].rearrange("(b p) e -> p b e", p=128), stk[:, :, :],
                  [r_stk], [r_krout[i]])
        else:
            store(kr_s[i].rearrange("(b p) e -> p b e", p=64), stk[:64, :, :], [r_stk], ())

        chk("p1c")
        B = mk_alloc(pbase)
        zx, r_zx = B("zx", 8 * nseg * E, F32); zx = zx.rearrange("p (c g e) -> p c g e", g=nseg, e=E)
        tA, r_tA = B("tA", nseg * E, F32); tA = tA.rearrange("p (g e) -> p g e", e=E)
        tB, r_tB = B("tB", nseg * E, F32); tB = tB.rearrange("p (g e) -> p g e", e=E)
        pT, r_pT = B("pT", 8 * T, BF16); pT = pT.rearrange("p (c t) -> p c t", t=T)
        pst, r_pst = B("pst", 1024, F32)
        if kind == "s":
            zst, r_zst = B("zst", 1024, F32)
        for pz in range(2):
            wt, wr = panel512(1088 + 512 * pz)
            for oc in range(4):
                pt, _, pr = nps()
                for kc in range(DC):
                    mm(pt[:, :T], v512(wt)[:, kc, oc * 128:(oc + 1) * 128], xn[:, kc, :T], kc == 0, kc == DC - 1,
                       [wr, r_xn[kc]], [pr])
                cp(zx[:, pz * 4 + oc, :, 15:E], pt[:, :T].rearrange("p (g l) -> p g l", l=L), [pr], [r_zx],
                   eng="act" if oc % 2 else "dve")
        if kind == "p":
            cp(zx[:, :, 0, 0:15], hpool_p[:, i, :, :], [r_hpool], [r_zx])
        else:
            for s in range(nseg):
                ld(zst[:15, :], s_pool[i, s], (), [r_zst])
                pt, _, pr = nps()
                for zc in range(8):
                    tr(pt[:, zc * 15:(zc + 1) * 15], zst[:15, zc * 128:(zc + 1) * 128], ident_f[:15, :15], [r_zst, r_idf], [pr])
                cp(zx[:, :, s, 0:15], pt[:, :120].rearrange("p (c e) -> p c e", e=15), [pr], [r_zx])
        for zc in range(8):
            gi = zc // 2
            w = 2 << gi
            cur, r_cur = zx[:, zc], r_zx
            d = 1
            k = 0
            while d < w:
                lo = 2 * d - 1
                nxt, r_nxt = (tA, r_tA) if k % 2 == 0 else (tB, r_tB)
                tt(nxt[:, :, lo:E], cur[:, :, lo:E], cur[:, :, lo - d:E - d], ALU.add, [r_cur], [r_nxt])
                cur, r_cur = nxt, r_nxt
                d *= 2
                k += 1
            pv = pT[:, zc, :].rearrange("p (g l) -> p g l", l=L)
            stt(pv, cur[:, :, 15:E], 1.0 / w, zx[:, zc, :, 15:E], ALU.mult, ALU.subtract, [r_cur, r_zx], [r_pT])
            if kind == "p" and tile["first"]:
                tt(pst[:, 0:16], cur[:, 0, 15:31], rcT[:, gi * 16:(gi + 1) * 16], ALU.mult, [r_cur, r_rc], [r_pst])
                tt(pT[:, zc, 0:16], pst[:, 0:16], zx[:, zc, 0, 15:31], ALU.subtract, [r_pst, r_zx], [r_pT])
        if kind == "p":
            cp(hpool_p[:, i, :, :], zx[:, :, 0, L:L + 15], [r_zx], [r_hpool])
        if kind == "s" or tile["last"]:
            for s in range(nseg):
                for h2 in range(2):
                    pt, _, pr = nps()
                    for q in range(4):
                        zc = h2 * 4 + q
                        tr(pt[:15, q * 128:(q + 1) * 128], zx[:, zc, s, L:L + 15], ident_f[:], [r_zx, r_idf], [pr])
                    cp(pst[:15, h2 * 512:(h2 + 1) * 512], pt[:15, :], [pr], [r_pst])
                store(pool_p[i] if kind == "p" else pool_s[i, s], pst[:15, :], [r_pst], ())
        wt, wr = wload([(lambda t: t[:, 0:2048].rearrange("p (g d) -> p g d", d=256),
                         w_pool[i].rearrange("g (cc p) d -> p (g cc) d", p=128))], key=f"pool{i}", n_used=2048)
        wp = wt[:, 0:2048].rearrange("p (g d) -> p g d", d=256)
        for gi in range(4):
            for dch in range(2):
                pt, _, pr = nps()
                for cc in range(2):
                    mm(pt[:, :T], wp[:, gi * 2 + cc, dch * 128:(dch + 1) * 128], pT[:, 2 * gi + cc, :], cc == 0, cc == 1,
                       [wr, r_pT], [pr])
                col = V_PSC + 8 * i + 2 * gi + dch
                ts(xn[:, 8 + 2 * gi + dch, :T], pt[:, :T], vec[:, col:col + 1], None, ALU.mult, None, [pr, r_vec], [r_xn[8 + 2 * gi + dch]])

        chk("p2")
        B = mk_alloc(pbase)
        nsmax = nkt + 64 if kind == "s" else 2048
        Ssb, r_S = B("Ssb", nsmax, F32)
        Sbufs = [(Ssb, r_S)]
        if kind == "p":
            Sbufs.append(B("Ssb2", nsmax, F32))
        r_pm = [Res("pm0"), Res("pm1")]
        Pb, r_P = B("Pb", nsmax, BF16)
        nbmax = nsmax // 128 + (1 if nsmax % 128 else 0)
        PTs, r_PT = B("PTs", nbmax * 128, BF16); PTs = PTs.rearrange("p (b r) -> p b r", r=128)
        QT, r_QT = B("QT", 5 * 512, BF16); QT = QT.rearrange("p (c r) -> p c r", r=512)
        osb, r_osb = B("osb", 512, BF16)
        oT, r_oT = B("oT", 4 * 512, BF16); oT = oT.rearrange("p (c r) -> p c r", r=512)
        if kind == "s":
            oacc, r_oacc = B("oacc", 4 * 512, F32); oacc = oacc.rearrange("p (g r) -> p g r", r=512)
        wt, wr = wload([(lambda t: t[:, 0:4096].rearrange("p (k n) -> p k n", n=1024), w_uk[i].rearrange("(k p) n -> p k n", p=128)),
                        (lambda t: t[:, 4096:8192].rearrange("p (k n) -> p k n", n=1024), w_uv[i].rearrange("(k p) n -> p k n", p=128))],
                       key=f"ukv{i}")
        uk = wt[:, 0:4096].rearrange("p (k n) -> p k n", n=1024)
        uv = wt[:, 4096:8192].rearrange("p (k n) -> p k n", n=1024)
        sB = wslot()
        tB_, rB = Wsl[sB]
        ukT = tB_[:, 0:4096].rearrange("p (h r) -> p h r", r=512)
        for h in range(8):
            _, ptb, pr = nps()
            for rc in range(4):
                tr(ptb[:, rc * 128:(rc + 1) * 128], uk[:, rc, h * 128:(h + 1) * 128], ident_b[:], [wr, r_idb], [pr])
            cp(ukT[:, h, :], ptb[:, 0:512], [pr], [rB], eng="act" if h % 2 else "dve")

        PM, MLOC, NEGB, RSUM, ALPHA, MNEW, RL = 0, 16, 17, 18, 19, 29, 28

        def sc(c):
            return stat[:, c:c + 1]

        def build_q(col):
            for rc in range(4):
                pt, _, pr = nps()
                for h in range(8):
                    mm(pt[:, h * 64:(h + 1) * 64], ukT[:, h, rc * 128:(rc + 1) * 128], qn[:, h, col:col + 64], True, True,
                       [rB, r_qn], [pr])
                cp(QT[:, rc, :], pt[:, :], [pr], [r_QT], eng="act" if rc % 2 else "dve")
            cp(QT[:64, 4, :].rearrange("p (h q) -> p h q", q=64), qr[:64, :, col:col + 64], [r_qr], [r_QT])

        def attend1a(rg, nl, nc0, nv):
            blocks = [(KT, r_KT, k0, min(512, nl - k0), k0) for k0 in range(0, nl, 512)]
            if nv > 0:
                blocks.append((nKT, r_nKT, nc0, nv, nl))
            out = []
            for j, (kt, rkt, k0, n, dcol) in enumerate(blocks):
                pt, _, pr = nps()
                for rc in range(4):
                    mm(pt[:, :n], QT[:, rc, rg * 128:(rg + 1) * 128], kt[:, rc, k0:k0 + n], rc == 0, False, [r_QT, rkt], [pr])
                mm(pt[:, :n], QT[:64, 4, rg * 128:(rg + 1) * 128], kt[:64, 4, k0:k0 + n], False, True, [r_QT, rkt], [pr])
                out.append((pt, pr, n, dcol))
            return out

        def attend1b(buf, blks):
            Ssb, r_S = Sbufs[buf]
            for j, (pt, pr, n, dcol) in enumerate(blks):
                act(Ssb[:, dcol:dcol + n], pt[:, :n], AF.Copy, [pr], [r_S])
                rmax(sc(PM + 8 * buf + j), Ssb[:, dcol:dcol + n], [r_S], [r_pm[buf]])
            return len(blks)

        def attend2(rg, nl, nc0, nv, newblks, sbi, nsb, buf, nblocks, mid=None):
            Ssb, r_S = Sbufs[buf]
            nk = nl + nv
            rmax(sc(MLOC), stat[:, PM + 8 * buf:PM + 8 * buf + nblocks], [r_pm[buf]], [r_stat])
            mrun = sc(20 + rg); lrun = sc(24 + rg)
            if sbi == 0:
                ts(sc(NEGB), sc(MLOC), -MLA_SCALE, None, ALU.mult, None, [r_stat], [r_stat])
                cp(mrun, sc(MLOC), [r_stat], [r_stat])
            else:
                tt(sc(MNEW), mrun, sc(MLOC), ALU.max, [r_stat], [r_stat])
                ts(sc(NEGB), sc(MNEW), -MLA_SCALE, None, ALU.mult, None, [r_stat], [r_stat])
                act(sc(ALPHA), mrun, AF.Exp, [r_stat], [r_stat], bias=sc(NEGB), scale=MLA_SCALE)
                cp(mrun, sc(MNEW), [r_stat], [r_stat])
            act(Pb[:, :nk], Ssb[:, :nk], AF.Exp, [r_S, r_stat], [r_P, r_stat], bias=sc(NEGB), scale=MLA_SCALE, accum=sc(RSUM))
            if sbi == 0:
                cp(lrun, sc(RSUM), [r_stat], [r_stat])
            else:
                stt(lrun, lrun, sc(ALPHA), sc(RSUM), ALU.mult, ALU.add, [r_stat], [r_stat])
            if mid is not None:
                mid()
            kb = [(KVx, r_KV, j, 128, j * 128) for j in range(nl // 128)]
            c0 = nl
            for (b, n) in newblks:
                kb.append((nKV, r_nKV, b, n, c0))
                c0 += n
            for g0 in range(0, len(kb), 8):
                _, ptb, pr = nps()
                grp = kb[g0:g0 + 8]
                for q, (kv, rkv, b, n, pc) in enumerate(grp):
                    tr(ptb[:n, q * 128:(q + 1) * 128], Pb[:, pc:pc + n], ident_b[:], [r_P, r_idb], [pr])
                cp(PTs[:, g0:g0 + len(grp), :], ptb[:, 0:len(grp) * 128].rearrange("p (b r) -> p b r", r=128), [pr], [r_PT],
                   eng="act" if (g0 // 8) % 2 else "dve")
            po, _, pro = nps()
            for j, (kv, rkv, b, n, pc) in enumerate(kb):
                mm(po[:, :], PTs[:n, j, :], kv[:n, b, 0:512], j == 0, j == len(kb) - 1, [r_PT, rkv], [pro])
            last = sbi == nsb - 1
            if nsb == 1:
                recip(sc(RL), lrun, [r_stat], [r_stat])
                ts(osb[:, :], po[:, :], sc(RL), None, ALU.mult, None, [pro, r_stat], [r_osb])
            elif sbi == 0:
                cp(oacc[:, rg, :], po[:, :], [pro], [r_oacc])
            else:
                stt(oacc[:, rg, :], oacc[:, rg, :], sc(ALPHA), po[:, :], ALU.mult, ALU.add, [r_oacc, r_stat, pro], [r_oacc])
                if last:
                    recip(sc(RL), lrun, [r_stat], [r_stat])
                    ts(osb[:, :], oacc[:, rg, :], sc(RL), None, ALU.mult, None, [r_oacc, r_stat], [r_osb])
            if last:
                _, ptb, pr = nps()
                for rc in range(4):
                    tr(ptb[:, rc * 128:(rc + 1) * 128], osb[:, rc * 128:(rc + 1) * 128], ident_b[:], [r_osb, r_idb], [pr])
                cp(oT[:, :, rg * 128:(rg + 1) * 128], ptb[:, 0:512].rearrange("p (c r) -> p c r", r=128), [pr], [r_oT])

        def ymla(col):
            pt, _, pr = nps()
            for h in range(8):
                for rc in range(4):
                    mm(pt[:, h * 64:(h + 1) * 64], uv[:, rc, h * 128:(h + 1) * 128], oT[:, rc, h * 64:(h + 1) * 64],
                       rc == 0, rc == 3, [wr, r_oT], [pr])
            cp(xn[:, 0:8, col:col + 64], pt[:, :].rearrange("p (h q) -> p h q", q=64), [pr], r_xn[0:8])

        if kind == "p":
            nl = segs[0]["n_prev"]
            for c in range(T // 64):
                if c == 0:
                    build_q(0)
                nv = 64 * (c + 1)
                newblks = [(b, min(128, nv - 128 * b)) for b in range((nv + 127) // 128)]
                if c == 0:
                    pend = (0, attend1b(0, attend1a(0, nl, 0, nv)), 0, nv, newblks)
                for rg in range(4):
                    cur = pend
                    nbuf = (cur[2] + 1) % 2
                    nxt = None
                    if rg < 3:
                        nxt = (rg + 1, attend1a(rg + 1, nl, 0, nv), nv, newblks)
                    elif c + 1 < T // 64:
                        build_q(64 * (c + 1))
                        nv2 = 64 * (c + 2)
                        nb2 = [(b, min(128, nv2 - 128 * b)) for b in range((nv2 + 127) // 128)]
                        nxt = (0, attend1a(0, nl, 0, nv2), nv2, nb2)
                    holder = {}

                    def mid(nxt=nxt, nbuf=nbuf, holder=holder):
                        if nxt is not None:
                            holder["p"] = (nxt[0], attend1b(nbuf, nxt[1]), nbuf, nxt[2], nxt[3])
                    attend2(cur[0], nl, 0, cur[3], cur[4], 0, 1, cur[2], cur[1], mid=mid)
                    if nxt is not None:
                        pend = holder["p"]
                ymla(64 * c)
        else:
            for s in range(nseg):
                build_q(64 * s)
                for sbi in range(2):
                    load_keys(c_lat[i, s, 2048 * sbi:2048 * (sbi + 1), :], c_kr[i, s, 2048 * sbi:2048 * (sbi + 1), :], 2048, ())
                    for rg in range(4):
                        if sbi == 0:
                            nbk = attend1b(0, attend1a(rg, 2048, 0, 0))
                            attend2(rg, 2048, 0, 0, [], 0, 2, 0, nbk)
                        else:
                            nbk = attend1b(0, attend1a(rg, 2048, 64 * s, 64))
                            attend2(rg, 2048, 64 * s, 64, [(s, 64)], 1, 2, 0, nbk)
                ymla(64 * s)

        chk("p3", xn[:, :, :].rearrange("p c t -> p (c t)") if T == 512 else None, r_xn)
        for pz in range(4):
            wt, wr = wload([(lambda t: t[:, 0:8192], w_out_a[i, pz])], key=f"outa{i}_{pz}")
            for oc in range(4):
                dc = pz * 4 + oc
                pt, _, pr = nps()
                for kc in range(DC):
                    mm(pt[:, :T], v512(wt)[:, kc, oc * 128:(oc + 1) * 128], xn[:, kc, :T], kc == 0, kc == DC - 1, [wr, r_xn[kc]], [pr])
                tt(xT[:, dc, :T], xT[:, dc, :T], pt[:, :T], ALU.add, [r_xT, pr], [r_xT])

    r_SstS = [Res("SstS0"), Res("SstS1")]

    def odd_mixer(tile, i):
        st["psn"] = 2
        st["psb"] = 4
        T = tile["T"]; segs = tile["segs"]; nseg = len(segs); kind = tile["kind"]
        NCH = T // 32
        A = mk_alloc()
        og, r_og = A("og", 16 * T, BF16); og = og.rearrange("p (h t) -> p h t", t=T)
        Sbf, r_Sbf = A("Sbf", 128, BF16)
        f32t = {}
        for nm in ("qf", "ff", "lf", "bb", "eb", "enb", "kf", "of0", "of1", "sq_", "sg_", "rs0", "rs1"):
            f32t[nm] = A(nm, T, F32)
        b16t = {}
        for nm in ("Qt", "Kt", "Kh", "vb", "gs0", "gs1", "sqo0", "sqo1"):
            b16t[nm] = A(nm, T, BF16)
        vt, r_vt = A("vt", NCH * 128, BF16); vt = vt.rearrange("p (c r) -> p c r", r=128)
        kt, r_kt = A("kt", NCH * 128, BF16); kt = kt.rearrange("p (c r) -> p c r", r=128)
        AmT, r_Am = A("AmT", T, BF16)
        rs2, r_rs2 = A("rs2", T, F32)
        lb = lambda h: lbv[:, i, 0, h:h + 1]
        oml = lambda h: lbv[:, i, 1, h:h + 1]
        gon = vec[:, V_GON + i:V_GON + i + 1]

        def v4(t):
            return t[:, :].rearrange("p (a k n) -> p a k n", a=4, n=128)

        def proj_steps(h):
            wt, wr = wload([(lambda t: t[:, 0:8192], w_in_c[i, h])], key=f"inc{i}_{h}")
            wv = v4(wt)
            steps = []
            for a in range(4):
                pt, _, pr = PS[a]
                for k0 in range(0, DC, 4):
                    def stp(a=a, k0=k0, pt=pt, pr=pr):
                        for kc in range(k0, k0 + 4):
                            mm(pt[:, :T], wv[:, a, kc, :], xn[:, kc, :T], kc == 0, kc == DC - 1, [wr, r_xn[kc]], [pr])
                    steps.append(stp)
            return steps

        def gla_head(h, S_ap, r_S, filler):
            qf, r_qf = f32t["qf"]; ff, r_ff = f32t["ff"]; lf, r_lf = f32t["lf"]; bb, r_bb = f32t["bb"]
            eb, r_eb = f32t["eb"]; enb, r_enb = f32t["enb"]; kf, r_kf = f32t["kf"]; of, r_of = f32t["of%d" % (h % 2)]
            Qt, r_Qt = b16t["Qt"]; Kt, r_Kt = b16t["Kt"]; Kh, r_Kh = b16t["Kh"]; vb, r_vb = b16t["vb"]
            gs, r_gs = b16t["gs%d" % (h % 2)]; sqo, r_sqo = b16t["sqo%d" % (h % 2)]
            sq_, r_sq = f32t["sq_"]; sg_, r_sg = f32t["sg_"]; rs2, r_rs2 = f32t["rs%d" % (h % 2)]
            act(sq_[:, :T], PS[0][0][:, :T], AF.Sigmoid, [PS[0][2]], [r_sq])
            act(sg_[:, :T], PS[3][0][:, :T], AF.Sigmoid, [PS[3][2]], [r_sg])
            act(ff[:, :T], PS[1][0][:, :T], AF.Sigmoid, [PS[1][2]], [r_ff])
            act(vb[:, :T], PS[2][0][:, :T], AF.Copy, [PS[2][2]], [r_vb])
            tt(qf[:, :T], PS[0][0][:, :T], sq_[:, :T], ALU.mult, [PS[0][2], r_sq], [r_qf])
            tt(gs[:, :T], PS[3][0][:, :T], sg_[:, :T], ALU.mult, [PS[3][2], r_sg], [r_gs])
            ts(ff[:, :T], ff[:, :T], oml(h), lb(h), ALU.mult, ALU.add, [r_ff, r_lbv], [r_ff])
            ts(ff[:, :T], ff[:, :T], 1e-30, None, ALU.max, None, [r_ff], [r_ff])
            act(lf[:, :T], ff[:, :T], AF.Ln, [r_ff], [r_lf])
            ts(kf[:, :T], ff[:, :T], -1.0, 1.0, ALU.mult, ALU.add, [r_ff], [r_kf])
            S.op("dve", lambda e: e.tensor_tensor_scan(bb[:, :T], scanm[:, :T], lf[:, :T], 0.0, ALU.mult, ALU.add),
                 [r_scanm, r_lf], [r_bb])
            act(eb[:, :T], bb[:, :T], AF.Exp, [r_bb], [r_eb])
            act(enb[:, :T], bb[:, :T], AF.Exp, [r_bb], [r_enb], scale=-1.0)
            tt(Qt[:, :T], qf[:, :T], eb[:, :T], ALU.mult, [r_qf, r_eb], [r_Qt])
            tt(Kt[:, :T], kf[:, :T], enb[:, :T], ALU.mult, [r_kf, r_enb], [r_Kt])
            ebe = eb[:, 31:T:32]
            tt(Kh[:, :T].rearrange("p (c t) -> p c t", t=32), Kt[:, :T].rearrange("p (c t) -> p c t", t=32),
               ebe.unsqueeze(2).to_broadcast([128, NCH, 32]), ALU.mult, [r_Kt, r_eb], [r_Kh])
            for src, r_src, dst, r_dst in ((vb, r_vb, vt, r_vt), (Kh, r_Kh, kt, r_kt)):
                for g0 in range(0, NCH, 8):
                    _, ptb, pr = nps()
                    for q in range(8):
                        j = g0 + q
                        tr(ptb[:32, q * 128:(q + 1) * 128], src[:, j * 32:(j + 1) * 32], ident_b[:], [r_src, r_idb], [pr])
                    cp(dst[:32, g0:g0 + 8, :], ptb[:32, :].rearrange("p (c r) -> p c r", r=128), [pr], [r_dst],
                       eng="act" if (g0 // 8) % 2 else "dve")
            pa, _, pra = nps()
            for j in range(NCH):
                mm(pa[:32, j * 32:(j + 1) * 32], Kt[:, j * 32:(j + 1) * 32], Qt[:, j * 32:(j + 1) * 32], True, True,
                   [r_Kt, r_Qt], [pra])
            tt(AmT[:32, :T], pa[:32, :T], glam[:, :T], ALU.mult, [pra, r_glam], [r_Am])
            po, _, pro = PS[6]
            for j in range(NCH):
                cp(Sbf[:, :], S_ap(j), [r_S], [r_Sbf], eng="act")
                mm(po[:, j * 32:(j + 1) * 32], Sbf[:, :], Qt[:, j * 32:(j + 1) * 32], True, False, [r_Sbf, r_Qt], [pro])
                mm(po[:, j * 32:(j + 1) * 32], vt[:32, j, :], AmT[:32, j * 32:(j + 1) * 32], False, True, [r_vt, r_Am], [pro])
                pS, _, prS = PS[7]
                mm(pS[:, 0:128], kt[:32, j, :], vt[:32, j, :], True, True, [r_kt, r_vt], [prS])
                stt(S_ap(j), S_ap(j), eb[:, j * 32 + 31:j * 32 + 32], pS[:, 0:128], ALU.mult, ALU.add,
                    [r_S, r_eb, prS], [r_S])
                filler(j)
            act(of[:, :T], po[:, :T], AF.Copy, [pro], [r_of])
            act(sqo[:, :T], po[:, :T], AF.Square, [pro], [r_sqo])

            def E2():
                p2, _, pr2 = nps()
                mm(p2[:, :T], ones_b[:], sqo[:, :T], True, True, [r_ones, r_sqo], [pr2])
                act(rs2[:, :T], p2[:, :T], AF.Ln, [pr2, r_eps], [r_rs2], bias=epsT[:, 0:1], scale=1.0 / 128)
                act(rs2[:, :T], rs2[:, :T], AF.Exp, [r_rs2], [r_rs2], scale=-0.5)
                stt(of[:, :T], of[:, :T], gon, rs2[:, :T], ALU.mult, ALU.mult, [r_of, r_vec, r_rs2], [r_of])
                tt(og[:, h, :], of[:, :T], gs[:, :T], ALU.mult, [r_of, r_gs], [r_og])
            return E2

        def run_heads(S_fn, r_S_fn, pre=None, post=None):
            steps = proj_steps(0)
            for s_ in steps:
                s_()
            pend = [None]
            for h in range(16):
                nxt = proj_steps(h + 1) if h + 1 < 16 else []
                per = (len(nxt) + NCH - 1) // NCH if nxt else 0

                def filler(j, nxt=nxt, per=per):
                    for _ in range(per):
                        if nxt:
                            nxt.pop(0)()
                    if j == 1 and pend[0] is not None:
                        pend[0]()
                        pend[0] = None
                if pre is not None:
                    pre(h)
                e2 = gla_head(h, S_fn(h), r_S_fn(h), filler)
                if pend[0] is not None:
                    pend[0]()
                pend[0] = e2
                while nxt:
                    nxt.pop(0)()
                if post is not None:
                    post(h)
            pend[0]()

        if kind == "p":
            run_heads(lambda h: (lambda j, h=h: Sst[:, i, h, :]), lambda h: r_Sst)
            if tile["last"]:
                store(hg_p[i].rearrange("h f v -> f h v"), Sst[:, i, :, :], [r_Sst], ())
        else:
            run_heads(lambda h: (lambda j, hb=h % 2: Sst[:, hb, j // 2, :]), lambda h: r_SstS[h % 2],
                      pre=lambda h: ld(Sst[:, h % 2, 0:4, :], s_hg[i, :, h].rearrange("s f v -> f s v"), (), [r_SstS[h % 2], r_Sst]),
                      post=lambda h: store(hg_s[i, :, h].rearrange("s f v -> f s v"), Sst[:, h % 2, 0:4, :], [r_SstS[h % 2]], ()))
        st["psn"] = 8
        st["psb"] = 0

        def v512(t):
            return t[:, :].rearrange("p (k n) -> p k n", n=512)
        for pz in range(4):
            wt, wr = wload([(lambda t: t[:, 0:8192], w_out_c[i, pz])], key=f"outc{i}_{pz}")
            for oc in range(4):
                dc = pz * 4 + oc
                pt, _, pr = nps()
                for kc in range(DC):
                    mm(pt[:, :T], v512(wt)[:, kc, oc * 128:(oc + 1) * 128], og[:, kc, :], kc == 0, kc == DC - 1, [wr, r_og], [pr])
                tt(xT[:, dc, :T], xT[:, dc, :T], pt[:, :T], ALU.add, [r_xT, pr], [r_xT])

    tiles = []
    for i in range(N_PT):
        tiles.append(dict(kind="p", T=T_P, tok0=i * T_P, first=i == 0, last=i == N_PT - 1, ropecol=i * T_P,
                          segs=[dict(seq=0, n_prev=i * T_P, col0=0, L=T_P)]))
    tiles.append(dict(kind="s", T=T_S, tok0=0, first=False, last=True, ropecol=2048,
                      segs=[dict(seq=s, n_prev=4096, col0=64 * s, L=64) for s in range(4)]))
    if tile_sel is not None:
        tiles = [tiles[i] for i in tile_sel]

    try:
        for tile in tiles:
            run_tile(tile)
    except _Stop:
        pass
    S.emit()
    return nc


def _fm(v, n):
    return np.ascontiguousarray(np.asarray(v, np.float32).reshape(n, 128).T)


def _consts():
    f32 = np.float32
    inv = (f32(10000.0) ** (-(np.arange(0, 64, 2, dtype=f32)) / f32(64))).astype(f32)
    pos = np.concatenate([np.arange(2048), np.tile(4096 + np.arange(64), 4)]).astype(f32)
    ang = (pos[:, None] * inv[None, :]).astype(f32)
    cos = np.cos(ang).astype(f32).T
    sin = np.sin(ang).astype(f32).T
    rope = np.stack([np.concatenate([cos, cos], 0), np.concatenate([-sin, sin], 0)], 0)
    rc = np.zeros((4, 16), f32)
    for gi, w in enumerate((2, 4, 8, 16)):
        rc[gi] = 1.0 / np.minimum(np.arange(16) + 1, w)
    rc = np.tile(rc.reshape(1, 64), (128, 1))
    scanm = np.ones((128, 512), f32)
    scanm[:, ::32] = 0.0
    tri = (np.arange(32)[:, None] <= np.arange(32)[None, :]).astype(f32)
    glam = np.tile(tri, (1, 16))
    return dict(ident=np.eye(128, dtype=f32), rope=np.ascontiguousarray(rope, f32), rc=np.ascontiguousarray(rc),
                scanm=scanm, glam=np.ascontiguousarray(glam))


_PROG = {}


def run_cores(inputs, cores, n_layers=NL, tile_sel=None):
    I = {k_: np.asarray(v) for k_, v in inputs.items()}
    key = (n_layers, tuple(tile_sel) if tile_sel is not None else None)
    nc = build_program(n_layers, tile_sel)
    vec = np.zeros((128, NV), np.float32)
    for l in range(4):
        vec[:, V_GMIX + 16 * l:V_GMIX + 16 * l + 16] = _fm(I["g_mix"][l], 16)
        vec[:, V_GFFN + 16 * l:V_GFFN + 16 * l + 16] = _fm(I["g_ffn"][l], 16)
    vec[:, V_GFIN:V_GFIN + 16] = _fm(I["g_final"], 16)
    for i in range(2):
        vec[:, V_GQA + 4 * i:V_GQA + 4 * i + 4] = _fm(I["g_qa"][i], 4)
        vec[:, V_GKVA + 4 * i:V_GKVA + 4 * i + 4] = _fm(I["g_kva"][i], 4)
        vec[:, V_PSC + 8 * i:V_PSC + 8 * i + 8] = _fm(I["pool_scale"][i], 8)
        vec[:, V_LBP + 16 * i:V_LBP + 16 * i + 16] = _fm(I["lb_param"][i], 16)
        vec[:, V_GON + i] = I["g_onorm"][i]
    cwb = np.zeros((4, 128, 4 * 88), np.float32)
    for l in range(4):
        for j in range(3):
            cwb[l, :, 88 * j:88 * j + 88] = _fm(I["conv_w"][l, j], 88)
        cwb[l, :, 264:352] = _fm(I["conv_b"][l], 88)
    shared = dict(vec=vec, cwb=cwb, **_consts())
    for nm in ("w_in_a", "w_qb", "w_pool"):
        shared[nm] = np.ascontiguousarray(I[nm], np.float32)

    def pm(w, cols):
        K = w.shape[0]
        return np.ascontiguousarray(w[:, cols].reshape(K // 128, 128, len(cols)).transpose(1, 0, 2).reshape(128, -1))

    ar = np.arange
    shared["w_up"] = np.stack([np.stack([pm(I["w_up"][l], np.concatenate([ar(256 * p, 256 * p + 256), ar(DFF + 256 * p, DFF + 256 * p + 256)]))
                                         for p in range(22)]) for l in range(4)])
    shared["w_down"] = np.stack([np.stack([pm(I["w_down"][l], ar(128 * d, 128 * d + 128)) for d in range(16)]) for l in range(4)])
    shared["w_in_a4"] = np.stack([np.stack([pm(I["w_in_a"][i], ar(c0, c0 + 512)) for c0 in (0, 512, 1088, 1600)]) for i in range(2)])
    shared["w_out_a"] = np.stack([np.stack([pm(I["w_out_a"][i], ar(512 * p, 512 * p + 512)) for p in range(4)]) for i in range(2)])
    shared["w_out_c"] = np.stack([np.stack([pm(I["w_out_c"][i], ar(512 * p, 512 * p + 512)) for p in range(4)]) for i in range(2)])

    def pm_inc(w, h):
        cols = np.concatenate([ar(2048 * a + 128 * h, 2048 * a + 128 * h + 128) for a in range(4)])
        x = w[:, cols].reshape(16, 128, 4, 128).transpose(1, 2, 0, 3)
        return np.ascontiguousarray(x.reshape(128, -1))
    shared["w_in_c"] = np.stack([np.stack([pm_inc(I["w_in_c"][i], h) for h in range(16)]) for i in range(2)])
    shared["w_uk"] = np.ascontiguousarray(I["w_uk"].reshape(2, 512, 1024), np.float32)
    shared["w_uv"] = np.ascontiguousarray(I["w_uv"].reshape(2, 512, 1024), np.float32)
    in_maps = []
    for c in cores:
        m = dict(shared)
        s4 = slice(4 * c, 4 * c + 4)
        m["xp"] = np.ascontiguousarray(I["x_prompt"][c])
        m["xs"] = np.ascontiguousarray(I["x_sample"][s4].reshape(256, D))
        m["c_lat"] = np.ascontiguousarray(I["cache_mla_latent"][:, s4])
        m["c_kr"] = np.ascontiguousarray(I["cache_mla_krope"][:, s4])
        m["s_pool"] = np.ascontiguousarray(I["state_pool"][:, s4])
        m["s_hg"] = np.ascontiguousarray(I["state_hgrn"][:, s4])
        m["s_cv"] = np.ascontiguousarray(I["state_ffn_conv"][:, s4])
        in_maps.append(m)
    res = run_bass_kernel_spmd(nc, in_maps, core_ids=list(range(len(cores))))
    return res.results


def kernel(**inputs):
    R = run_cores(inputs, list(range(8)))
    f = np.float32
    y_prompt = np.stack([r["y_p"] for r in R], 0).astype(f)
    y_sample = np.concatenate([r["y_s"].reshape(4, 64, D) for r in R], 0).astype(f)
    lat_p = np.stack([r["lat_p"] for r in R], 1).astype(f)
    kr_p = np.stack([r["kr_p"] for r in R], 1).astype(f)
    pool_p = np.stack([r["pool_p"] for r in R], 1).astype(f)
    hg_p = np.stack([r["hg_p"] for r in R], 1).astype(f)
    cv_p = np.stack([r["cv_p"] for r in R], 1).astype(f)
    lat_s = np.concatenate([r["lat_s"].reshape(2, 4, 64, 512) for r in R], 1).astype(f)
    kr_s = np.concatenate([r["kr_s"].reshape(2, 4, 64, 64) for r in R], 1).astype(f)
    pool_s = np.concatenate([r["pool_s"] for r in R], 1).astype(f)
    hg_s = np.concatenate([r["hg_s"] for r in R], 1).astype(f)
    cv_s = np.concatenate([r["cv_s"] for r in R], 1).astype(f)
    return (y_prompt, y_sample, lat_p, kr_p, pool_p, hg_p, cv_p, lat_s, kr_s, pool_s, hg_s, cv_s)
```

```python
import numpy as np
import concourse.bass as bass
import concourse.mybir as mybir
from concourse.bass_utils import run_bass_kernel_spmd

F32 = mybir.dt.float32
BF16 = mybir.dt.bfloat16
AF = mybir.ActivationFunctionType
ALU = mybir.AluOpType
AX = mybir.AxisListType


class Res:
    __slots__ = ("name", "lw", "rd")

    def __init__(self, name):
        self.name = name
        self.lw = None
        self.rd = {}


class Chan:
    __slots__ = ("sem", "val")

    def __init__(self, nc, name):
        self.sem = nc.alloc_semaphore(name)
        self.val = 0


class Arena:
    def __init__(self, nc, nbytes):
        self.t = nc.alloc_sbuf_tensor("arena", [128, nbytes // 4], F32)
        self.tb = self.t.bitcast(BF16)
        self.nbytes = nbytes
        self.regs = []

    def view(self, name, off, cols, dtype):
        esz = 4 if dtype == F32 else 2
        nb = cols * esz
        assert off % 4 == 0 and off + nb <= self.nbytes, (name, off, nb, self.nbytes)
        res = Res(name)
        keep = []
        for (s, e, r) in self.regs:
            if s < off + nb and off < e:
                if r.lw is not None:
                    res.rd[("lw", id(r))] = r.lw
                for k, v in r.rd.items():
                    res.rd[(k, id(r))] = v
                if s < off:
                    keep.append((s, off, r))
                if e > off + nb:
                    keep.append((off + nb, e, r))
            else:
                keep.append((s, e, r))
        keep.append((off, off + nb, res))
        self.regs = keep
        base = self.t if dtype == F32 else self.tb
        o = off // esz
        return base[:, o:o + cols], res


class Sched:
    ENGS = ("pe", "act", "dve", "pool", "sp")

    def __init__(self, nc, sync_same_engine=False):
        self.nc = nc
        self.ops = {e: [] for e in self.ENGS}
        self.sync_same = sync_same_engine
        self.same_dist = 3
        self.out_res = []

    def _deps(self, eng, reads, writes):
        deps = []
        for r in reads:
            if r.lw is not None:
                deps.append(r.lw)
        for w in writes:
            if w.lw is not None:
                deps.append(w.lw)
            deps.extend(w.rd.values())
        out = []
        cur = len(self.ops[eng])
        for d in deps:
            if d[0] == "e" and d[1] == eng:
                if eng == "pe" or cur - d[2] > self.same_dist:
                    continue
            out.append(d)
        return out

    def _mark(self, me, reads, writes):
        key = me[1] if me[0] == "e" else id(me[1])
        for r in reads:
            r.rd[key] = me
        for w in writes:
            w.lw = me
            w.rd = {}

    def op(self, eng, fn, reads=(), writes=()):
        deps = self._deps(eng, reads, writes)
        idx = len(self.ops[eng])
        self.ops[eng].append(dict(fn=fn, deps=deps, dma=None))
        self._mark(("e", eng, idx), reads, writes)

    def dma(self, eng, dst, src, chan, reads=(), writes=()):
        deps = self._deps(eng, reads, writes)
        if chan.val > 0:
            deps.append(("d", chan, chan.val))
        chan.val += 16
        me = ("d", chan, chan.val)
        self.ops[eng].append(dict(fn=lambda e: e.dma_start(out=dst, in_=src), deps=deps, dma=chan))
        self._mark(me, reads, writes)

    def emit(self):
        nc = self.nc
        mil = {e: set() for e in self.ENGS}
        for e in self.ENGS:
            for o in self.ops[e]:
                for d in o["deps"]:
                    if d[0] == "e":
                        mil[d[1]].add(d[2])
        milidx = {}
        for e in self.ENGS:
            for k, i in enumerate(sorted(mil[e])):
                milidx[(e, i)] = k + 1
        sems = {e: nc.alloc_semaphore("c_" + e) for e in self.ENGS}
        ops = self.ops
        out_res = self.out_res

        def run(ename, eng):
            waited = {}
            for i, o in enumerate(ops[ename]):
                need = {}
                for d in o["deps"]:
                    if d[0] == "e":
                        k = ("e", d[1])
                        v = milidx[(d[1], d[2])]
                        s = sems[d[1]]
                    else:
                        k = ("d", id(d[1]))
                        v = d[2]
                        s = d[1].sem
                    if waited.get(k, 0) >= v:
                        continue
                    if k not in need or need[k][1] < v:
                        need[k] = (s, v)
                for k, (s, v) in need.items():
                    eng.wait_ge(s, v)
                    waited[k] = v
                inst = o["fn"](eng)
                if o["dma"] is not None:
                    inst.then_inc(o["dma"].sem, 16)
                elif (ename, i) in milidx:
                    inst.then_inc(sems[ename], 1)
            if ename == "sp":
                for r in out_res:
                    if r.val > 0:
                        eng.wait_ge(r.sem, r.val)

        with nc.Block() as block:
            @block.tensor
            def _(e):
                run("pe", e)

            @block.scalar
            def _(e):
                run("act", e)

            @block.vector
            def _(e):
                run("dve", e)

            @block.gpsimd
            def _(e):
                run("pool", e)

            @block.sync
            def _(e):
                run("sp", e)


D = 2048
DC = 16
DFF = 5632
FC = 44
NL = 4
T_P = 512
N_PT = 4
T_S = 256
EPS = 1e-6
MLA_SCALE = 192 ** -0.5
KBMAX = 2112
ARENA_BYTES = 91392

V_GMIX, V_GFFN, V_GFIN, V_GQA, V_GKVA, V_PSC, V_LBP, V_GON, NV = 0, 64, 128, 144, 152, 160, 176, 208, 210


class _Stop(Exception):
    pass


STOP = None
SCRATCH = True


def build_program(n_layers=NL, tile_sel=None):
    nc = bass.Bass("TRN2", target_bir_lowering=False)
    S = Sched(nc)

    def din(name, shape):
        return nc.dram_tensor(name, list(shape), F32, kind="ExternalInput").ap()

    def dout(name, shape):
        return nc.dram_tensor(name, list(shape), F32, kind="ExternalOutput").ap()

    xp = din("xp", [2048, D]); xs = din("xs", [256, D])
    c_lat = din("c_lat", [2, 4, 4096, 512]); c_kr = din("c_kr", [2, 4, 4096, 64])
    s_pool = din("s_pool", [2, 4, 15, 1024]); s_hg = din("s_hg", [2, 4, 16, 128, 128])
    s_cv = din("s_cv", [4, 4, 2, 2 * DFF])
    vec_d = din("vec", [128, NV]); cwb_d = din("cwb", [4, 128, 4 * 88])
    ident_d = din("ident", [128, 128]); rope_d = din("rope", [2, 64, 2048 + 256])
    rc_d = din("rc", [128, 64]); scanm_d = din("scanm", [128, 512]); glam_d = din("glam", [32, 512])
    w_in_a = din("w_in_a", [2, D, 2112]); w_qb = din("w_qb", [2, 512, 1536])
    w_uk = din("w_uk", [2, 512, 1024]); w_uv = din("w_uv", [2, 512, 1024])
    w_pool = din("w_pool", [2, 4, 256, 256]); w_out_a = din("w_out_a", [2, 4, 128, 8192])
    w_in_c = din("w_in_c", [2, 16, 128, 8192]); w_out_c = din("w_out_c", [2, 4, 128, 8192])
    w_up = din("w_up", [4, 22, 128, 8192]); w_down = din("w_down", [4, 16, 128, FC * 128])
    w_in_a4 = din("w_in_a4", [2, 4, 128, 8192])

    y_p = dout("y_p", [2048, D]); y_s = dout("y_s", [256, D])
    lat_p = dout("lat_p", [2, 2048, 512]); kr_p = dout("kr_p", [2, 2048, 64])
    pool_p = dout("pool_p", [2, 15, 1024]); hg_p = dout("hg_p", [2, 16, 128, 128])
    cv_p = dout("cv_p", [4, 2, 2 * DFF])
    lat_s = dout("lat_s", [2, 256, 512]); kr_s = dout("kr_s", [2, 256, 64])
    pool_s = dout("pool_s", [2, 4, 15, 1024]); hg_s = dout("hg_s", [2, 4, 16, 128, 128])
    cv_s = dout("cv_s", [4, 4, 2, 2 * DFF])
    dbg_d = dout("dbg", [128, 8192]) if STOP is not None else None

    def sb(name, shape, dt):
        return nc.alloc_sbuf_tensor(name, list(shape), dt), Res(name)

    xT, r_xT = sb("xT", [128, DC, 512], F32)
    xn, _r_xn_unused = sb("xn", [128, DC, 512], BF16)
    r_xn = [Res("xn%d" % c) for c in range(DC)]
    vec, r_vec = sb("vecs", [128, NV], F32)
    cwb, r_cwb = sb("cwbs", [128, 4 * 88], F32)
    ident_f, r_idf = sb("ident_f", [128, 128], F32)
    ident_b, r_idb = sb("ident_b", [128, 128], BF16)
    ones_b, r_ones = sb("ones_b", [128, 128], BF16)
    ropeT, r_rope = sb("ropeT", [64, 2, 512], F32)
    scanm, r_scanm = sb("scanms", [128, 512], F32)
    glam, r_glam = sb("glams", [32, 512], F32)
    rcT, r_rc = sb("rcT", [128, 64], F32)
    lbv, r_lbv = sb("lbv", [128, 2, 2, 16], F32)
    sqb = [sb(f"sqb{i}", [128, 512], BF16) for i in range(2)]
    rstd, r_rstd = sb("rstd", [128, 512], F32)
    hcv_p, r_hcvp = sb("hcv_p", [128, 4, 88, 2], F32)
    hcv_s, r_hcvs = sb("hcv_s", [128, 88, 4, 2], F32)
    hpool_p, r_hpool = sb("hpool_p", [128, 2, 8, 15], F32)
    Sst, r_Sst = sb("Sst", [128, 2, 16, 128], F32)
    stat, r_stat = sb("stat", [128, 64], F32)
    Wsl = [sb(f"wslot{i}", [128, 8192], BF16) for i in range(2)]
    arena = Arena(nc, ARENA_BYTES)

    PS = []
    for i in range(8):
        t = nc.alloc_psum_tensor(f"ps{i}", [128, 512], F32)
        PS.append((t, t.bitcast(BF16), Res(f"ps{i}")))
    st = dict(ps=0, w=0, ld=0, ldp=0, stc=0, sq=0)

    def nps():
        st["ps"] = (st["ps"] + 1) % st.get("psn", 6)
        return PS[st.get("psb", 0) + st["ps"]]

    wch = [[Chan(nc, f"w{s}_{p}") for p in range(4)] for s in range(3)]
    ldch = [Chan(nc, f"ld{i}") for i in range(6)]
    ldpch = [Chan(nc, f"ldp{i}") for i in range(4)]
    stch = [Chan(nc, f"st{i}") for i in range(8)]
    S.out_res = stch

    def ld(dst, src, reads=(), writes=()):
        st["ld"] = (st["ld"] + 1) % len(ldch)
        S.dma("sp", dst, src, ldch[st["ld"]], reads, writes)

    def ldp(dst, src, reads=(), writes=()):
        st["ldp"] = (st["ldp"] + 1) % len(ldpch)
        S.dma("pool", dst, src, ldpch[st["ldp"]], reads, writes)

    def store(dst, src, reads=(), writes=()):
        st["stc"] = (st["stc"] + 1) % len(stch)
        S.dma("sp", dst, src, stch[st["stc"]], reads, writes)

    def wslot():
        st["w"] = (st["w"] + 1) % len(Wsl)
        return st["w"]

    wscr = {}
    scr_ch = [Chan(nc, f"scr{i}") for i in range(4)]

    def wload(parts, key=None, n_used=8192):
        s = wslot()
        t, r = Wsl[s]
        if key is not None and key in wscr:
            scr, rscr = wscr[key]
            S.dma("pool", t[:, 0:n_used], scr.ap(), wch[s][0], [rscr], [r])
            return t, r
        for i, (dv, src) in enumerate(parts):
            S.dma("pool", dv(t), src, wch[s][i % 4], (), [r])
        if key is not None and SCRATCH:
            scr = nc.dram_tensor("scr_" + key, [128, n_used], BF16, kind="Internal")
            rscr = Res("scr_" + key)
            wscr[key] = (scr, rscr)
            st["scr"] = (st.get("scr", 0) + 1) % len(scr_ch)
            S.dma("sp", scr.ap(), t[:, 0:n_used], scr_ch[st["scr"]], [r], [rscr])
        return t, r

    def mm(out, lhsT, rhs, start, stop, reads, writes):
        S.op("pe", lambda e: e.matmul(out, lhsT, rhs, start=start, stop=stop), reads, writes)

    def tr(out, in_, idn, reads, writes):
        S.op("pe", lambda e: e.transpose(out, in_, idn), reads, writes)

    def act(out, in_, func, reads, writes, bias=None, scale=None, accum=None):
        kw = {}
        if bias is not None:
            kw["bias"] = bias
        if scale is not None:
            kw["scale"] = scale
        if accum is not None:
            kw["accum_out"] = accum
        S.op("act", lambda e: e.activation(out, in_, func, **kw), reads, writes)

    def tt(out, a, b, op, reads, writes, eng="dve"):
        S.op(eng, lambda e: e.tensor_tensor(out, a, b, op), reads, writes)

    def ts(out, a, s1, s2, op0, op1, reads, writes, eng="dve"):
        if op1 is None:
            S.op(eng, lambda e: e.tensor_scalar(out, a, s1, None, op0), reads, writes)
        else:
            S.op(eng, lambda e: e.tensor_scalar(out, a, s1, s2, op0, op1), reads, writes)

    def stt(out, a, sc, b, op0, op1, reads, writes):
        S.op("dve", lambda e: e.scalar_tensor_tensor(out, a, sc, b, op0, op1), reads, writes)

    def cp(out, in_, reads, writes, eng="dve"):
        if eng == "act":
            S.op("act", lambda e: e.copy(out, in_), reads, writes)
        else:
            S.op(eng, lambda e: e.tensor_copy(out, in_), reads, writes)

    def rmax(out, in_, reads, writes):
        S.op("dve", lambda e: e.reduce_max(out, in_, AX.X), reads, writes)

    def recip(out, in_, reads, writes):
        S.op("dve", lambda e: e.reciprocal(out, in_), reads, writes)

    def memset(ap, val, writes, eng="dve"):
        S.op(eng, lambda e: e.memset(ap, val), (), writes)

    ld(vec[:], vec_d, (), [r_vec])
    ld(ident_f[:], ident_d, (), [r_idf])
    ld(scanm[:], scanm_d, (), [r_scanm])
    ld(glam[:], glam_d, (), [r_glam])
    ld(rcT[:], rc_d, (), [r_rc])
    cp(ident_b[:], ident_f[:], [r_idf], [r_idb])
    memset(ones_b[:], 1.0, [r_ones])
    p0 = vec[:, V_LBP:V_LBP + 16]; p1 = vec[:, V_LBP + 16:V_LBP + 32]
    sm0 = lbv[:, 0, 1, :]; sm1 = lbv[:, 1, 1, :]
    tt(sm0, p0, p1, ALU.subtract, [r_vec], [r_lbv])
    tt(sm1, p1, p0, ALU.subtract, [r_vec], [r_lbv])
    act(sm0, sm0, AF.Sigmoid, [r_lbv], [r_lbv])
    act(sm1, sm1, AF.Sigmoid, [r_lbv], [r_lbv])
    tt(lbv[:, 0, 0, :], sm0, sm0, ALU.subtract, [r_lbv], [r_lbv])
    tt(lbv[:, 1, 0, :], sm0, sm1, ALU.add, [r_lbv], [r_lbv])
    tt(lbv[:, 1, 0, :], lbv[:, 1, 0, :], sm0, ALU.subtract, [r_lbv], [r_lbv])
    for i in range(2):
        ts(lbv[:, i, 0, :], lbv[:, i, 0, :], 0.0, 1.0, ALU.max, ALU.min, [r_lbv], [r_lbv])
        ts(lbv[:, i, 1, :], lbv[:, i, 0, :], -1.0, 1.0, ALU.mult, ALU.add, [r_lbv], [r_lbv])
    memset(Sst[:], 0.0, [r_Sst])
    memset(hcv_p[:], 0.0, [r_hcvp])
    memset(hpool_p[:], 0.0, [r_hpool])

    epsT, r_eps = sb("epsT", [128, 1], F32)
    memset(epsT[:], EPS, [r_eps])

    def chk(tag, src=None, res=None, n=8192):
        if STOP in (tag, "%s@%d" % (tag, st.get("tile", -1))):
            if src is not None:
                dv, dr = arena.view("dbgbuf", 0, n, F32)
                cp(dv, src, res if isinstance(res, list) else [res], [dr])
                store(dbg_d[:, 0:n], dv, [dr], ())
            raise _Stop()

    def mk_alloc(base=0):
        off = [base]

        def A(name, cols, dt):
            v, r = arena.view(name, off[0], cols, dt)
            off[0] += (cols * (4 if dt == F32 else 2) + 31) // 32 * 32
            return v, r
        A.off = off
        return A

    def rms_stats(src_fn, nch, T, reads, nfeat):
        pt, _, pr = nps()
        for c in range(nch):
            q, qr = sqb[c % 2]
            if c % 2 == 0:
                act(q[:, :T], src_fn(c), AF.Square, reads, [qr])
            else:
                tt(q[:, :T], src_fn(c), src_fn(c), ALU.mult, reads, [qr])
            mm(pt[:, :T], ones_b[:], q[:, :T], c == 0, c == nch - 1, [qr, r_ones], [pr])
        act(rstd[:, :T], pt[:, :T], AF.Ln, [pr, r_eps], [r_rstd], bias=epsT[:, 0:1], scale=1.0 / nfeat)
        act(rstd[:, :T], rstd[:, :T], AF.Exp, [r_rstd], [r_rstd], scale=-0.5)

    def rmsnorm(src_fn, r_src, nch, nfeat, T, g_fn, dst_fn, r_dst):
        rms_stats(src_fn, nch, T, [r_src], nfeat)
        for c in range(nch):
            rd = r_dst[c] if isinstance(r_dst, list) else r_dst
            stt(dst_fn(c), src_fn(c), g_fn(c), rstd[:, :T], ALU.mult, ALU.mult, [r_src, r_vec, r_rstd], [rd])

    def load_x(tile):
        st["psn"] = 8
        st["psb"] = 0
        T = tile["T"]; nb = T // 128
        A = mk_alloc()
        stg, r_stg = A("xstage", nb * D, F32)
        stg = stg.rearrange("p (b d) -> p b d", d=D)
        src = (xp if tile["kind"] == "p" else xs)[tile["tok0"]:tile["tok0"] + T, :]
        ld(stg, src.rearrange("(b p) d -> p b d", p=128), (), [r_stg])
        for dc in range(DC):
            pt, _, pr = nps()
            for b in range(nb):
                tr(pt[:, b * 128:(b + 1) * 128], stg[:, b, dc * 128:(dc + 1) * 128], ident_f[:], [r_stg, r_idf], [pr])
            cp(xT[:, dc, :T], pt[:, :T], [pr], [r_xT], eng="act" if dc % 2 else "dve")
        ld(ropeT[:, 0, :T], rope_d[0, :, tile["ropecol"]:tile["ropecol"] + T], (), [r_rope])
        ld(ropeT[:, 1, :T], rope_d[1, :, tile["ropecol"]:tile["ropecol"] + T], (), [r_rope])

    def store_y(tile):
        T = tile["T"]; nb = T // 128
        rms_stats(lambda c: xT[:, c, :T], DC, T, [r_xT], D)
        for dc in range(DC):
            stt(xT[:, dc, :T], xT[:, dc, :T], vec[:, V_GFIN + dc:V_GFIN + dc + 1], rstd[:, :T], ALU.mult, ALU.mult,
                [r_xT, r_vec, r_rstd], [r_xT])
        A = mk_alloc()
        stg, r_stg = A("ystage", nb * D, F32)
        stg = stg.rearrange("p (b d) -> p b d", d=D)
        k = 0
        for b in range(nb):
            for d4 in range(4):
                pt, _, pr = nps()
                for j in range(4):
                    dc = d4 * 4 + j
                    tr(pt[:, j * 128:(j + 1) * 128], xT[:, dc, b * 128:(b + 1) * 128], ident_f[:], [r_xT, r_idf], [pr])
                cp(stg[:, b, d4 * 512:(d4 + 1) * 512], pt[:, :], [pr], [r_stg], eng="act" if k % 2 else "dve")
                k += 1
        dst = (y_p if tile["kind"] == "p" else y_s)[tile["tok0"]:tile["tok0"] + T, :]
        store(dst.rearrange("(b p) d -> p b d", p=128), stg, [r_stg], ())

    def ffn(tile, l):
        st["psn"] = 8
        st["psb"] = 0
        T = tile["T"]; segs = tile["segs"]; nseg = len(segs); L = segs[0]["L"]; E = L + 2
        kind = tile["kind"]
        A = mk_alloc()
        gT, r_gT = A("gT", FC * T, BF16)
        gT = gT.rearrange("p (c t) -> p c t", t=T)
        hx = [[A(f"hx{ab}{i}", nseg * E, F32) for i in range(2)] for ab in range(2)]
        t1 = [[A(f"t1{ab}{i}", T, F32) for i in range(2)] for ab in range(2)]
        sa = [A(f"sa{i}", T, F32) for i in range(2)]
        cst, r_cst = A("cvstage", 8 * 128, F32)
        cst = cst.rearrange("p (g f) -> p g f", f=128)
        w3, r_w3 = A("wslot3", 8192, BF16)
        Wsl.append((w3, r_w3))
        rmsnorm(lambda c: xT[:, c, :T], r_xT, DC, D, T, lambda c: vec[:, V_GFFN + 16 * l + c:V_GFFN + 16 * l + c + 1],
                lambda c: xn[:, c, :T], r_xn)
        ld(cwb[:], cwb_d[l], (), [r_cwb])
        if kind == "s":
            ld(cst[:88, :, :], s_cv[l].rearrange("s t (c f) -> c (s t) f", f=128), (), [r_cst])
            for h2 in range(2):
                pt, _, pr = nps()
                for q in range(4):
                    g = h2 * 4 + q
                    tr(pt[:, q * 88:(q + 1) * 88], cst[:88, g, :], ident_f[:88, :88], [r_cst, r_idf], [pr])
                cp(hcv_s[:, :, 2 * h2:2 * h2 + 2, :].rearrange("p c s t -> p (s t) c") if False else
                   hcv_s[:, :, 2 * h2:2 * h2 + 2, :],
                   pt[:, :4 * 88].rearrange("p (s t c) -> p c s t", s=2, t=2), [pr], [r_hcvs])

        def hist_ap(ch):
            return hcv_p[:, l, ch:ch + 1, :] if kind == "p" else hcv_s[:, ch, :, :]

        r_hist = r_hcvp if kind == "p" else r_hcvs
        it = 0
        for ps_i in range(FC // 2):
            def dva(t):
                return t[:, :].rearrange("p (k n) -> p k n", n=512)[:, :, 0:256]

            def dvb(t):
                return t[:, :].rearrange("p (k n) -> p k n", n=512)[:, :, 256:512]
            wt, wr = wload([(lambda t: t[:, 0:8192], w_up[l, ps_i])], key=f"up{l}_{ps_i}")
            wv = wt[:, :].rearrange("p (k n) -> p k n", n=512)
            for jj in range(2):
                j = 2 * ps_i + jj
                buf = it % 2
                it += 1
                tv = []
                for ab in range(2):
                    ch = j + ab * FC
                    pt, _, pr = nps()
                    for kc in range(DC):
                        mm(pt[:, :T], wv[:, kc, ab * 256 + jj * 128:ab * 256 + (jj + 1) * 128], xn[:, kc, :T],
                           kc == 0, kc == DC - 1, [wr, r_xn[kc]], [pr])
                    hxt, hxr = hx[ab][buf]
                    hxv = hxt.rearrange("p (g e) -> p g e", e=E)
                    act(hxv[:, :, 2:E], pt[:, :T].rearrange("p (g l) -> p g l", l=L), AF.Copy, [pr], [hxr])
                    cp(hxv[:, :, 0:2], hist_ap(ch), [r_hist], [hxr])
                    cp(hist_ap(ch), hxv[:, :, L:L + 2], [hxr], [r_hist])
                    tt_, tr_ = t1[ab][buf]
                    t3 = tt_.rearrange("p (g l) -> p g l", l=L)
                    w0 = cwb[:, ch:ch + 1]; w1 = cwb[:, 88 + ch:88 + ch + 1]; w2 = cwb[:, 176 + ch:176 + ch + 1]
                    bb = cwb[:, 264 + ch:264 + ch + 1]
                    ts(t3, hxv[:, :, 2:E], w2, bb, ALU.mult, ALU.add, [hxr, r_cwb], [tr_])
                    stt(t3, hxv[:, :, 1:E - 1], w1, t3, ALU.mult, ALU.add, [hxr, r_cwb, tr_], [tr_])
                    stt(t3, hxv[:, :, 0:L], w0, t3, ALU.mult, ALU.add, [hxr, r_cwb, tr_], [tr_])
                    tv.append((tt_, tr_))
                sat, sar = sa[buf]
                act(sat[:, :T], tv[0][0][:, :T], AF.Silu, [tv[0][1]], [sar])
                tt(gT[:, j, :], sat[:, :T], tv[1][0][:, :T], ALU.mult, [sar, tv[1][1]], [r_gT])
        chk("f1")
        if kind == "s" or tile["last"]:
            ng = 8 if kind == "s" else 2
            for h2 in range((ng + 3) // 4):
                pt, _, pr = nps()
                nq = min(4, ng - 4 * h2)
                for q in range(nq):
                    g = h2 * 4 + q
                    src = hcv_s[:, :, g // 2, g % 2] if kind == "s" else hcv_p[:, l, :, g]
                    tr(pt[:88, q * 128:(q + 1) * 128], src, ident_f[:], [r_hist, r_idf], [pr])
                cp(cst[:88, 4 * h2:4 * h2 + nq, :], pt[:88, :nq * 128].rearrange("p (g f) -> p g f", f=128), [pr], [r_cst])
            if kind == "s":
                store(cv_s[l].rearrange("s t (c f) -> c (s t) f", f=128), cst[:88, :, :], [r_cst], ())
            else:
                store(cv_p[l].rearrange("t (c f) -> c t f", f=128), cst[:88, 0:2, :], [r_cst], ())
        for dm in range(DC):
            def dv0(t):
                return t[:, :FC * 128].rearrange("p (c n) -> p c n", n=128)[:, 0:22, :]

            def dv1(t):
                return t[:, :FC * 128].rearrange("p (c n) -> p c n", n=128)[:, 22:44, :]
            wt, wr = wload([(lambda t: t[:, 0:FC * 128], w_down[l, dm])], key=f"dn{l}_{dm}", n_used=FC * 128)
            wv = wt[:, :FC * 128].rearrange("p (c n) -> p c n", n=128)
            pt, _, pr = nps()
            for fc in range(FC):
                mm(pt[:, :T], wv[:, fc, :], gT[:, fc, :], fc == 0, fc == FC - 1, [wr, r_gT], [pr])
            tt(xT[:, dm, :T], xT[:, dm, :T], pt[:, :T], ALU.add, [r_xT, pr], [r_xT])
        Wsl.pop()
        st["w"] = st["w"] % 2

    def run_tile(tile):
        st["tile"] = st.get("tile", -1) + 1
        load_x(tile)
        for l in range(n_layers):
            T = tile["T"]
            rmsnorm(lambda c: xT[:, c, :T], r_xT, DC, D, T, lambda c: vec[:, V_GMIX + 16 * l + c:V_GMIX + 16 * l + c + 1],
                    lambda c: xn[:, c, :T], r_xn)
            if l % 2 == 0:
                even_mixer(tile, l // 2)
            else:
                odd_mixer(tile, l // 2)
            chk("mix%d" % l, xT[:, :, :].rearrange("p c t -> p (c t)"), r_xT)
            ffn(tile, l)
            chk("ffn%d" % l, xT[:, :, :].rearrange("p c t -> p (c t)"), r_xT)
        store_y(tile)

    r_latout = [Res("latout0"), Res("latout1")]
    r_krout = [Res("krout0"), Res("krout1")]

    def even_mixer(tile, i):
        st["psn"] = 8
        st["psb"] = 0
        T = tile["T"]; segs = tile["segs"]; nseg = len(segs); L = segs[0]["L"]; kind = tile["kind"]
        E = 15 + L
        nblk = T // 128 if kind == "p" else nseg
        bn = 128 if kind == "p" else 64
        nkt = 1536 if kind == "p" else 2048
        A = mk_alloc()
        qn, r_qn = A("qn", 8 * T, BF16); qn = qn.rearrange("p (h t) -> p h t", t=T)
        qr, r_qr = A("qr", 8 * T, BF16); qr = qr.rearrange("p (h t) -> p h t", t=T)
        KT, r_KT = A("KT", 5 * nkt, BF16); KT = KT.rearrange("p (c k) -> p c k", k=nkt)
        KVx, r_KV = A("KVx", (nkt // 128) * 576, BF16); KVx = KVx.rearrange("p (b r) -> p b r", r=576)
        nKT, r_nKT = A("nKT", 5 * T, BF16); nKT = nKT.rearrange("p (c t) -> p c t", t=T)
        nKV, r_nKV = A("nKV", nblk * 512, BF16); nKV = nKV.rearrange("p (b r) -> p b r", r=512)
        pbase = A.off[0]

        def load_keys(src_lat, src_kr, n, reads):
            nb = n // 128
            ldp(KVx[:, 0:nb, 0:512], src_lat.rearrange("(b p) r -> p b r", p=128), reads, [r_KV])
            ldp(KVx[:, 0:nb, 512:576], src_kr.rearrange("(b p) r -> p b r", p=128), reads, [r_KV])
            for j in range(nb):
                _, ptb, pr = nps()
                for rc in range(4):
                    tr(ptb[:, rc * 128:(rc + 1) * 128], KVx[:, j, rc * 128:(rc + 1) * 128], ident_b[:], [r_KV, r_idb], [pr])
                tr(ptb[:64, 512:640], KVx[:, j, 512:576], ident_b[:], [r_KV, r_idb], [pr])
                cp(KT[:, 0:4, j * 128:(j + 1) * 128], ptb[:, 0:512].rearrange("p (c k) -> p c k", k=128), [pr], [r_KT],
                   eng="act" if j % 2 else "dve")
                cp(KT[:64, 4, j * 128:(j + 1) * 128], ptb[:64, 512:640], [pr], [r_KT], eng="act" if j % 2 else "dve")

        if kind == "p" and segs[0]["n_prev"] > 0:
            n = segs[0]["n_prev"]
            load_keys(lat_p[i][0:n, :], kr_p[i][0:n, :], n, [r_latout[i], r_krout[i]])
            chk("lkV", KVx[:, :, :].rearrange("p b r -> p (b r)"), r_KV, n=12 * 576)
            chk("lkT", KT[:, :, :].rearrange("p c k -> p (c k)"), r_KT, n=5 * 1536)

        B = mk_alloc(pbase)
        cq, r_cq = B("cq", 4 * T, F32); cq = cq.rearrange("p (c t) -> p c t", t=T)
        cqn, r_cqn = B("cqn", 4 * T, BF16); cqn = cqn.rearrange("p (c t) -> p c t", t=T)
        kpe, r_kpe = B("kpe", T, F32)
        tk1, r_tk1 = B("tk1", T, F32)
        tk2, r_tk2 = B("tk2", T, F32)
        stl, r_stl = B("stl", nblk * 512, F32); stl = stl.rearrange("p (b r) -> p b r", r=512)
        stk, r_stk = B("stk", nblk * 64, F32); stk = stk.rearrange("p (b r) -> p b r", r=64)
        wsrc = w_in_a[i].rearrange("(k p) n -> p k n", p=128)

        def v512(t):
            return t[:, :].rearrange("p (k n) -> p k n", n=512)

        def panel512(c0):
            pi = {0: 0, 512: 1, 1088: 2, 1600: 3}[c0]
            return wload([(lambda t: t[:, 0:8192], w_in_a4[i, pi])], key=f"ina{i}_{c0}")

        def proj4(wv, wr, dst, r_dst):
            for oc in range(4):
                pt, _, pr = nps()
                for kc in range(DC):
                    mm(pt[:, :T], wv[:, kc, oc * 128:(oc + 1) * 128], xn[:, kc, :T], kc == 0, kc == DC - 1, [wr, r_xn[kc]], [pr])
                cp(dst(oc), pt[:, :T], [pr], [r_dst], eng="act" if oc % 2 else "dve")

        def rope(ps_r, pr_r, ps_s, pr_s, out, r_out):
            tt(tk1[:64, :T], ps_r[:64, :T], ropeT[:, 0, :T], ALU.mult, [pr_r, r_rope], [r_tk1])
            tt(tk2[:64, :T], ps_s[:64, :T], ropeT[:, 1, :T], ALU.mult, [pr_s, r_rope], [r_tk2])
            tt(out, tk1[:64, :T], tk2[:64, :T], ALU.add, [r_tk1, r_tk2], [r_out])

        wt, wr = panel512(0)
        proj4(v512(wt), wr, lambda oc: cq[:, oc, :], r_cq)
        rmsnorm(lambda c: cq[:, c, :], r_cq, 4, 512, T, lambda c: vec[:, V_GQA + 4 * i + c:V_GQA + 4 * i + c + 1],
                lambda c: cqn[:, c, :], r_cqn)
        qsrc = w_qb[i].rearrange("(k p) n -> p k n", p=128)
        qsrc4 = w_qb[i].rearrange("(k p) (h e) -> p k h e", p=128, e=192)

        def vq(t):
            return t[:, 0:6144].rearrange("p (k n) -> p k n", n=1536)

        def vqs(t):
            return t[:, 6144:8192].rearrange("p (k h e) -> p k h e", h=8, e=64)
        parts = [(vq, qsrc)]
        for kq in range(4):
            parts.append((lambda t, kq=kq: vqs(t)[:, kq, :, 0:32], qsrc4[:, kq, :, 160:192]))
            parts.append((lambda t, kq=kq: vqs(t)[:, kq, :, 32:64], qsrc4[:, kq, :, 128:160]))
        wt, wr = wload(parts, key=f"qb{i}")
        wq = vq(wt); wqs = vqs(wt)
        for h in range(8):
            pt, _, pr = nps()
            for k in range(4):
                mm(pt[:, :T], wq[:, k, h * 192:h * 192 + 128], cqn[:, k, :], k == 0, k == 3, [wr, r_cqn], [pr])
            cp(qn[:, h, :], pt[:, :T], [pr], [r_qn], eng="act")
            p1, _, pr1 = nps()
            for k in range(4):
                mm(p1[:64, :T], wq[:, k, h * 192 + 128:h * 192 + 192], cqn[:, k, :], k == 0, k == 3, [wr, r_cqn], [pr1])
            p2, _, pr2 = nps()
            for k in range(4):
                mm(p2[:64, :T], wqs[:, k, h, :], cqn[:, k, :], k == 0, k == 3, [wr, r_cqn], [pr2])
            rope(p1, pr1, p2, pr2, qr[:64, h, :], r_qr)
        chk("p1a")
        wt, wr = panel512(512)
        proj4(v512(wt), wr, lambda oc: cq[:, oc, :], r_cq)
        rms_stats(lambda c: cq[:, c, :], 4, T, [r_cq], 512)
        for c in range(4):
            stt(cq[:, c, :], cq[:, c, :], vec[:, V_GKVA + 4 * i + c:V_GKVA + 4 * i + c + 1], rstd[:, :T], ALU.mult, ALU.mult,
                [r_cq, r_vec, r_rstd], [r_cq])
        chk("k1")
        cp(nKT[:, 0:4, :], cq[:, :, :], [r_cq], [r_nKT], eng="act")
        chk("k2")
        for b in range(nblk):
            pt, _, pr = nps()
            for rc in range(4):
                tr(pt[:bn, rc * 128:(rc + 1) * 128], cq[:, rc, b * bn:(b + 1) * bn], ident_f[:], [r_cq, r_idf], [pr])
            cp(stl[:bn, b, :], pt[:bn, :], [pr], [r_stl], eng="act")
            cp(nKV[:bn, b, :], stl[:bn, b, :], [r_stl], [r_nKV], eng="dve")
        chk("k3")
        if kind == "p":
            store(lat_p[i][tile["tok0"]:tile["tok0"] + T, :].rearrange("(b p) r -> p b r", p=128), stl[:, :, :],
                  [r_stl], [r_latout[i]])
        else:
            store(lat_s[i].rearrange("(b p) r -> p b r", p=64), stl[:64, :, :], [r_stl], ())
        chk("p1b")
        ksrc = wsrc

        def vk(t):
            return t[:, 0:1024].rearrange("p (k n) -> p k n", n=64)

        def vks(t):
            return t[:, 1024:2048].rearrange("p (k n) -> p k n", n=64)
        wt, wr = wload([(vk, ksrc[:, :, 1024:1088]), (lambda t: vks(t)[:, :, 0:32], ksrc[:, :, 1056:1088]),
                        (lambda t: vks(t)[:, :, 32:64], ksrc[:, :, 1024:1056])], key=f"kpe{i}", n_used=2048)
        p1, _, pr1 = nps()
        for kc in range(DC):
            mm(p1[:64, :T], vk(wt)[:, kc, :], xn[:, kc, :T], kc == 0, kc == DC - 1, [wr, r_xn[kc]], [pr1])
        p2, _, pr2 = nps()
        for kc in range(DC):
            mm(p2[:64, :T], vks(wt)[:, kc, :], xn[:, kc, :T], kc == 0, kc == DC - 1, [wr, r_xn[kc]], [pr2])
        rope(p1, pr1, p2, pr2, kpe[:64, :T], r_kpe)
        cp(nKT[:64, 4, :], kpe[:64, :T], [r_kpe], [r_nKT], eng="act")
        pt, _, pr = nps()
        for b in range(nblk):
            tr(pt[:bn, b * 64:(b + 1) * 64], kpe[:64, b * bn:(b + 1) * bn], ident_f[:64, :64], [r_kpe, r_idf], [pr])
        cp(stk[:bn, :, :], pt[:bn, :nblk * 64].rearrange("p (b e) -> p b e", e=64), [pr], [r_stk])
        if kind == "p":
            store(kr_p[i][tile["tok0"]:tile["tok0"] + T, :].rearrange("(b p) e -> p b e", p=128), stk[:, :, :],
                  [r_stk], [r_krout[i]])
        else:
            store(kr_s[i].rearrange("(b p) e -> p b e", p=64), stk[:64, :, :], [r_stk], ())

        chk("p1c")
        B = mk_alloc(pbase)
        zx, r_zx = B("zx", 8 * nseg * E, F32); zx = zx.rearrange("p (c g e) -> p c g e", g=nseg, e=E)
        tA, r_tA = B("tA", nseg * E, F32); tA = tA.rearrange("p (g e) -> p g e", e=E)
        tB, r_tB = B("tB", nseg * E, F32); tB = tB.rearrange("p (g e) -> p g e", e=E)
        pT, r_pT = B("pT", 8 * T, BF16); pT = pT.rearrange("p (c t) -> p c t", t=T)
        pst, r_pst = B("pst", 1024, F32)
        if kind == "s":
            zst, r_zst = B("zst", 1024, F32)
        for pz in range(2):
            wt, wr = panel512(1088 + 512 * pz)
            for oc in range(4):
                pt, _, pr = nps()
                for kc in range(DC):
                    mm(pt[:, :T], v512(wt)[:, kc, oc * 128:(oc + 1) * 128], xn[:, kc, :T], kc == 0, kc == DC - 1,
                       [wr, r_xn[kc]], [pr])
                cp(zx[:, pz * 4 + oc, :, 15:E], pt[:, :T].rearrange("p (g l) -> p g l", l=L), [pr], [r_zx],
                   eng="act" if oc % 2 else "dve")
        if kind == "p":
            cp(zx[:, :, 0, 0:15], hpool_p[:, i, :, :], [r_hpool], [r_zx])
        else:
            for s in range(nseg):
                ld(zst[:15, :], s_pool[i, s], (), [r_zst])
                pt, _, pr = nps()
                for zc in range(8):
                    tr(pt[:, zc * 15:(zc + 1) * 15], zst[:15, zc * 128:(zc + 1) * 128], ident_f[:15, :15], [r_zst, r_idf], [pr])
                cp(zx[:, :, s, 0:15], pt[:, :120].rearrange("p (c e) -> p c e", e=15), [pr], [r_zx])
        for zc in range(8):
            gi = zc // 2
            w = 2 << gi
            cur, r_cur = zx[:, zc], r_zx
            d = 1
            k = 0
            while d < w:
                lo = 2 * d - 1
                nxt, r_nxt = (tA, r_tA) if k % 2 == 0 else (tB, r_tB)
                tt(nxt[:, :, lo:E], cur[:, :, lo:E], cur[:, :, lo - d:E - d], ALU.add, [r_cur], [r_nxt])
                cur, r_cur = nxt, r_nxt
                d *= 2
                k += 1
            pv = pT[:, zc, :].rearrange("p (g l) -> p g l", l=L)
            stt(pv, cur[:, :, 15:E], 1.0 / w, zx[:, zc, :, 15:E], ALU.mult, ALU.subtract, [r_cur, r_zx], [r_pT])
            if kind == "p" and tile["first"]:
                tt(pst[:, 0:16], cur[:, 0, 15:31], rcT[:, gi * 16:(gi + 1) * 16], ALU.mult, [r_cur, r_rc], [r_pst])
                tt(pT[:, zc, 0:16], pst[:, 0:16], zx[:, zc, 0, 15:31], ALU.subtract, [r_pst, r_zx], [r_pT])
        if kind == "p":
            cp(hpool_p[:, i, :, :], zx[:, :, 0, L:L + 15], [r_zx], [r_hpool])
        if kind == "s" or tile["last"]:
            for s in range(nseg):
                for h2 in range(2):
                    pt, _, pr = nps()
                    for q in range(4):
                        zc = h2 * 4 + q
                        tr(pt[:15, q * 128:(q + 1) * 128], zx[:, zc, s, L:L + 15], ident_f[:], [r_zx, r_idf], [pr])
                    cp(pst[:15, h2 * 512:(h2 + 1) * 512], pt[:15, :], [pr], [r_pst])
                store(pool_p[i] if kind == "p" else pool_s[i, s], pst[:15, :], [r_pst], ())
        wt, wr = wload([(lambda t: t[:, 0:2048].rearrange("p (g d) -> p g d", d=256),
                         w_pool[i].rearrange("g (cc p) d -> p (g cc) d", p=128))], key=f"pool{i}", n_used=2048)
        wp = wt[:, 0:2048].rearrange("p (g d) -> p g d", d=256)
        for gi in range(4):
            for dch in range(2):
                pt, _, pr = nps()
                for cc in range(2):
                    mm(pt[:, :T], wp[:, gi * 2 + cc, dch * 128:(dch + 1) * 128], pT[:, 2 * gi + cc, :], cc == 0, cc == 1,
                       [wr, r_pT], [pr])
                col = V_PSC + 8 * i + 2 * gi + dch
                ts(xn[:, 8 + 2 * gi + dch, :T], pt[:, :T], vec[:, col:col + 1], None, ALU.mult, None, [pr, r_vec], [r_xn[8 + 2 * gi + dch]])

        chk("p2")
        B = mk_alloc(pbase)
        nsmax = nkt + 64 if kind == "s" else 2048
        Ssb, r_S = B("Ssb", nsmax, F32)
        Sbufs = [(Ssb, r_S)]
        if kind == "p":
            Sbufs.append(B("Ssb2", nsmax, F32))
        r_pm = [Res("pm0"), Res("pm1")]
        Pb, r_P = B("Pb", nsmax, BF16)
        nbmax = nsmax // 128 + (1 if nsmax % 128 else 0)
        PTs, r_PT = B("PTs", nbmax * 128, BF16); PTs = PTs.rearrange("p (b r) -> p b r", r=128)
        QT, r_QT = B("QT", 5 * 512, BF16); QT = QT.rearrange("p (c r) -> p c r", r=512)
        osb, r_osb = B("osb", 512, BF16)
        oT, r_oT = B("oT", 4 * 512, BF16); oT = oT.rearrange("p (c r) -> p c r", r=512)
        if kind == "s":
            oacc, r_oacc = B("oacc", 4 * 512, F32); oacc = oacc.rearrange("p (g r) -> p g r", r=512)
        wt, wr = wload([(lambda t: t[:, 0:4096].rearrange("p (k n) -> p k n", n=1024), w_uk[i].rearrange("(k p) n -> p k n", p=128)),
                        (lambda t: t[:, 4096:8192].rearrange("p (k n) -> p k n", n=1024), w_uv[i].rearrange("(k p) n -> p k n", p=128))],
                       key=f"ukv{i}")
        uk = wt[:, 0:4096].rearrange("p (k n) -> p k n", n=1024)
        uv = wt[:, 4096:8192].rearrange("p (k n) -> p k n", n=1024)
        sB = wslot()
        tB_, rB = Wsl[sB]
        ukT = tB_[:, 0:4096].rearrange("p (h r) -> p h r", r=512)
        for h in range(8):
            _, ptb, pr = nps()
            for rc in range(4):
                tr(ptb[:, rc * 128:(rc + 1) * 128], uk[:, rc, h * 128:(h + 1) * 128], ident_b[:], [wr, r_idb], [pr])
            cp(ukT[:, h, :], ptb[:, 0:512], [pr], [rB], eng="act" if h % 2 else "dve")

        PM, MLOC, NEGB, RSUM, ALPHA, MNEW, RL = 0, 16, 17, 18, 19, 29, 28

        def sc(c):
            return stat[:, c:c + 1]

        def build_q(col):
            for rc in range(4):
                pt, _, pr = nps()
                for h in range(8):
                    mm(pt[:, h * 64:(h + 1) * 64], ukT[:, h, rc * 128:(rc + 1) * 128], qn[:, h, col:col + 64], True, True,
                       [rB, r_qn], [pr])
                cp(QT[:, rc, :], pt[:, :], [pr], [r_QT], eng="act" if rc % 2 else "dve")
            cp(QT[:64, 4, :].rearrange("p (h q) -> p h q", q=64), qr[:64, :, col:col + 64], [r_qr], [r_QT])

        def attend1a(rg, nl, nc0, nv):
            blocks = [(KT, r_KT, k0, min(512, nl - k0), k0) for k0 in range(0, nl, 512)]
            if nv > 0:
                blocks.append((nKT, r_nKT, nc0, nv, nl))
            out = []
            for j, (kt, rkt, k0, n, dcol) in enumerate(blocks):
                pt, _, pr = nps()
                for rc in range(4):
                    mm(pt[:, :n], QT[:, rc, rg * 128:(rg + 1) * 128], kt[:, rc, k0:k0 + n], rc == 0, False, [r_QT, rkt], [pr])
                mm(pt[:, :n], QT[:64, 4, rg * 128:(rg + 1) * 128], kt[:64, 4, k0:k0 + n], False, True, [r_QT, rkt], [pr])
                out.append((pt, pr, n, dcol))
            return out

        def attend1b(buf, blks):
            Ssb, r_S = Sbufs[buf]
            for j, (pt, pr, n, dcol) in enumerate(blks):
                act(Ssb[:, dcol:dcol + n], pt[:, :n], AF.Copy, [pr], [r_S])
                rmax(sc(PM + 8 * buf + j), Ssb[:, dcol:dcol + n], [r_S], [r_pm[buf]])
            return len(blks)

        def attend2(rg, nl, nc0, nv, newblks, sbi, nsb, buf, nblocks, mid=None):
            Ssb, r_S = Sbufs[buf]
            nk = nl + nv
            rmax(sc(MLOC), stat[:, PM + 8 * buf:PM + 8 * buf + nblocks], [r_pm[buf]], [r_stat])
            mrun = sc(20 + rg); lrun = sc(24 + rg)
            if sbi == 0:
                ts(sc(NEGB), sc(MLOC), -MLA_SCALE, None, ALU.mult, None, [r_stat], [r_stat])
                cp(mrun, sc(MLOC), [r_stat], [r_stat])
            else:
                tt(sc(MNEW), mrun, sc(MLOC), ALU.max, [r_stat], [r_stat])
                ts(sc(NEGB), sc(MNEW), -MLA_SCALE, None, ALU.mult, None, [r_stat], [r_stat])
                act(sc(ALPHA), mrun, AF.Exp, [r_stat], [r_stat], bias=sc(NEGB), scale=MLA_SCALE)
                cp(mrun, sc(MNEW), [r_stat], [r_stat])
            act(Pb[:, :nk], Ssb[:, :nk], AF.Exp, [r_S, r_stat], [r_P, r_stat], bias=sc(NEGB), scale=MLA_SCALE, accum=sc(RSUM))
            if sbi == 0:
                cp(lrun, sc(RSUM), [r_stat], [r_stat])
            else:
                stt(lrun, lrun, sc(ALPHA), sc(RSUM), ALU.mult, ALU.add, [r_stat], [r_stat])
            if mid is not None:
                mid()
            kb = [(KVx, r_KV, j, 128, j * 128) for j in range(nl // 128)]
            c0 = nl
            for (b, n) in newblks:
                kb.append((nKV, r_nKV, b, n, c0))
                c0 += n
            for g0 in range(0, len(kb), 8):
                _, ptb, pr = nps()
                grp = kb[g0:g0 + 8]
                for q, (kv, rkv, b, n, pc) in enumerate(grp):
                    tr(ptb[:n, q * 128:(q + 1) * 128], Pb[:, pc:pc + n], ident_b[:], [r_P, r_idb], [pr])
                cp(PTs[:, g0:g0 + len(grp), :], ptb[:, 0:len(grp) * 128].rearrange("p (b r) -> p b r", r=128), [pr], [r_PT],
                   eng="act" if (g0 // 8) % 2 else "dve")
            po, _, pro = nps()
            for j, (kv, rkv, b, n, pc) in enumerate(kb):
                mm(po[:, :], PTs[:n, j, :], kv[:n, b, 0:512], j == 0, j == len(kb) - 1, [r_PT, rkv], [pro])
            last = sbi == nsb - 1
            if nsb == 1:
                recip(sc(RL), lrun, [r_stat], [r_stat])
                ts(osb[:, :], po[:, :], sc(RL), None, ALU.mult, None, [pro, r_stat], [r_osb])
            elif sbi == 0:
                cp(oacc[:, rg, :], po[:, :], [pro], [r_oacc])
            else:
                stt(oacc[:, rg, :], oacc[:, rg, :], sc(ALPHA), po[:, :], ALU.mult, ALU.add, [r_oacc, r_stat, pro], [r_oacc])
                if last:
                    recip(sc(RL), lrun, [r_stat], [r_stat])
                    ts(osb[:, :], oacc[:, rg, :], sc(RL), None, ALU.mult, None, [r_oacc, r_stat], [r_osb])
            if last:
                _, ptb, pr = nps()
                for rc in range(4):
                    tr(ptb[:, rc * 128:(rc + 1) * 128], osb[:, rc * 128:(rc + 1) * 128], ident_b[:], [r_osb, r_idb], [pr])
                cp(oT[:, :, rg * 128:(rg + 1) * 128], ptb[:, 0:512].rearrange("p (c r) -> p c r", r=128), [pr], [r_oT])

        def ymla(col):
            pt, _, pr = nps()
            for h in range(8):
                for rc in range(4):
                    mm(pt[:, h * 64:(h + 1) * 64], uv[:, rc, h * 128:(h + 1) * 128], oT[:, rc, h * 64:(h + 1) * 64],
                       rc == 0, rc == 3, [wr, r_oT], [pr])
            cp(xn[:, 0:8, col:col + 64], pt[:, :].rearrange("p (h q) -> p h q", q=64), [pr], r_xn[0:8])

        if kind == "p":
            nl = segs[0]["n_prev"]
            for c in range(T // 64):
                if c == 0:
                    build_q(0)
                nv = 64 * (c + 1)
                newblks = [(b, min(128, nv - 128 * b)) for b in range((nv + 127) // 128)]
                if c == 0:
                    pend = (0, attend1b(0, attend1a(0, nl, 0, nv)), 0, nv, newblks)
                for rg in range(4):
                    cur = pend
                    nbuf = (cur[2] + 1) % 2
                    nxt = None
                    if rg < 3:
                        nxt = (rg + 1, attend1a(rg + 1, nl, 0, nv), nv, newblks)
                    elif c + 1 < T // 64:
                        build_q(64 * (c + 1))
                        nv2 = 64 * (c + 2)
                        nb2 = [(b, min(128, nv2 - 128 * b)) for b in range((nv2 + 127) // 128)]
                        nxt = (0, attend1a(0, nl, 0, nv2), nv2, nb2)
                    holder = {}

                    def mid(nxt=nxt, nbuf=nbuf, holder=holder):
                        if nxt is not None:
                            holder["p"] = (nxt[0], attend1b(nbuf, nxt[1]), nbuf, nxt[2], nxt[3])
                    attend2(cur[0], nl, 0, cur[3], cur[4], 0, 1, cur[2], cur[1], mid=mid)
                    if nxt is not None:
                        pend = holder["p"]
                ymla(64 * c)
        else:
            for s in range(nseg):
                build_q(64 * s)
                for sbi in range(2):
                    load_keys(c_lat[i, s, 2048 * sbi:2048 * (sbi + 1), :], c_kr[i, s, 2048 * sbi:2048 * (sbi + 1), :], 2048, ())
                    for rg in range(4):
                        if sbi == 0:
                            nbk = attend1b(0, attend1a(rg, 2048, 0, 0))
                            attend2(rg, 2048, 0, 0, [], 0, 2, 0, nbk)
                        else:
                            nbk = attend1b(0, attend1a(rg, 2048, 64 * s, 64))
                            attend2(rg, 2048, 64 * s, 64, [(s, 64)], 1, 2, 0, nbk)
                ymla(64 * s)

        chk("p3", xn[:, :, :].rearrange("p c t -> p (c t)") if T == 512 else None, r_xn)
        for pz in range(4):
            wt, wr = wload([(lambda t: t[:, 0:8192], w_out_a[i, pz])], key=f"outa{i}_{pz}")
            for oc in range(4):
                dc = pz * 4 + oc
                pt, _, pr = nps()
                for kc in range(DC):
                    mm(pt[:, :T], v512(wt)[:, kc, oc * 128:(oc + 1) * 128], xn[:, kc, :T], kc == 0, kc == DC - 1, [wr, r_xn[kc]], [pr])
                tt(xT[:, dc, :T], xT[:, dc, :T], pt[:, :T], ALU.add, [r_xT, pr], [r_xT])

    r_SstS = [Res("SstS0"), Res("SstS1")]

    def odd_mixer(tile, i):
        st["psn"] = 2
        st["psb"] = 4
        T = tile["T"]; segs = tile["segs"]; nseg = len(segs); kind = tile["kind"]
        NCH = T // 32
        A = mk_alloc()
        og, r_og = A("og", 16 * T, BF16); og = og.rearrange("p (h t) -> p h t", t=T)
        Sbf, r_Sbf = A("Sbf", 128, BF16)
        f32t = {}
        for nm in ("qf", "ff", "lf", "bb", "eb", "enb", "kf", "of0", "of1", "sq_", "sg_", "rs0", "rs1"):
            f32t[nm] = A(nm, T, F32)
        b16t = {}
        for nm in ("Qt", "Kt", "Kh", "vb", "gs0", "gs1", "sqo0", "sqo1"):
            b16t[nm] = A(nm, T, BF16)
        vt, r_vt = A("vt", NCH * 128, BF16); vt = vt.rearrange("p (c r) -> p c r", r=128)
        kt, r_kt = A("kt", NCH * 128, BF16); kt = kt.rearrange("p (c r) -> p c r", r=128)
        AmT, r_Am = A("AmT", T, BF16)
        rs2, r_rs2 = A("rs2", T, F32)
        lb = lambda h: lbv[:, i, 0, h:h + 1]
        oml = lambda h: lbv[:, i, 1, h:h + 1]
        gon = vec[:, V_GON + i:V_GON + i + 1]

        def v4(t):
            return t[:, :].rearrange("p (a k n) -> p a k n", a=4, n=128)

        def proj_steps(h):
            wt, wr = wload([(lambda t: t[:, 0:8192], w_in_c[i, h])], key=f"inc{i}_{h}")
            wv = v4(wt)
            steps = []
            for a in range(4):
                pt, _, pr = PS[a]
                for k0 in range(0, DC, 4):
                    def stp(a=a, k0=k0, pt=pt, pr=pr):
                        for kc in range(k0, k0 + 4):
                            mm(pt[:, :T], wv[:, a, kc, :], xn[:, kc, :T], kc == 0, kc == DC - 1, [wr, r_xn[kc]], [pr])
                    steps.append(stp)
            return steps

        def gla_head(h, S_ap, r_S, filler):
            qf, r_qf = f32t["qf"]; ff, r_ff = f32t["ff"]; lf, r_lf = f32t["lf"]; bb, r_bb = f32t["bb"]
            eb, r_eb = f32t["eb"]; enb, r_enb = f32t["enb"]; kf, r_kf = f32t["kf"]; of, r_of = f32t["of%d" % (h % 2)]
            Qt, r_Qt = b16t["Qt"]; Kt, r_Kt = b16t["Kt"]; Kh, r_Kh = b16t["Kh"]; vb, r_vb = b16t["vb"]
            gs, r_gs = b16t["gs%d" % (h % 2)]; sqo, r_sqo = b16t["sqo%d" % (h % 2)]
            sq_, r_sq = f32t["sq_"]; sg_, r_sg = f32t["sg_"]; rs2, r_rs2 = f32t["rs%d" % (h % 2)]
            act(sq_[:, :T], PS[0][0][:, :T], AF.Sigmoid, [PS[0][2]], [r_sq])
            act(sg_[:, :T], PS[3][0][:, :T], AF.Sigmoid, [PS[3][2]], [r_sg])
            act(ff[:, :T], PS[1][0][:, :T], AF.Sigmoid, [PS[1][2]], [r_ff])
            act(vb[:, :T], PS[2][0][:, :T], AF.Copy, [PS[2][2]], [r_vb])
            tt(qf[:, :T], PS[0][0][:, :T], sq_[:, :T], ALU.mult, [PS[0][2], r_sq], [r_qf])
            tt(gs[:, :T], PS[3][0][:, :T], sg_[:, :T], ALU.mult, [PS[3][2], r_sg], [r_gs])
            ts(ff[:, :T], ff[:, :T], oml(h), lb(h), ALU.mult, ALU.add, [r_ff, r_lbv], [r_ff])
            ts(ff[:, :T], ff[:, :T], 1e-30, None, ALU.max, None, [r_ff], [r_ff])
            act(lf[:, :T], ff[:, :T], AF.Ln, [r_ff], [r_lf])
            ts(kf[:, :T], ff[:, :T], -1.0, 1.0, ALU.mult, ALU.add, [r_ff], [r_kf])
            S.op("dve", lambda e: e.tensor_tensor_scan(bb[:, :T], scanm[:, :T], lf[:, :T], 0.0, ALU.mult, ALU.add),
                 [r_scanm, r_lf], [r_bb])
            act(eb[:, :T], bb[:, :T], AF.Exp, [r_bb], [r_eb])
            act(enb[:, :T], bb[:, :T], AF.Exp, [r_bb], [r_enb], scale=-1.0)
            tt(Qt[:, :T], qf[:, :T], eb[:, :T], ALU.mult, [r_qf, r_eb], [r_Qt])
            tt(Kt[:, :T], kf[:, :T], enb[:, :T], ALU.mult, [r_kf, r_enb], [r_Kt])
            ebe = eb[:, 31:T:32]
            tt(Kh[:, :T].rearrange("p (c t) -> p c t", t=32), Kt[:, :T].rearrange("p (c t) -> p c t", t=32),
               ebe.unsqueeze(2).to_broadcast([128, NCH, 32]), ALU.mult, [r_Kt, r_eb], [r_Kh])
            for src, r_src, dst, r_dst in ((vb, r_vb, vt, r_vt), (Kh, r_Kh, kt, r_kt)):
                for g0 in range(0, NCH, 8):
                    _, ptb, pr = nps()
                    for q in range(8):
                        j = g0 + q
                        tr(ptb[:32, q * 128:(q + 1) * 128], src[:, j * 32:(j + 1) * 32], ident_b[:], [r_src, r_idb], [pr])
                    cp(dst[:32, g0:g0 + 8, :], ptb[:32, :].rearrange("p (c r) -> p c r", r=128), [pr], [r_dst],
                       eng="act" if (g0 // 8) % 2 else "dve")
            pa, _, pra = nps()
            for j in range(NCH):
                mm(pa[:32, j * 32:(j + 1) * 32], Kt[:, j * 32:(j + 1) * 32], Qt[:, j * 32:(j + 1) * 32], True, True,
                   [r_Kt, r_Qt], [pra])
            tt(AmT[:32, :T], pa[:32, :T], glam[:, :T], ALU.mult, [pra, r_glam], [r_Am])
            po, _, pro = PS[6]
            for j in range(NCH):
                cp(Sbf[:, :], S_ap(j), [r_S], [r_Sbf], eng="act")
                mm(po[:, j * 32:(j + 1) * 32], Sbf[:, :], Qt[:, j * 32:(j + 1) * 32], True, False, [r_Sbf, r_Qt], [pro])
                mm(po[:, j * 32:(j + 1) * 32], vt[:32, j, :], AmT[:32, j * 32:(j + 1) * 32], False, True, [r_vt, r_Am], [pro])
                pS, _, prS = PS[7]
                mm(pS[:, 0:128], kt[:32, j, :], vt[:32, j, :], True, True, [r_kt, r_vt], [prS])
                stt(S_ap(j), S_ap(j), eb[:, j * 32 + 31:j * 32 + 32], pS[:, 0:128], ALU.mult, ALU.add,
                    [r_S, r_eb, prS], [r_S])
                filler(j)
            act(of[:, :T], po[:, :T], AF.Copy, [pro], [r_of])
            act(sqo[:, :T], po[:, :T], AF.Square, [pro], [r_sqo])

            def E2():
                p2, _, pr2 = nps()
                mm(p2[:, :T], ones_b[:], sqo[:, :T], True, True, [r_ones, r_sqo], [pr2])
                act(rs2[:, :T], p2[:, :T], AF.Ln, [pr2, r_eps], [r_rs2], bias=epsT[:, 0:1], scale=1.0 / 128)
                act(rs2[:, :T], rs2[:, :T], AF.Exp, [r_rs2], [r_rs2], scale=-0.5)
                stt(of[:, :T], of[:, :T], gon, rs2[:, :T], ALU.mult, ALU.mult, [r_of, r_vec, r_rs2], [r_of])
                tt(og[:, h, :], of[:, :T], gs[:, :T], ALU.mult, [r_of, r_gs], [r_og])
            return E2

        def run_heads(S_fn, r_S_fn, pre=None, post=None):
            steps = proj_steps(0)
            for s_ in steps:
                s_()
            pend = [None]
            for h in range(16):
                nxt = proj_steps(h + 1) if h + 1 < 16 else []
                per = (len(nxt) + NCH - 1) // NCH if nxt else 0

                def filler(j, nxt=nxt, per=per):
                    for _ in range(per):
                        if nxt:
                            nxt.pop(0)()
                    if j == 1 and pend[0] is not None:
                        pend[0]()
                        pend[0] = None
                if pre is not None:
                    pre(h)
                e2 = gla_head(h, S_fn(h), r_S_fn(h), filler)
                if pend[0] is not None:
                    pend[0]()
                pend[0] = e2
                while nxt:
                    nxt.pop(0)()
                if post is not None:
                    post(h)
            pend[0]()

        if kind == "p":
            run_heads(lambda h: (lambda j, h=h: Sst[:, i, h, :]), lambda h: r_Sst)
            if tile["last"]:
                store(hg_p[i].rearrange("h f v -> f h v"), Sst[:, i, :, :], [r_Sst], ())
        else:
            run_heads(lambda h: (lambda j, hb=h % 2: Sst[:, hb, j // 2, :]), lambda h: r_SstS[h % 2],
                      pre=lambda h: ld(Sst[:, h % 2, 0:4, :], s_hg[i, :, h].rearrange("s f v -> f s v"), (), [r_SstS[h % 2], r_Sst]),
                      post=lambda h: store(hg_s[i, :, h].rearrange("s f v -> f s v"), Sst[:, h % 2, 0:4, :], [r_SstS[h % 2]], ()))
        st["psn"] = 8
        st["psb"] = 0

        def v512(t):
            return t[:, :].rearrange("p (k n) -> p k n", n=512)
        for pz in range(4):
            wt, wr = wload([(lambda t: t[:, 0:8192], w_out_c[i, pz])], key=f"outc{i}_{pz}")
            for oc in range(4):
                dc = pz * 4 + oc
                pt, _, pr = nps()
                for kc in range(DC):
                    mm(pt[:, :T], v512(wt)[:, kc, oc * 128:(oc + 1) * 128], og[:, kc, :], kc == 0, kc == DC - 1, [wr, r_og], [pr])
                tt(xT[:, dc, :T], xT[:, dc, :T], pt[:, :T], ALU.add, [r_xT, pr], [r_xT])

    tiles = []
    for i in range(N_PT):
        tiles.append(dict(kind="p", T=T_P, tok0=i * T_P, first=i == 0, last=i == N_PT - 1, ropecol=i * T_P,
                          segs=[dict(seq=0, n_prev=i * T_P, col0=0, L=T_P)]))
    tiles.append(dict(kind="s", T=T_S, tok0=0, first=False, last=True, ropecol=2048,
                      segs=[dict(seq=s, n_prev=4096, col0=64 * s, L=64) for s in range(4)]))
    if tile_sel is not None:
        tiles = [tiles[i] for i in tile_sel]

    try:
        for tile in tiles:
            run_tile(tile)
    except _Stop:
        pass
    S.emit()
    return nc


def _fm(v, n):
    return np.ascontiguousarray(np.asarray(v, np.float32).reshape(n, 128).T)


def _consts():
    f32 = np.float32
    inv = (f32(10000.0) ** (-(np.arange(0, 64, 2, dtype=f32)) / f32(64))).astype(f32)
    pos = np.concatenate([np.arange(2048), np.tile(4096 + np.arange(64), 4)]).astype(f32)
    ang = (pos[:, None] * inv[None, :]).astype(f32)
    cos = np.cos(ang).astype(f32).T
    sin = np.sin(ang).astype(f32).T
    rope = np.stack([np.concatenate([cos, cos], 0), np.concatenate([-sin, sin], 0)], 0)
    rc = np.zeros((4, 16), f32)
    for gi, w in enumerate((2, 4, 8, 16)):
        rc[gi] = 1.0 / np.minimum(np.arange(16) + 1, w)
    rc = np.tile(rc.reshape(1, 64), (128, 1))
    scanm = np.ones((128, 512), f32)
    scanm[:, ::32] = 0.0
    tri = (np.arange(32)[:, None] <= np.arange(32)[None, :]).astype(f32)
    glam = np.tile(tri, (1, 16))
    return dict(ident=np.eye(128, dtype=f32), rope=np.ascontiguousarray(rope, f32), rc=np.ascontiguousarray(rc),
                scanm=scanm, glam=np.ascontiguousarray(glam))


_PROG = {}


def run_cores(inputs, cores, n_layers=NL, tile_sel=None):
    I = {k_: np.asarray(v) for k_, v in inputs.items()}
    key = (n_layers, tuple(tile_sel) if tile_sel is not None else None)
    nc = build_program(n_layers, tile_sel)
    vec = np.zeros((128, NV), np.float32)
    for l in range(4):
        vec[:, V_GMIX + 16 * l:V_GMIX + 16 * l + 16] = _fm(I["g_mix"][l], 16)
        vec[:, V_GFFN + 16 * l:V_GFFN + 16 * l + 16] = _fm(I["g_ffn"][l], 16)
    vec[:, V_GFIN:V_GFIN + 16] = _fm(I["g_final"], 16)
    for i in range(2):
        vec[:, V_GQA + 4 * i:V_GQA + 4 * i + 4] = _fm(I["g_qa"][i], 4)
        vec[:, V_GKVA + 4 * i:V_GKVA + 4 * i + 4] = _fm(I["g_kva"][i], 4)
        vec[:, V_PSC + 8 * i:V_PSC + 8 * i + 8] = _fm(I["pool_scale"][i], 8)
        vec[:, V_LBP + 16 * i:V_LBP + 16 * i + 16] = _fm(I["lb_param"][i], 16)
        vec[:, V_GON + i] = I["g_onorm"][i]
    cwb = np.zeros((4, 128, 4 * 88), np.float32)
    for l in range(4):
        for j in range(3):
            cwb[l, :, 88 * j:88 * j + 88] = _fm(I["conv_w"][l, j], 88)
        cwb[l, :, 264:352] = _fm(I["conv_b"][l], 88)
    shared = dict(vec=vec, cwb=cwb, **_consts())
    for nm in ("w_in_a", "w_qb", "w_pool"):
        shared[nm] = np.ascontiguousarray(I[nm], np.float32)

    def pm(w, cols):
        K = w.shape[0]
        return np.ascontiguousarray(w[:, cols].reshape(K // 128, 128, len(cols)).transpose(1, 0, 2).reshape(128, -1))

    ar = np.arange
    shared["w_up"] = np.stack([np.stack([pm(I["w_up"][l], np.concatenate([ar(256 * p, 256 * p + 256), ar(DFF + 256 * p, DFF + 256 * p + 256)]))
                                         for p in range(22)]) for l in range(4)])
    shared["w_down"] = np.stack([np.stack([pm(I["w_down"][l], ar(128 * d, 128 * d + 128)) for d in range(16)]) for l in range(4)])
    shared["w_in_a4"] = np.stack([np.stack([pm(I["w_in_a"][i], ar(c0, c0 + 512)) for c0 in (0, 512, 1088, 1600)]) for i in range(2)])
    shared["w_out_a"] = np.stack([np.stack([pm(I["w_out_a"][i], ar(512 * p, 512 * p + 512)) for p in range(4)]) for i in range(2)])
    shared["w_out_c"] = np.stack([np.stack([pm(I["w_out_c"][i], ar(512 * p, 512 * p + 512)) for p in range(4)]) for i in range(2)])

    def pm_inc(w, h):
        cols = np.concatenate([ar(2048 * a + 128 * h, 2048 * a + 128 * h + 128) for a in range(4)])
        x = w[:, cols].reshape(16, 128, 4, 128).transpose(1, 2, 0, 3)
        return np.ascontiguousarray(x.reshape(128, -1))
    shared["w_in_c"] = np.stack([np.stack([pm_inc(I["w_in_c"][i], h) for h in range(16)]) for i in range(2)])
    shared["w_uk"] = np.ascontiguousarray(I["w_uk"].reshape(2, 512, 1024), np.float32)
    shared["w_uv"] = np.ascontiguousarray(I["w_uv"].reshape(2, 512, 1024), np.float32)
    in_maps = []
    for c in cores:
        m = dict(shared)
        s4 = slice(4 * c, 4 * c + 4)
        m["xp"] = np.ascontiguousarray(I["x_prompt"][c])
        m["xs"] = np.ascontiguousarray(I["x_sample"][s4].reshape(256, D))
        m["c_lat"] = np.ascontiguousarray(I["cache_mla_latent"][:, s4])
        m["c_kr"] = np.ascontiguousarray(I["cache_mla_krope"][:, s4])
        m["s_pool"] = np.ascontiguousarray(I["state_pool"][:, s4])
        m["s_hg"] = np.ascontiguousarray(I["state_hgrn"][:, s4])
        m["s_cv"] = np.ascontiguousarray(I["state_ffn_conv"][:, s4])
        in_maps.append(m)
    res = run_bass_kernel_spmd(nc, in_maps, core_ids=list(range(len(cores))))
    return res.results


def kernel(**inputs):
    R = run_cores(inputs, list(range(8)))
    f = np.float32
    y_prompt = np.stack([r["y_p"] for r in R], 0).astype(f)
    y_sample = np.concatenate([r["y_s"].reshape(4, 64, D) for r in R], 0).astype(f)
    lat_p = np.stack([r["lat_p"] for r in R], 1).astype(f)
    kr_p = np.stack([r["kr_p"] for r in R], 1).astype(f)
    pool_p = np.stack([r["pool_p"] for r in R], 1).astype(f)
    hg_p = np.stack([r["hg_p"] for r in R], 1).astype(f)
    cv_p = np.stack([r["cv_p"] for r in R], 1).astype(f)
    lat_s = np.concatenate([r["lat_s"].reshape(2, 4, 64, 512) for r in R], 1).astype(f)
    kr_s = np.concatenate([r["kr_s"].reshape(2, 4, 64, 64) for r in R], 1).astype(f)
    pool_s = np.concatenate([r["pool_s"] for r in R], 1).astype(f)
    hg_s = np.concatenate([r["hg_s"] for r in R], 1).astype(f)
    cv_s = np.concatenate([r["cv_s"] for r in R], 1).astype(f)
    return (y_prompt, y_sample, lat_p, kr_p, pool_p, hg_p, cv_p, lat_s, kr_s, pool_s, hg_s, cv_s)
```

```python
import numpy as np
import concourse.bass as bass
import concourse.mybir as mybir
from concourse.bass_utils import run_bass_kernel_spmd

F32 = mybir.dt.float32
BF16 = mybir.dt.bfloat16
AF = mybir.ActivationFunctionType
ALU = mybir.AluOpType
AX = mybir.AxisListType


class Res:
    __slots__ = ("name", "lw", "rd")

    def __init__(self, name):
        self.name = name
        self.lw = None
        self.rd = {}


class Chan:
    __slots__ = ("sem", "val")

    def __init__(self, nc, name):
        self.sem = nc.alloc_semaphore(name)
        self.val = 0


class Arena:
    def __init__(self, nc, nbytes):
        self.t = nc.alloc_sbuf_tensor("arena", [128, nbytes // 4], F32)
        self.tb = self.t.bitcast(BF16)
        self.nbytes = nbytes
        self.regs = []

    def view(self, name, off, cols, dtype):
        esz = 4 if dtype == F32 else 2
        nb = cols * esz
        assert off % 4 == 0 and off + nb <= self.nbytes, (name, off, nb, self.nbytes)
        res = Res(name)
        keep = []
        for (s, e, r) in self.regs:
            if s < off + nb and off < e:
                if r.lw is not None:
                    res.rd[("lw", id(r))] = r.lw
                for k, v in r.rd.items():
                    res.rd[(k, id(r))] = v
                if s < off:
                    keep.append((s, off, r))
                if e > off + nb:
                    keep.append((off + nb, e, r))
            else:
                keep.append((s, e, r))
        keep.append((off, off + nb, res))
        self.regs = keep
        base = self.t if dtype == F32 else self.tb
        o = off // esz
        return base[:, o:o + cols], res


class Sched:
    ENGS = ("pe", "act", "dve", "pool", "sp")

    def __init__(self, nc, sync_same_engine=False):
        self.nc = nc
        self.ops = {e: [] for e in self.ENGS}
        self.sync_same = sync_same_engine
        self.same_dist = 3
        self.out_res = []

    def _deps(self, eng, reads, writes):
        deps = []
        for r in reads:
            if r.lw is not None:
                deps.append(r.lw)
        for w in writes:
            if w.lw is not None:
                deps.append(w.lw)
            deps.extend(w.rd.values())
        out = []
        cur = len(self.ops[eng])
        for d in deps:
            if d[0] == "e" and d[1] == eng:
                if eng == "pe" or cur - d[2] > self.same_dist:
                    continue
            out.append(d)
        return out

    def _mark(self, me, reads, writes):
        key = me[1] if me[0] == "e" else id(me[1])
        for r in reads:
            r.rd[key] = me
        for w in writes:
            w.lw = me
            w.rd = {}

    def op(self, eng, fn, reads=(), writes=()):
        deps = self._deps(eng, reads, writes)
        idx = len(self.ops[eng])
        self.ops[eng].append(dict(fn=fn, deps=deps, dma=None))
        self._mark(("e", eng, idx), reads, writes)

    def dma(self, eng, dst, src, chan, reads=(), writes=()):
        deps = self._deps(eng, reads, writes)
        if chan.val > 0:
            deps.append(("d", chan, chan.val))
        chan.val += 16
        me = ("d", chan, chan.val)
        self.ops[eng].append(dict(fn=lambda e: e.dma_start(out=dst, in_=src), deps=deps, dma=chan))
        self._mark(me, reads, writes)

    def emit(self):
        nc = self.nc
        mil = {e: set() for e in self.ENGS}
        for e in self.ENGS:
            for o in self.ops[e]:
                for d in o["deps"]:
                    if d[0] == "e":
                        mil[d[1]].add(d[2])
        milidx = {}
        for e in self.ENGS:
            for k, i in enumerate(sorted(mil[e])):
                milidx[(e, i)] = k + 1
        sems = {e: nc.alloc_semaphore("c_" + e) for e in self.ENGS}
        ops = self.ops
        out_res = self.out_res

        def run(ename, eng):
            waited = {}
            for i, o in enumerate(ops[ename]):
                need = {}
                for d in o["deps"]:
                    if d[0] == "e":
                        k = ("e", d[1])
                        v = milidx[(d[1], d[2])]
                        s = sems[d[1]]
                    else:
                        k = ("d", id(d[1]))
                        v = d[2]
                        s = d[1].sem
                    if waited.get(k, 0) >= v:
                        continue
                    if k not in need or need[k][1] < v:
                        need[k] = (s, v)
                for k, (s, v) in need.items():
                    eng.wait_ge(s, v)
                    waited[k] = v
                inst = o["fn"](eng)
                if o["dma"] is not None:
                    inst.then_inc(o["dma"].sem, 16)
                elif (ename, i) in milidx:
                    inst.then_inc(sems[ename], 1)
            if ename == "sp":
                for r in out_res:
                    if r.val > 0:
                        eng.wait_ge(r.sem, r.val)

        with nc.Block() as block:
            @block.tensor
            def _(e):
                run("pe", e)

            @block.scalar
            def _(e):
                run("act", e)

            @block.vector
            def _(e):
                run("dve", e)

            @block.gpsimd
            def _(e):
                run("pool", e)

            @block.sync
            def _(e):
                run("sp", e)


D = 2048
DC = 16
DFF = 5632
FC = 44
NL = 4
T_P = 512
N_PT = 4
T_S = 256
EPS = 1e-6
MLA_SCALE = 192 ** -0.5
KBMAX = 2112
ARENA_BYTES = 91392

V_GMIX, V_GFFN, V_GFIN, V_GQA, V_GKVA, V_PSC, V_LBP, V_GON, NV = 0, 64, 128, 144, 152, 160, 176, 208, 210


class _Stop(Exception):
    pass


STOP = None
SCRATCH = True


def build_program(n_layers=NL, tile_sel=None):
    nc = bass.Bass("TRN2", target_bir_lowering=False)
    S = Sched(nc)

    def din(name, shape):
        return nc.dram_tensor(name, list(shape), F32, kind="ExternalInput").ap()

    def dout(name, shape):
        return nc.dram_tensor(name, list(shape), F32, kind="ExternalOutput").ap()

    xp = din("xp", [2048, D]); xs = din("xs", [256, D])
    c_lat = din("c_lat", [2, 4, 4096, 512]); c_kr = din("c_kr", [2, 4, 4096, 64])
    s_pool = din("s_pool", [2, 4, 15, 1024]); s_hg = din("s_hg", [2, 4, 16, 128, 128])
    s_cv = din("s_cv", [4, 4, 2, 2 * DFF])
    vec_d = din("vec", [128, NV]); cwb_d = din("cwb", [4, 128, 4 * 88])
    ident_d = din("ident", [128, 128]); rope_d = din("rope", [2, 64, 2048 + 256])
    rc_d = din("rc", [128, 64]); scanm_d = din("scanm", [128, 512]); glam_d = din("glam", [32, 512])
    w_in_a = din("w_in_a", [2, D, 2112]); w_qb = din("w_qb", [2, 512, 1536])
    w_uk = din("w_uk", [2, 512, 1024]); w_uv = din("w_uv", [2, 512, 1024])
    w_pool = din("w_pool", [2, 4, 256, 256]); w_out_a = din("w_out_a", [2, 4, 128, 8192])
    w_in_c = din("w_in_c", [2, 16, 128, 8192]); w_out_c = din("w_out_c", [2, 4, 128, 8192])
    w_up = din("w_up", [4, 22, 128, 8192]); w_down = din("w_down", [4, 16, 128, FC * 128])
    w_in_a4 = din("w_in_a4", [2, 4, 128, 8192])

    y_p = dout("y_p", [2048, D]); y_s = dout("y_s", [256, D])
    lat_p = dout("lat_p", [2, 2048, 512]); kr_p = dout("kr_p", [2, 2048, 64])
    pool_p = dout("pool_p", [2, 15, 1024]); hg_p = dout("hg_p", [2, 16, 128, 128])
    cv_p = dout("cv_p", [4, 2, 2 * DFF])
    lat_s = dout("lat_s", [2, 256, 512]); kr_s = dout("kr_s", [2, 256, 64])
    pool_s = dout("pool_s", [2, 4, 15, 1024]); hg_s = dout("hg_s", [2, 4, 16, 128, 128])
    cv_s = dout("cv_s", [4, 4, 2, 2 * DFF])
    dbg_d = dout("dbg", [128, 8192]) if STOP is not None else None

    def sb(name, shape, dt):
        return nc.alloc_sbuf_tensor(name, list(shape), dt), Res(name)

    xT, r_xT = sb("xT", [128, DC, 512], F32)
    xn, _r_xn_unused = sb("xn", [128, DC, 512], BF16)
    r_xn = [Res("xn%d" % c) for c in range(DC)]
    vec, r_vec = sb("vecs", [128, NV], F32)
    cwb, r_cwb = sb("cwbs", [128, 4 * 88], F32)
    ident_f, r_idf = sb("ident_f", [128, 128], F32)
    ident_b, r_idb = sb("ident_b", [128, 128], BF16)
    ones_b, r_ones = sb("ones_b", [128, 128], BF16)
    ropeT, r_rope = sb("ropeT", [64, 2, 512], F32)
    scanm, r_scanm = sb("scanms", [128, 512], F32)
    glam, r_glam = sb("glams", [32, 512], F32)
    rcT, r_rc = sb("rcT", [128, 64], F32)
    lbv, r_lbv = sb("lbv", [128, 2, 2, 16], F32)
    sqb = [sb(f"sqb{i}", [128, 512], BF16) for i in range(2)]
    rstd, r_rstd = sb("rstd", [128, 512], F32)
    hcv_p, r_hcvp = sb("hcv_p", [128, 4, 88, 2], F32)
    hcv_s, r_hcvs = sb("hcv_s", [128, 88, 4, 2], F32)
    hpool_p, r_hpool = sb("hpool_p", [128, 2, 8, 15], F32)
    Sst, r_Sst = sb("Sst", [128, 2, 16, 128], F32)
    stat, r_stat = sb("stat", [128, 64], F32)
    Wsl = [sb(f"wslot{i}", [128, 8192], BF16) for i in range(2)]
    arena = Arena(nc, ARENA_BYTES)

    PS = []
    for i in range(8):
        t = nc.alloc_psum_tensor(f"ps{i}", [128, 512], F32)
        PS.append((t, t.bitcast(BF16), Res(f"ps{i}")))
    st = dict(ps=0, w=0, ld=0, ldp=0, stc=0, sq=0)

    def nps():
        st["ps"] = (st["ps"] + 1) % st.get("psn", 6)
        return PS[st.get("psb", 0) + st["ps"]]

    wch = [[Chan(nc, f"w{s}_{p}") for p in range(4)] for s in range(3)]
    ldch = [Chan(nc, f"ld{i}") for i in range(6)]
    ldpch = [Chan(nc, f"ldp{i}") for i in range(4)]
    stch = [Chan(nc, f"st{i}") for i in range(8)]
    S.out_res = stch

    def ld(dst, src, reads=(), writes=()):
        st["ld"] = (st["ld"] + 1) % len(ldch)
        S.dma("sp", dst, src, ldch[st["ld"]], reads, writes)

    def ldp(dst, src, reads=(), writes=()):
        st["ldp"] = (st["ldp"] + 1) % len(ldpch)
        S.dma("pool", dst, src, ldpch[st["ldp"]], reads, writes)

    def store(dst, src, reads=(), writes=()):
        st["stc"] = (st["stc"] + 1) % len(stch)
        S.dma("sp", dst, src, stch[st["stc"]], reads, writes)

    def wslot():
        st["w"] = (st["w"] + 1) % len(Wsl)
        return st["w"]

    wscr = {}
    scr_ch = [Chan(nc, f"scr{i}") for i in range(4)]

    def wload(parts, key=None, n_used=8192):
        s = wslot()
        t, r = Wsl[s]
        if key is not None and key in wscr:
            scr, rscr = wscr[key]
            S.dma("pool", t[:, 0:n_used], scr.ap(), wch[s][0], [rscr], [r])
            return t, r
        for i, (dv, src) in enumerate(parts):
            S.dma("pool", dv(t), src, wch[s][i % 4], (), [r])
        if key is not None and SCRATCH:
            scr = nc.dram_tensor("scr_" + key, [128, n_used], BF16, kind="Internal")
            rscr = Res("scr_" + key)
            wscr[key] = (scr, rscr)
            st["scr"] = (st.get("scr", 0) + 1) % len(scr_ch)
            S.dma("sp", scr.ap(), t[:, 0:n_used], scr_ch[st["scr"]], [r], [rscr])
        return t, r

    def mm(out, lhsT, rhs, start, stop, reads, writes):
        S.op("pe", lambda e: e.matmul(out, lhsT, rhs, start=start, stop=stop), reads, writes)

    def tr(out, in_, idn, reads, writes):
        S.op("pe", lambda e: e.transpose(out, in_, idn), reads, writes)

    def act(out, in_, func, reads, writes, bias=None, scale=None, accum=None):
        kw = {}
        if bias is not None:
            kw["bias"] = bias
        if scale is not None:
            kw["scale"] = scale
        if accum is not None:
            kw["accum_out"] = accum
        S.op("act", lambda e: e.activation(out, in_, func, **kw), reads, writes)

    def tt(out, a, b, op, reads, writes, eng="dve"):
        S.op(eng, lambda e: e.tensor_tensor(out, a, b, op), reads, writes)

    def ts(out, a, s1, s2, op0, op1, reads, writes, eng="dve"):
        if op1 is None:
            S.op(eng, lambda e: e.tensor_scalar(out, a, s1, None, op0), reads, writes)
        else:
            S.op(eng, lambda e: e.tensor_scalar(out, a, s1, s2, op0, op1), reads, writes)

    def stt(out, a, sc, b, op0, op1, reads, writes):
        S.op("dve", lambda e: e.scalar_tensor_tensor(out, a, sc, b, op0, op1), reads, writes)

    def cp(out, in_, reads, writes, eng="dve"):
        if eng == "act":
            S.op("act", lambda e: e.copy(out, in_), reads, writes)
        else:
            S.op(eng, lambda e: e.tensor_copy(out, in_), reads, writes)

    def rmax(out, in_, reads, writes):
        S.op("dve", lambda e: e.reduce_max(out, in_, AX.X), reads, writes)

    def recip(out, in_, reads, writes):
        S.op("dve", lambda e: e.reciprocal(out, in_), reads, writes)

    def memset(ap, val, writes, eng="dve"):
        S.op(eng, lambda e: e.memset(ap, val), (), writes)

    ld(vec[:], vec_d, (), [r_vec])
    ld(ident_f[:], ident_d, (), [r_idf])
    ld(scanm[:], scanm_d, (), [r_scanm])
    ld(glam[:], glam_d, (), [r_glam])
    ld(rcT[:], rc_d, (), [r_rc])
    cp(ident_b[:], ident_f[:], [r_idf], [r_idb])
    memset(ones_b[:], 1.0, [r_ones])
    p0 = vec[:, V_LBP:V_LBP + 16]; p1 = vec[:, V_LBP + 16:V_LBP + 32]
    sm0 = lbv[:, 0, 1, :]; sm1 = lbv[:, 1, 1, :]
    tt(sm0, p0, p1, ALU.subtract, [r_vec], [r_lbv])
    tt(sm1, p1, p0, ALU.subtract, [r_vec], [r_lbv])
    act(sm0, sm0, AF.Sigmoid, [r_lbv], [r_lbv])
    act(sm1, sm1, AF.Sigmoid, [r_lbv], [r_lbv])
    tt(lbv[:, 0, 0, :], sm0, sm0, ALU.subtract, [r_lbv], [r_lbv])
    tt(lbv[:, 1, 0, :], sm0, sm1, ALU.add, [r_lbv], [r_lbv])
    tt(lbv[:, 1, 0, :], lbv[:, 1, 0, :], sm0, ALU.subtract, [r_lbv], [r_lbv])
    for i in range(2):
        ts(lbv[:, i, 0, :], lbv[:, i, 0, :], 0.0, 1.0, ALU.max, ALU.min, [r_lbv], [r_lbv])
        ts(lbv[:, i, 1, :], lbv[:, i, 0, :], -1.0, 1.0, ALU.mult, ALU.add, [r_lbv], [r_lbv])
    memset(Sst[:], 0.0, [r_Sst])
    memset(hcv_p[:], 0.0, [r_hcvp])
    memset(hpool_p[:], 0.0, [r_hpool])

    epsT, r_eps = sb("epsT", [128, 1], F32)
    memset(epsT[:], EPS, [r_eps])

    def chk(tag, src=None, res=None, n=8192):
        if STOP in (tag, "%s@%d" % (tag, st.get("tile", -1))):
            if src is not None:
                dv, dr = arena.view("dbgbuf", 0, n, F32)
                cp(dv, src, res if isinstance(res, list) else [res], [dr])
                store(dbg_d[:, 0:n], dv, [dr], ())
            raise _Stop()

    def mk_alloc(base=0):
        off = [base]

        def A(name, cols, dt):
            v, r = arena.view(name, off[0], cols, dt)
            off[0] += (cols * (4 if dt == F32 else 2) + 31) // 32 * 32
            return v, r
        A.off = off
        return A

    def rms_stats(src_fn, nch, T, reads, nfeat):
        pt, _, pr = nps()
        for c in range(nch):
            q, qr = sqb[c % 2]
            if c % 2 == 0:
                act(q[:, :T], src_fn(c), AF.Square, reads, [qr])
            else:
                tt(q[:, :T], src_fn(c), src_fn(c), ALU.mult, reads, [qr])
            mm(pt[:, :T], ones_b[:], q[:, :T], c == 0, c == nch - 1, [qr, r_ones], [pr])
        act(rstd[:, :T], pt[:, :T], AF.Ln, [pr, r_eps], [r_rstd], bias=epsT[:, 0:1], scale=1.0 / nfeat)
        act(rstd[:, :T], rstd[:, :T], AF.Exp, [r_rstd], [r_rstd], scale=-0.5)

    def rmsnorm(src_fn, r_src, nch, nfeat, T, g_fn, dst_fn, r_dst):
        rms_stats(src_fn, nch, T, [r_src], nfeat)
        for c in range(nch):
            rd = r_dst[c] if isinstance(r_dst, list) else r_dst
            stt(dst_fn(c), src_fn(c), g_fn(c), rstd[:, :T], ALU.mult, ALU.mult, [r_src, r_vec, r_rstd], [rd])

    def load_x(tile):
        st["psn"] = 8
        st["psb"] = 0
        T = tile["T"]; nb = T // 128
        A = mk_alloc()
        stg, r_stg = A("xstage", nb * D, F32)
        stg = stg.rearrange("p (b d) -> p b d", d=D)
        src = (xp if tile["kind"] == "p" else xs)[tile["tok0"]:tile["tok0"] + T, :]
        ld(stg, src.rearrange("(b p) d -> p b d", p=128), (), [r_stg])
        for dc in range(DC):
            pt, _, pr = nps()
            for b in range(nb):
                tr(pt[:, b * 128:(b + 1) * 128], stg[:, b, dc * 128:(dc + 1) * 128], ident_f[:], [r_stg, r_idf], [pr])
            cp(xT[:, dc, :T], pt[:, :T], [pr], [r_xT], eng="act" if dc % 2 else "dve")
        ld(ropeT[:, 0, :T], rope_d[0, :, tile["ropecol"]:tile["ropecol"] + T], (), [r_rope])
        ld(ropeT[:, 1, :T], rope_d[1, :, tile["ropecol"]:tile["ropecol"] + T], (), [r_rope])

    def store_y(tile):
        T = tile["T"]; nb = T // 128
        rms_stats(lambda c: xT[:, c, :T], DC, T, [r_xT], D)
        for dc in range(DC):
            stt(xT[:, dc, :T], xT[:, dc, :T], vec[:, V_GFIN + dc:V_GFIN + dc + 1], rstd[:, :T], ALU.mult, ALU.mult,
                [r_xT, r_vec, r_rstd], [r_xT])
        A = mk_alloc()
        stg, r_stg = A("ystage", nb * D, F32)
        stg = stg.rearrange("p (b d) -> p b d", d=D)
        k = 0
        for b in range(nb):
            for d4 in range(4):
                pt, _, pr = nps()
                for j in range(4):
                    dc = d4 * 4 + j
                    tr(pt[:, j * 128:(j + 1) * 128], xT[:, dc, b * 128:(b + 1) * 128], ident_f[:], [r_xT, r_idf], [pr])
                cp(stg[:, b, d4 * 512:(d4 + 1) * 512], pt[:, :], [pr], [r_stg], eng="act" if k % 2 else "dve")
                k += 1
        dst = (y_p if tile["kind"] == "p" else y_s)[tile["tok0"]:tile["tok0"] + T, :]
        store(dst.rearrange("(b p) d -> p b d", p=128), stg, [r_stg], ())

    def ffn(tile, l):
        st["psn"] = 8
        st["psb"] = 0
        T = tile["T"]; segs = tile["segs"]; nseg = len(segs); L = segs[0]["L"]; E = L + 2
        kind = tile["kind"]
        A = mk_alloc()
        gT, r_gT = A("gT", FC * T, BF16)
        gT = gT.rearrange("p (c t) -> p c t", t=T)
        hx = [[A(f"hx{ab}{i}", nseg * E, F32) for i in range(2)] for ab in range(2)]
        t1 = [[A(f"t1{ab}{i}", T, F32) for i in range(2)] for ab in range(2)]
        sa = [A(f"sa{i}", T, F32) for i in range(2)]
        cst, r_cst = A("cvstage", 8 * 128, F32)
        cst = cst.rearrange("p (g f) -> p g f", f=128)
        w3, r_w3 = A("wslot3", 8192, BF16)
        Wsl.append((w3, r_w3))
        rmsnorm(lambda c: xT[:, c, :T], r_xT, DC, D, T, lambda c: vec[:, V_GFFN + 16 * l + c:V_GFFN + 16 * l + c + 1],
                lambda c: xn[:, c, :T], r_xn)
        ld(cwb[:], cwb_d[l], (), [r_cwb])
        if kind == "s":
            ld(cst[:88, :, :], s_cv[l].rearrange("s t (c f) -> c (s t) f", f=128), (), [r_cst])
            for h2 in range(2):
                pt, _, pr = nps()
                for q in range(4):
                    g = h2 * 4 + q
                    tr(pt[:, q * 88:(q + 1) * 88], cst[:88, g, :], ident_f[:88, :88], [r_cst, r_idf], [pr])
                cp(hcv_s[:, :, 2 * h2:2 * h2 + 2, :].rearrange("p c s t -> p (s t) c") if False else
                   hcv_s[:, :, 2 * h2:2 * h2 + 2, :],
                   pt[:, :4 * 88].rearrange("p (s t c) -> p c s t", s=2, t=2), [pr], [r_hcvs])

        def hist_ap(ch):
            return hcv_p[:, l, ch:ch + 1, :] if kind == "p" else hcv_s[:, ch, :, :]

        r_hist = r_hcvp if kind == "p" else r_hcvs
        it = 0
        for ps_i in range(FC // 2):
            def dva(t):
                return t[:, :].rearrange("p (k n) -> p k n", n=512)[:, :, 0:256]

            def dvb(t):
                return t[:, :].rearrange("p (k n) -> p k n", n=512)[:, :, 256:512]
            wt, wr = wload([(lambda t: t[:, 0:8192], w_up[l, ps_i])], key=f"up{l}_{ps_i}")
            wv = wt[:, :].rearrange("p (k n) -> p k n", n=512)
            for jj in range(2):
                j = 2 * ps_i + jj
                buf = it % 2
                it += 1
                tv = []
                for ab in range(2):
                    ch = j + ab * FC
                    pt, _, pr = nps()
                    for kc in range(DC):
                        mm(pt[:, :T], wv[:, kc, ab * 256 + jj * 128:ab * 256 + (jj + 1) * 128], xn[:, kc, :T],
                           kc == 0, kc == DC - 1, [wr, r_xn[kc]], [pr])
                    hxt, hxr = hx[ab][buf]
                    hxv = hxt.rearrange("p (g e) -> p g e", e=E)
                    act(hxv[:, :, 2:E], pt[:, :T].rearrange("p (g l) -> p g l", l=L), AF.Copy, [pr], [hxr])
                    cp(hxv[:, :, 0:2], hist_ap(ch), [r_hist], [hxr])
                    cp(hist_ap(ch), hxv[:, :, L:L + 2], [hxr], [r_hist])
                    tt_, tr_ = t1[ab][buf]
                    t3 = tt_.rearrange("p (g l) -> p g l", l=L)
                    w0 = cwb[:, ch:ch + 1]; w1 = cwb[:, 88 + ch:88 + ch + 1]; w2 = cwb[:, 176 + ch:176 + ch + 1]
                    bb = cwb[:, 264 + ch:264 + ch + 1]
                    ts(t3, hxv[:, :, 2:E], w2, bb, ALU.mult, ALU.add, [hxr, r_cwb], [tr_])
                    stt(t3, hxv[:, :, 1:E - 1], w1, t3, ALU.mult, ALU.add, [hxr, r_cwb, tr_], [tr_])
                    stt(t3, hxv[:, :, 0:L], w0, t3, ALU.mult, ALU.add, [hxr, r_cwb, tr_], [tr_])
                    tv.append((tt_, tr_))
                sat, sar = sa[buf]
                act(sat[:, :T], tv[0][0][:, :T], AF.Silu, [tv[0][1]], [sar])
                tt(gT[:, j, :], sat[:, :T], tv[1][0][:, :T], ALU.mult, [sar, tv[1][1]], [r_gT])
        chk("f1")
        if kind == "s" or tile["last"]:
            ng = 8 if kind == "s" else 2
            for h2 in range((ng + 3) // 4):
                pt, _, pr = nps()
                nq = min(4, ng - 4 * h2)
                for q in range(nq):
                    g = h2 * 4 + q
                    src = hcv_s[:, :, g // 2, g % 2] if kind == "s" else hcv_p[:, l, :, g]
                    tr(pt[:88, q * 128:(q + 1) * 128], src, ident_f[:], [r_hist, r_idf], [pr])
                cp(cst[:88, 4 * h2:4 * h2 + nq, :], pt[:88, :nq * 128].rearrange("p (g f) -> p g f", f=128), [pr], [r_cst])
            if kind == "s":
                store(cv_s[l].rearrange("s t (c f) -> c (s t) f", f=128), cst[:88, :, :], [r_cst], ())
            else:
                store(cv_p[l].rearrange("t (c f) -> c t f", f=128), cst[:88, 0:2, :], [r_cst], ())
        for dm in range(DC):
            def dv0(t):
                return t[:, :FC * 128].rearrange("p (c n) -> p c n", n=128)[:, 0:22, :]

            def dv1(t):
                return t[:, :FC * 128].rearrange("p (c n) -> p c n", n=128)[:, 22:44, :]
            wt, wr = wload([(lambda t: t[:, 0:FC * 128], w_down[l, dm])], key=f"dn{l}_{dm}", n_used=FC * 128)
            wv = wt[:, :FC * 128].rearrange("p (c n) -> p c n", n=128)
            pt, _, pr = nps()
            for fc in range(FC):
                mm(pt[:, :T], wv[:, fc, :], gT[:, fc, :], fc == 0, fc == FC - 1, [wr, r_gT], [pr])
            tt(xT[:, dm, :T], xT[:, dm, :T], pt[:, :T], ALU.add, [r_xT, pr], [r_xT])
        Wsl.pop()
        st["w"] = st["w"] % 2

    def run_tile(tile):
        st["tile"] = st.get("tile", -1) + 1
        load_x(tile)
        for l in range(n_layers):
            T = tile["T"]
            rmsnorm(lambda c: xT[:, c, :T], r_xT, DC, D, T, lambda c: vec[:, V_GMIX + 16 * l + c:V_GMIX + 16 * l + c + 1],
                    lambda c: xn[:, c, :T], r_xn)
            if l % 2 == 0:
                even_mixer(tile, l // 2)
            else:
                odd_mixer(tile, l // 2)
            chk("mix%d" % l, xT[:, :, :].rearrange("p c t -> p (c t)"), r_xT)
            ffn(tile, l)
            chk("ffn%d" % l, xT[:, :, :].rearrange("p c t -> p (c t)"), r_xT)
        store_y(tile)

    r_latout = [Res("latout0"), Res("latout1")]
    r_krout = [Res("krout0"), Res("krout1")]

    def even_mixer(tile, i):
        st["psn"] = 8
        st["psb"] = 0
        T = tile["T"]; segs = tile["segs"]; nseg = len(segs); L = segs[0]["L"]; kind = tile["kind"]
        E = 15 + L
        nblk = T // 128 if kind == "p" else nseg
        bn = 128 if kind == "p" else 64
        nkt = 1536 if kind == "p" else 2048
        A = mk_alloc()
        qn, r_qn = A("qn", 8 * T, BF16); qn = qn.rearrange("p (h t) -> p h t", t=T)
        qr, r_qr = A("qr", 8 * T, BF16); qr = qr.rearrange("p (h t) -> p h t", t=T)
        KT, r_KT = A("KT", 5 * nkt, BF16); KT = KT.rearrange("p (c k) -> p c k", k=nkt)
        KVx, r_KV = A("KVx", (nkt // 128) * 576, BF16); KVx = KVx.rearrange("p (b r) -> p b r", r=576)
        nKT, r_nKT = A("nKT", 5 * T, BF16); nKT = nKT.rearrange("p (c t) -> p c t", t=T)
        nKV, r_nKV = A("nKV", nblk * 512, BF16); nKV = nKV.rearrange("p (b r) -> p b r", r=512)
        pbase = A.off[0]

        def load_keys(src_lat, src_kr, n, reads):
            nb = n // 128
            ldp(KVx[:, 0:nb, 0:512], src_lat.rearrange("(b p) r -> p b r", p=128), reads, [r_KV])
            ldp(KVx[:, 0:nb, 512:576], src_kr.rearrange("(b p) r -> p b r", p=128), reads, [r_KV])
            for j in range(nb):
                _, ptb, pr = nps()
                for rc in range(4):
                    tr(ptb[:, rc * 128:(rc + 1) * 128], KVx[:, j, rc * 128:(rc + 1) * 128], ident_b[:], [r_KV, r_idb], [pr])
                tr(ptb[:64, 512:640], KVx[:, j, 512:576], ident_b[:], [r_KV, r_idb], [pr])
                cp(KT[:, 0:4, j * 128:(j + 1) * 128], ptb[:, 0:512].rearrange("p (c k) -> p c k", k=128), [pr], [r_KT],
                   eng="act" if j % 2 else "dve")
                cp(KT[:64, 4, j * 128:(j + 1) * 128], ptb[:64, 512:640], [pr], [r_KT], eng="act" if j % 2 else "dve")

        if kind == "p" and segs[0]["n_prev"] > 0:
            n = segs[0]["n_prev"]
            load_keys(lat_p[i][0:n, :], kr_p[i][0:n, :], n, [r_latout[i], r_krout[i]])
            chk("lkV", KVx[:, :, :].rearrange("p b r -> p (b r)"), r_KV, n=12 * 576)
            chk("lkT", KT[:, :, :].rearrange("p c k -> p (c k)"), r_KT, n=5 * 1536)

        B = mk_alloc(pbase)
        cq, r_cq = B("cq", 4 * T, F32); cq = cq.rearrange("p (c t) -> p c t", t=T)
        cqn, r_cqn = B("cqn", 4 * T, BF16); cqn = cqn.rearrange("p (c t) -> p c t", t=T)
        kpe, r_kpe = B("kpe", T, F32)
        tk1, r_tk1 = B("tk1", T, F32)
        tk2, r_tk2 = B("tk2", T, F32)
        stl, r_stl = B("stl", nblk * 512, F32); stl = stl.rearrange("p (b r) -> p b r", r=512)
        stk, r_stk = B("stk", nblk * 64, F32); stk = stk.rearrange("p (b r) -> p b r", r=64)
        wsrc = w_in_a[i].rearrange("(k p) n -> p k n", p=128)

        def v512(t):
            return t[:, :].rearrange("p (k n) -> p k n", n=512)

        def panel512(c0):
            pi = {0: 0, 512: 1, 1088: 2, 1600: 3}[c0]
            return wload([(lambda t: t[:, 0:8192], w_in_a4[i, pi])], key=f"ina{i}_{c0}")

        def proj4(wv, wr, dst, r_dst):
            for oc in range(4):
                pt, _, pr = nps()
                for kc in range(DC):
                    mm(pt[:, :T], wv[:, kc, oc * 128:(oc + 1) * 128], xn[:, kc, :T], kc == 0, kc == DC - 1, [wr, r_xn[kc]], [pr])
                cp(dst(oc), pt[:, :T], [pr], [r_dst], eng="act" if oc % 2 else "dve")

        def rope(ps_r, pr_r, ps_s, pr_s, out, r_out):
            tt(tk1[:64, :T], ps_r[:64, :T], ropeT[:, 0, :T], ALU.mult, [pr_r, r_rope], [r_tk1])
            tt(tk2[:64, :T], ps_s[:64, :T], ropeT[:, 1, :T], ALU.mult, [pr_s, r_rope], [r_tk2])
            tt(out, tk1[:64, :T], tk2[:64, :T], ALU.add, [r_tk1, r_tk2], [r_out])

        wt, wr = panel512(0)
        proj4(v512(wt), wr, lambda oc: cq[:, oc, :], r_cq)
        rmsnorm(lambda c: cq[:, c, :], r_cq, 4, 512, T, lambda c: vec[:, V_GQA + 4 * i + c:V_GQA + 4 * i + c + 1],
                lambda c: cqn[:, c, :], r_cqn)
        qsrc = w_qb[i].rearrange("(k p) n -> p k n", p=128)
        qsrc4 = w_qb[i].rearrange("(k p) (h e) -> p k h e", p=128, e=192)

        def vq(t):
            return t[:, 0:6144].rearrange("p (k n) -> p k n", n=1536)

        def vqs(t):
            return t[:, 6144:8192].rearrange("p (k h e) -> p k h e", h=8, e=64)
        parts = [(vq, qsrc)]
        for kq in range(4):
            parts.append((lambda t, kq=kq: vqs(t)[:, kq, :, 0:32], qsrc4[:, kq, :, 160:192]))
            parts.append((lambda t, kq=kq: vqs(t)[:, kq, :, 32:64], qsrc4[:, kq, :, 128:160]))
        wt, wr = wload(parts, key=f"qb{i}")
        wq = vq(wt); wqs = vqs(wt)
        for h in range(8):
            pt, _, pr = nps()
            for k in range(4):
                mm(pt[:, :T], wq[:, k, h * 192:h * 192 + 128], cqn[:, k, :], k == 0, k == 3, [wr, r_cqn], [pr])
            cp(qn[:, h, :], pt[:, :T], [pr], [r_qn], eng="act")
            p1, _, pr1 = nps()
            for k in range(4):
                mm(p1[:64, :T], wq[:, k, h * 192 + 128:h * 192 + 192], cqn[:, k, :], k == 0, k == 3, [wr, r_cqn], [pr1])
            p2, _, pr2 = nps()
            for k in range(4):
                mm(p2[:64, :T], wqs[:, k, h, :], cqn[:, k, :], k == 0, k == 3, [wr, r_cqn], [pr2])
            rope(p1, pr1, p2, pr2, qr[:64, h, :], r_qr)
        chk("p1a")
        wt, wr = panel512(512)
        proj4(v512(wt), wr, lambda oc: cq[:, oc, :], r_cq)
        rms_stats(lambda c: cq[:, c, :], 4, T, [r_cq], 512)
        for c in range(4):
            stt(cq[:, c, :], cq[:, c, :], vec[:, V_GKVA + 4 * i + c:V_GKVA + 4 * i + c + 1], rstd[:, :T], ALU.mult, ALU.mult,
                [r_cq, r_vec, r_rstd], [r_cq])
        chk("k1")
        cp(nKT[:, 0:4, :], cq[:, :, :], [r_cq], [r_nKT], eng="act")
        chk("k2")
        for b in range(nblk):
            pt, _, pr = nps()
            for rc in range(4):
                tr(pt[:bn, rc * 128:(rc + 1) * 128], cq[:, rc, b * bn:(b + 1) * bn], ident_f[:], [r_cq, r_idf], [pr])
            cp(stl[:bn, b, :], pt[:bn, :], [pr], [r_stl], eng="act")
            cp(nKV[:bn, b, :], stl[:bn, b, :], [r_stl], [r_nKV], eng="dve")
        chk("k3")
        if kind == "p":
            store(lat_p[i][tile["tok0"]:tile["tok0"] + T, :].rearrange("(b p) r -> p b r", p=128), stl[:, :, :],
                  [r_stl], [r_latout[i]])
        else:
            store(lat_s[i].rearrange("(b p) r -> p b r", p=64), stl[:64, :, :], [r_stl], ())
        chk("p1b")
        ksrc = wsrc

        def vk(t):
            return t[:, 0:1024].rearrange("p (k n) -> p k n", n=64)

        def vks(t):
            return t[:, 1024:2048].rearrange("p (k n) -> p k n", n=64)
        wt, wr = wload([(vk, ksrc[:, :, 1024:1088]), (lambda t: vks(t)[:, :, 0:32], ksrc[:, :, 1056:1088]),
                        (lambda t: vks(t)[:, :, 32:64], ksrc[:, :, 1024:1056])], key=f"kpe{i}", n_used=2048)
        p1, _, pr1 = nps()
        for kc in range(DC):
            mm(p1[:64, :T], vk(wt)[:, kc, :], xn[:, kc, :T], kc == 0, kc == DC - 1, [wr, r_xn[kc]], [pr1])
        p2, _, pr2 = nps()
        for kc in range(DC):
            mm(p2[:64, :T], vks(wt)[:, kc, :], xn[:, kc, :T], kc == 0, kc == DC - 1, [wr, r_xn[kc]], [pr2])
        rope(p1, pr1, p2, pr2, kpe[:64, :T], r_kpe)
        cp(nKT[:64, 4, :], kpe[:64, :T], [r_kpe], [r_nKT], eng="act")
        pt, _, pr = nps()
        for b in range(nblk):
            tr(pt[:bn, b * 64:(b + 1) * 64], kpe[:64, b * bn:(b + 1) * bn], ident_f[:64, :64], [r_kpe, r_idf], [pr])
        cp(stk[:bn, :, :], pt[:bn, :nblk * 64].rearrange("p (b e) -> p b e", e=64), [pr], [r_stk])
        if kind == "p":
            store(kr_p[i][tile["tok0"]:tile["tok0"] + T, :].rearrange("(b p) e -> p b e", p=128), stk[:, :, :],
                  [r_stk], [r_krout[i]])
        else:
            store(kr_s[i].rearrange("(b p) e -> p b e", p=64), stk[:64, :, :], [r_stk], ())

        chk("p1c")
        B = mk_alloc(pbase)
        zx, r_zx = B("zx", 8 * nseg * E, F32); zx = zx.rearrange("p (c g e) -> p c g e", g=nseg, e=E)
        tA, r_tA = B("tA", nseg * E, F32); tA = tA.rearrange("p (g e) -> p g e", e=E)
        tB, r_tB = B("tB", nseg * E, F32); tB = tB.rearrange("p (g e) -> p g e", e=E)
        pT, r_pT = B("pT", 8 * T, BF16); pT = pT.rearrange("p (c t) -> p c t", t=T)
        pst, r_pst = B("pst", 1024, F32)
        if kind == "s":
            zst, r_zst = B("zst", 1024, F32)
        for pz in range(2):
            wt, wr = panel512(1088 + 512 * pz)
            for oc in range(4):
                pt, _, pr = nps()
                for kc in range(DC):
                    mm(pt[:, :T], v512(wt)[:, kc, oc * 128:(oc + 1) * 128], xn[:, kc, :T], kc == 0, kc == DC - 1,
                       [wr, r_xn[kc]], [pr])
                cp(zx[:, pz * 4 + oc, :, 15:E], pt[:, :T].rearrange("p (g l) -> p g l", l=L), [pr], [r_zx],
                   eng="act" if oc % 2 else "dve")
        if kind == "p":
            cp(zx[:, :, 0, 0:15], hpool_p[:, i, :, :], [r_hpool], [r_zx])
        else:
            for s in range(nseg):
                ld(zst[:15, :], s_pool[i, s], (), [r_zst])
                pt, _, pr = nps()
                for zc in range(8):
                    tr(pt[:, zc * 15:(zc + 1) * 15], zst[:15, zc * 128:(zc + 1) * 128], ident_f[:15, :15], [r_zst, r_idf], [pr])
                cp(zx[:, :, s, 0:15], pt[:, :120].rearrange("p (c e) -> p c e", e=15), [pr], [r_zx])
        for zc in range(8):
            gi = zc // 2
            w = 2 << gi
            cur, r_cur = zx[:, zc], r_zx
            d = 1
            k = 0
            while d < w:
                lo = 2 * d - 1
                nxt, r_nxt = (tA, r_tA) if k % 2 == 0 else (tB, r_tB)
                tt(nxt[:, :, lo:E], cur[:, :, lo:E], cur[:, :, lo - d:E - d], ALU.add, [r_cur], [r_nxt])
                cur, r_cur = nxt, r_nxt
                d *= 2
                k += 1
            pv = pT[:, zc, :].rearrange("p (g l) -> p g l", l=L)
            stt(pv, cur[:, :, 15:E], 1.0 / w, zx[:, zc, :, 15:E], ALU.mult, ALU.subtract, [r_cur, r_zx], [r_pT])
            if kind == "p" and tile["first"]:
                tt(pst[:, 0:16], cur[:, 0, 15:31], rcT[:, gi * 16:(gi + 1) * 16], ALU.mult, [r_cur, r_rc], [r_pst])
                tt(pT[:, zc, 0:16], pst[:, 0:16], zx[:, zc, 0, 15:31], ALU.subtract, [r_pst, r_zx], [r_pT])
        if kind == "p":
            cp(hpool_p[:, i, :, :], zx[:, :, 0, L:L + 15], [r_zx], [r_hpool])
        if kind == "s" or tile["last"]:
            for s in range(nseg):
                for h2 in range(2):
                    pt, _, pr = nps()
                    for q in range(4):
                        zc = h2 * 4 + q
                        tr(pt[:15, q * 128:(q + 1) * 128], zx[:, zc, s, L:L + 15], ident_f[:], [r_zx, r_idf], [pr])
                    cp(pst[:15, h2 * 512:(h2 + 1) * 512], pt[:15, :], [pr], [r_pst])
                store(pool_p[i] if kind == "p" else pool_s[i, s], pst[:15, :], [r_pst], ())
        wt, wr = wload([(lambda t: t[:, 0:2048].rearrange("p (g d) -> p g d", d=256),
                         w_pool[i].rearrange("g (cc p) d -> p (g cc) d", p=128))], key=f"pool{i}", n_used=2048)
        wp = wt[:, 0:2048].rearrange("p (g d) -> p g d", d=256)
        for gi in range(4):
            for dch in range(2):
                pt, _, pr = nps()
                for cc in range(2):
                    mm(pt[:, :T], wp[:, gi * 2 + cc, dch * 128:(dch + 1) * 128], pT[:, 2 * gi + cc, :], cc == 0, cc == 1,
                       [wr, r_pT], [pr])
                col = V_PSC + 8 * i + 2 * gi + dch
                ts(xn[:, 8 + 2 * gi + dch, :T], pt[:, :T], vec[:, col:col + 1], None, ALU.mult, None, [pr, r_vec], [r_xn[8 + 2 * gi + dch]])

        chk("p2")
        B = mk_alloc(pbase)
        nsmax = nkt + 64 if kind == "s" else 2048
        Ssb, r_S = B("Ssb", nsmax, F32)
        Sbufs = [(Ssb, r_S)]
        if kind == "p":
            Sbufs.append(B("Ssb2", nsmax, F32))
        r_pm = [Res("pm0"), Res("pm1")]
        Pb, r_P = B("Pb", nsmax, BF16)
        nbmax = nsmax // 128 + (1 if nsmax % 128 else 0)
        PTs, r_PT = B("PTs", nbmax * 128, BF16); PTs = PTs.rearrange("p (b r) -> p b r", r=128)
        QT, r_QT = B("QT", 5 * 512, BF16); QT = QT.rearrange("p (c r) -> p c r", r=512)
        osb, r_osb = B("osb", 512, BF16)
        oT, r_oT = B("oT", 4 * 512, BF16); oT = oT.rearrange("p (c r) -> p c r", r=512)
        if kind == "s":
            oacc, r_oacc = B("oacc", 4 * 512, F32); oacc = oacc.rearrange("p (g r) -> p g r", r=512)
        wt, wr = wload([(lambda t: t[:, 0:4096].rearrange("p (k n) -> p k n", n=1024), w_uk[i].rearrange("(k p) n -> p k n", p=128)),
                        (lambda t: t[:, 4096:8192].rearrange("p (k n) -> p k n", n=1024), w_uv[i].rearrange("(k p) n -> p k n", p=128))],
                       key=f"ukv{i}")
        uk = wt[:, 0:4096].rearrange("p (k n) -> p k n", n=1024)
        uv = wt[:, 4096:8192].rearrange("p (k n) -> p k n", n=1024)
        sB = wslot()
        tB_, rB = Wsl[sB]
        ukT = tB_[:, 0:4096].rearrange("p (h r) -> p h r", r=512)
        for h in range(8):
            _, ptb, pr = nps()
            for rc in range(4):
                tr(ptb[:, rc * 128:(rc + 1) * 128], uk[:, rc, h * 128:(h + 1) * 128], ident_b[:], [wr, r_idb], [pr])
            cp(ukT[:, h, :], ptb[:, 0:512], [pr], [rB], eng="act" if h % 2 else "dve")

        PM, MLOC, NEGB, RSUM, ALPHA, MNEW, RL = 0, 16, 17, 18, 19, 29, 28

        def sc(c):
            return stat[:, c:c + 1]

        def build_q(col):
            for rc in range(4):
                pt, _, pr = nps()
                for h in range(8):
                    mm(pt[:, h * 64:(h + 1) * 64], ukT[:, h, rc * 128:(rc + 1) * 128], qn[:, h, col:col + 64], True, True,
                       [rB, r_qn], [pr])
                cp(QT[:, rc, :], pt[:, :], [pr], [r_QT], eng="act" if rc % 2 else "dve")
            cp(QT[:64, 4, :].rearrange("p (h q) -> p h q", q=64), qr[:64, :, col:col + 64], [r_qr], [r_QT])

        def attend1a(rg, nl, nc0, nv):
            blocks = [(KT, r_KT, k0, min(512, nl - k0), k0) for k0 in range(0, nl, 512)]
            if nv > 0:
                blocks.append((nKT, r_nKT, nc0, nv, nl))
            out = []
            for j, (kt, rkt, k0, n, dcol) in enumerate(blocks):
                pt, _, pr = nps()
                for rc in range(4):
                    mm(pt[:, :n], QT[:, rc, rg * 128:(rg + 1) * 128], kt[:, rc, k0:k0 + n], rc == 0, False, [r_QT, rkt], [pr])
                mm(pt[:, :n], QT[:64, 4, rg * 128:(rg + 1) * 128], kt[:64, 4, k0:k0 + n], False, True, [r_QT, rkt], [pr])
                out.append((pt, pr, n, dcol))
            return out

        def attend1b(buf, blks):
            Ssb, r_S = Sbufs[buf]
            for j, (pt, pr, n, dcol) in enumerate(blks):
                act(Ssb[:, dcol:dcol + n], pt[:, :n], AF.Copy, [pr], [r_S])
                rmax(sc(PM + 8 * buf + j), Ssb[:, dcol:dcol + n], [r_S], [r_pm[buf]])
            return len(blks)

        def attend2(rg, nl, nc0, nv, newblks, sbi, nsb, buf, nblocks, mid=None):
            Ssb, r_S = Sbufs[buf]
            nk = nl + nv
            rmax(sc(MLOC), stat[:, PM + 8 * buf:PM + 8 * buf + nblocks], [r_pm[buf]], [r_stat])
            mrun = sc(20 + rg); lrun = sc(24 + rg)
            if sbi == 0:
                ts(sc(NEGB), sc(MLOC), -MLA_SCALE, None, ALU.mult, None, [r_stat], [r_stat])
                cp(mrun, sc(MLOC), [r_stat], [r_stat])
            else:
                tt(sc(MNEW), mrun, sc(MLOC), ALU.max, [r_stat], [r_stat])
                ts(sc(NEGB), sc(MNEW), -MLA_SCALE, None, ALU.mult, None, [r_stat], [r_stat])
                act(sc(ALPHA), mrun, AF.Exp, [r_stat], [r_stat], bias=sc(NEGB), scale=MLA_SCALE)
                cp(mrun, sc(MNEW), [r_stat], [r_stat])
            act(Pb[:, :nk], Ssb[:, :nk], AF.Exp, [r_S, r_stat], [r_P, r_stat], bias=sc(NEGB), scale=MLA_SCALE, accum=sc(RSUM))
            if sbi == 0:
                cp(lrun, sc(RSUM), [r_stat], [r_stat])
            else:
                stt(lrun, lrun, sc(ALPHA), sc(RSUM), ALU.mult, ALU.add, [r_stat], [r_stat])
            if mid is not None:
                mid()
            kb = [(KVx, r_KV, j, 128, j * 128) for j in range(nl // 128)]
            c0 = nl
            for (b, n) in newblks:
                kb.append((nKV, r_nKV, b, n, c0))
                c0 += n
            for g0 in range(0, len(kb), 8):
                _, ptb, pr = nps()
                grp = kb[g0:g0 + 8]
                for q, (kv, rkv, b, n, pc) in enumerate(grp):
                    tr(ptb[:n, q * 128:(q + 1) * 128], Pb[:, pc:pc + n], ident_b[:], [r_P, r_idb], [pr])
                cp(PTs[:, g0:g0 + len(grp), :], ptb[:, 0:len(grp) * 128].rearrange("p (b r) -> p b r", r=128), [pr], [r_PT],
                   eng="act" if (g0 // 8) % 2 else "dve")
            po, _, pro = nps()
            for j, (kv, rkv, b, n, pc) in enumerate(kb):
                mm(po[:, :], PTs[:n, j, :], kv[:n, b, 0:512], j == 0, j == len(kb) - 1, [r_PT, rkv], [pro])
            last = sbi == nsb - 1
            if nsb == 1:
                recip(sc(RL), lrun, [r_stat], [r_stat])
                ts(osb[:, :], po[:, :], sc(RL), None, ALU.mult, None, [pro, r_stat], [r_osb])
            elif sbi == 0:
                cp(oacc[:, rg, :], po[:, :], [pro], [r_oacc])
            else:
                stt(oacc[:, rg, :], oacc[:, rg, :], sc(ALPHA), po[:, :], ALU.mult, ALU.add, [r_oacc, r_stat, pro], [r_oacc])
                if last:
                    recip(sc(RL), lrun, [r_stat], [r_stat])
                    ts(osb[:, :], oacc[:, rg, :], sc(RL), None, ALU.mult, None, [r_oacc, r_stat], [r_osb])
            if last:
                _, ptb, pr = nps()
                for rc in range(4):
                    tr(ptb[:, rc * 128:(rc + 1) * 128], osb[:, rc * 128:(rc + 1) * 128], ident_b[:], [r_osb, r_idb], [pr])
                cp(oT[:, :, rg * 128:(rg + 1) * 128], ptb[:, 0:512].rearrange("p (c r) -> p c r", r=128), [pr], [r_oT])

        def ymla(col):
            pt, _, pr = nps()
            for h in range(8):
                for rc in range(4):
                    mm(pt[:, h * 64:(h + 1) * 64], uv[:, rc, h * 128:(h + 1) * 128], oT[:, rc, h * 64:(h + 1) * 64],
                       rc == 0, rc == 3, [wr, r_oT], [pr])
            cp(xn[:, 0:8, col:col + 64], pt[:, :].rearrange("p (h q) -> p h q", q=64), [pr], r_xn[0:8])

        if kind == "p":
            nl = segs[0]["n_prev"]
            for c in range(T // 64):
                if c == 0:
                    build_q(0)
                nv = 64 * (c + 1)
                newblks = [(b, min(128, nv - 128 * b)) for b in range((nv + 127) // 128)]
                if c == 0:
                    pend = (0, attend1b(0, attend1a(0, nl, 0, nv)), 0, nv, newblks)
                for rg in range(4):
                    cur = pend
                    nbuf = (cur[2] + 1) % 2
                    nxt = None
                    if rg < 3:
                        nxt = (rg + 1, attend1a(rg + 1, nl, 0, nv), nv, newblks)
                    elif c + 1 < T // 64:
                        build_q(64 * (c + 1))
                        nv2 = 64 * (c + 2)
                        nb2 = [(b, min(128, nv2 - 128 * b)) for b in range((nv2 + 127) // 128)]
                        nxt = (0, attend1a(0, nl, 0, nv2), nv2, nb2)
                    holder = {}

                    def mid(nxt=nxt, nbuf=nbuf, holder=holder):
                        if nxt is not None:
                            holder["p"] = (nxt[0], attend1b(nbuf, nxt[1]), nbuf, nxt[2], nxt[3])
                    attend2(cur[0], nl, 0, cur[3], cur[4], 0, 1, cur[2], cur[1], mid=mid)
                    if nxt is not None:
                        pend = holder["p"]
                ymla(64 * c)
        else:
            for s in range(nseg):
                build_q(64 * s)
                for sbi in range(2):
                    load_keys(c_lat[i, s, 2048 * sbi:2048 * (sbi + 1), :], c_kr[i, s, 2048 * sbi:2048 * (sbi + 1), :], 2048, ())
                    for rg in range(4):
                        if sbi == 0:
                            nbk = attend1b(0, attend1a(rg, 2048, 0, 0))
                            attend2(rg, 2048, 0, 0, [], 0, 2, 0, nbk)
                        else:
                            nbk = attend1b(0, attend1a(rg, 2048, 64 * s, 64))
                            attend2(rg, 2048, 64 * s, 64, [(s, 64)], 1, 2, 0, nbk)
                ymla(64 * s)

        chk("p3", xn[:, :, :].rearrange("p c t -> p (c t)") if T == 512 else None, r_xn)
        for pz in range(4):
            wt, wr = wload([(lambda t: t[:, 0:8192], w_out_a[i, pz])], key=f"outa{i}_{pz}")
            for oc in range(4):
                dc = pz * 4 + oc
                pt, _, pr = nps()
                for kc in range(DC):
                    mm(pt[:, :T], v512(wt)[:, kc, oc * 128:(oc + 1) * 128], xn[:, kc, :T], kc == 0, kc == DC - 1, [wr, r_xn[kc]], [pr])
                tt(xT[:, dc, :T], xT[:, dc, :T], pt[:, :T], ALU.add, [r_xT, pr], [r_xT])

    r_SstS = [Res("SstS0"), Res("SstS1")]

    def odd_mixer(tile, i):
        st["psn"] = 2
        st["psb"] = 4
        T = tile["T"]; segs = tile["segs"]; nseg = len(segs); kind = tile["kind"]
        NCH = T // 32
        A = mk_alloc()
        og, r_og = A("og", 16 * T, BF16); og = og.rearrange("p (h t) -> p h t", t=T)
        Sbf, r_Sbf = A("Sbf", 128, BF16)
        f32t = {}
        for nm in ("qf", "ff", "lf", "bb", "eb", "enb", "kf", "of0", "of1", "sq_", "sg_", "rs0", "rs1"):
            f32t[nm] = A(nm, T, F32)
        b16t = {}
        for nm in ("Qt", "Kt", "Kh", "vb", "gs0", "gs1", "sqo0", "sqo1"):
            b16t[nm] = A(nm, T, BF16)
        vt, r_vt = A("vt", NCH * 128, BF16); vt = vt.rearrange("p (c r) -> p c r", r=128)
        kt, r_kt = A("kt", NCH * 128, BF16); kt = kt.rearrange("p (c r) -> p c r", r=128)
        AmT, r_Am = A("AmT", T, BF16)
        rs2, r_rs2 = A("rs2", T, F32)
        w3, r_w3 = A("wslot3o", 8192, BF16)
        Wsl.append((w3, r_w3))
        lb = lambda h: lbv[:, i, 0, h:h + 1]
        oml = lambda h: lbv[:, i, 1, h:h + 1]
        gon = vec[:, V_GON + i:V_GON + i + 1]

        def v4(t):
            return t[:, :].rearrange("p (a k n) -> p a k n", a=4, n=128)

        def proj_steps(h):
            wt, wr = wload([(lambda t: t[:, 0:8192], w_in_c[i, h])], key=f"inc{i}_{h}")
            wv = v4(wt)
            steps = []
            for a in range(4):
                pt, _, pr = PS[a]
                for k0 in range(0, DC, 4):
                    def stp(a=a, k0=k0, pt=pt, pr=pr):
                        for kc in range(k0, k0 + 4):
                            mm(pt[:, :T], wv[:, a, kc, :], xn[:, kc, :T], kc == 0, kc == DC - 1, [wr, r_xn[kc]], [pr])
                    steps.append(stp)
            return steps

        def gla_head(h, S_ap, r_S, filler):
            qf, r_qf = f32t["qf"]; ff, r_ff = f32t["ff"]; lf, r_lf = f32t["lf"]; bb, r_bb = f32t["bb"]
            eb, r_eb = f32t["eb"]; enb, r_enb = f32t["enb"]; kf, r_kf = f32t["kf"]; of, r_of = f32t["of%d" % (h % 2)]
            Qt, r_Qt = b16t["Qt"]; Kt, r_Kt = b16t["Kt"]; Kh, r_Kh = b16t["Kh"]; vb, r_vb = b16t["vb"]
            gs, r_gs = b16t["gs%d" % (h % 2)]; sqo, r_sqo = b16t["sqo%d" % (h % 2)]
            sq_, r_sq = f32t["sq_"]; sg_, r_sg = f32t["sg_"]; rs2, r_rs2 = f32t["rs%d" % (h % 2)]
            act(sq_[:, :T], PS[0][0][:, :T], AF.Sigmoid, [PS[0][2]], [r_sq])
            act(sg_[:, :T], PS[3][0][:, :T], AF.Sigmoid, [PS[3][2]], [r_sg])
            act(ff[:, :T], PS[1][0][:, :T], AF.Sigmoid, [PS[1][2]], [r_ff])
            act(vb[:, :T], PS[2][0][:, :T], AF.Copy, [PS[2][2]], [r_vb])
            tt(qf[:, :T], PS[0][0][:, :T], sq_[:, :T], ALU.mult, [PS[0][2], r_sq], [r_qf])
            tt(gs[:, :T], PS[3][0][:, :T], sg_[:, :T], ALU.mult, [PS[3][2], r_sg], [r_gs])
            ts(ff[:, :T], ff[:, :T], oml(h), lb(h), ALU.mult, ALU.add, [r_ff, r_lbv], [r_ff])
            ts(ff[:, :T], ff[:, :T], 1e-30, None, ALU.max, None, [r_ff], [r_ff])
            act(lf[:, :T], ff[:, :T], AF.Ln, [r_ff], [r_lf])
            ts(kf[:, :T], ff[:, :T], -1.0, 1.0, ALU.mult, ALU.add, [r_ff], [r_kf])
            S.op("dve", lambda e: e.tensor_tensor_scan(bb[:, :T], scanm[:, :T], lf[:, :T], 0.0, ALU.mult, ALU.add),
                 [r_scanm, r_lf], [r_bb])
            act(eb[:, :T], bb[:, :T], AF.Exp, [r_bb], [r_eb])
            act(enb[:, :T], bb[:, :T], AF.Exp, [r_bb], [r_enb], scale=-1.0)
            tt(Qt[:, :T], qf[:, :T], eb[:, :T], ALU.mult, [r_qf, r_eb], [r_Qt])
            tt(Kt[:, :T], kf[:, :T], enb[:, :T], ALU.mult, [r_kf, r_enb], [r_Kt])
            ebe = eb[:, 31:T:32]
            tt(Kh[:, :T].rearrange("p (c t) -> p c t", t=32), Kt[:, :T].rearrange("p (c t) -> p c t", t=32),
               ebe.unsqueeze(2).to_broadcast([128, NCH, 32]), ALU.mult, [r_Kt, r_eb], [r_Kh])
            for src, r_src, dst, r_dst in ((vb, r_vb, vt, r_vt), (Kh, r_Kh, kt, r_kt)):
                for g0 in range(0, NCH, 8):
                    _, ptb, pr = nps()
                    for q in range(8):
                        j = g0 + q
                        tr(ptb[:32, q * 128:(q + 1) * 128], src[:, j * 32:(j + 1) * 32], ident_b[:], [r_src, r_idb], [pr])
                    cp(dst[:32, g0:g0 + 8, :], ptb[:32, :].rearrange("p (c r) -> p c r", r=128), [pr], [r_dst],
                       eng="act" if (g0 // 8) % 2 else "dve")
            pa, _, pra = nps()
            for j in range(NCH):
                mm(pa[:32, j * 32:(j + 1) * 32], Kt[:, j * 32:(j + 1) * 32], Qt[:, j * 32:(j + 1) * 32], True, True,
                   [r_Kt, r_Qt], [pra])
            tt(AmT[:32, :T], pa[:32, :T], glam[:, :T], ALU.mult, [pra, r_glam], [r_Am])
            po, _, pro = PS[6]
            for j in range(NCH):
                cp(Sbf[:, :], S_ap(j), [r_S], [r_Sbf], eng="act")
                mm(po[:, j * 32:(j + 1) * 32], Sbf[:, :], Qt[:, j * 32:(j + 1) * 32], True, False, [r_Sbf, r_Qt], [pro])
                mm(po[:, j * 32:(j + 1) * 32], vt[:32, j, :], AmT[:32, j * 32:(j + 1) * 32], False, True, [r_vt, r_Am], [pro])
                pS, _, prS = PS[7]
                mm(pS[:, 0:128], kt[:32, j, :], vt[:32, j, :], True, True, [r_kt, r_vt], [prS])
                stt(S_ap(j), S_ap(j), eb[:, j * 32 + 31:j * 32 + 32], pS[:, 0:128], ALU.mult, ALU.add,
                    [r_S, r_eb, prS], [r_S])
                filler(j)
            act(of[:, :T], po[:, :T], AF.Copy, [pro], [r_of])
            act(sqo[:, :T], po[:, :T], AF.Square, [pro], [r_sqo])

            def E2():
                p2, _, pr2 = nps()
                mm(p2[:, :T], ones_b[:], sqo[:, :T], True, True, [r_ones, r_sqo], [pr2])
                act(rs2[:, :T], p2[:, :T], AF.Ln, [pr2, r_eps], [r_rs2], bias=epsT[:, 0:1], scale=1.0 / 128)
                act(rs2[:, :T], rs2[:, :T], AF.Exp, [r_rs2], [r_rs2], scale=-0.5)
                stt(of[:, :T], of[:, :T], gon, rs2[:, :T], ALU.mult, ALU.mult, [r_of, r_vec, r_rs2], [r_of])
                tt(og[:, h, :], of[:, :T], gs[:, :T], ALU.mult, [r_of, r_gs], [r_og])
            return E2

        def run_heads(S_fn, r_S_fn, pre=None, post=None):
            steps = proj_steps(0)
            for s_ in steps:
                s_()
            pend = [None]
            for h in range(16):
                nxt = proj_steps(h + 1) if h + 1 < 16 else []
                per = (len(nxt) + NCH - 1) // NCH if nxt else 0

                def filler(j, nxt=nxt, per=per):
                    for _ in range(per):
                        if nxt:
                            nxt.pop(0)()
                    if j == 1 and pend[0] is not None:
                        pend[0]()
                        pend[0] = None
                if pre is not None:
                    pre(h)
                e2 = gla_head(h, S_fn(h), r_S_fn(h), filler)
                if pend[0] is not None:
                    pend[0]()
                pend[0] = e2
                while nxt:
                    nxt.pop(0)()
                if post is not None:
                    post(h)
            pend[0]()

        if kind == "p":
            run_heads(lambda h: (lambda j, h=h: Sst[:, i, h, :]), lambda h: r_Sst)
            if tile["last"]:
                store(hg_p[i].rearrange("h f v -> f h v"), Sst[:, i, :, :], [r_Sst], ())
        else:
            run_heads(lambda h: (lambda j, hb=h % 2: Sst[:, hb, j // 2, :]), lambda h: r_SstS[h % 2],
                      pre=lambda h: ld(Sst[:, h % 2, 0:4, :], s_hg[i, :, h].rearrange("s f v -> f s v"), (), [r_SstS[h % 2], r_Sst]),
                      post=lambda h: store(hg_s[i, :, h].rearrange("s f v -> f s v"), Sst[:, h % 2, 0:4, :], [r_SstS[h % 2]], ()))
        st["psn"] = 8
        st["psb"] = 0

        def v512(t):
            return t[:, :].rearrange("p (k n) -> p k n", n=512)
        for pz in range(4):
            wt, wr = wload([(lambda t: t[:, 0:8192], w_out_c[i, pz])], key=f"outc{i}_{pz}")
            for oc in range(4):
                dc = pz * 4 + oc
                pt, _, pr = nps()
                for kc in range(DC):
                    mm(pt[:, :T], v512(wt)[:, kc, oc * 128:(oc + 1) * 128], og[:, kc, :], kc == 0, kc == DC - 1, [wr, r_og], [pr])
                tt(xT[:, dc, :T], xT[:, dc, :T], pt[:, :T], ALU.add, [r_xT, pr], [r_xT])
        Wsl.pop()
        st["w"] = st["w"] % 2

    tiles = []
    for i in range(N_PT):
        tiles.append(dict(kind="p", T=T_P, tok0=i * T_P, first=i == 0, last=i == N_PT - 1, ropecol=i * T_P,
                          segs=[dict(seq=0, n_prev=i * T_P, col0=0, L=T_P)]))
    tiles.append(dict(kind="s", T=T_S, tok0=0, first=False, last=True, ropecol=2048,
                      segs=[dict(seq=s, n_prev=4096, col0=64 * s, L=64) for s in range(4)]))
    if tile_sel is not None:
        tiles = [tiles[i] for i in tile_sel]

    try:
        for tile in tiles:
            run_tile(tile)
    except _Stop:
        pass
    S.emit()
    return nc


def _fm(v, n):
    return np.ascontiguousarray(np.asarray(v, np.float32).reshape(n, 128).T)


def _consts():
    f32 = np.float32
    inv = (f32(10000.0) ** (-(np.arange(0, 64, 2, dtype=f32)) / f32(64))).astype(f32)
    pos = np.concatenate([np.arange(2048), np.tile(4096 + np.arange(64), 4)]).astype(f32)
    ang = (pos[:, None] * inv[None, :]).astype(f32)
    cos = np.cos(ang).astype(f32).T
    sin = np.sin(ang).astype(f32).T
    rope = np.stack([np.concatenate([cos, cos], 0), np.concatenate([-sin, sin], 0)], 0)
    rc = np.zeros((4, 16), f32)
    for gi, w in enumerate((2, 4, 8, 16)):
        rc[gi] = 1.0 / np.minimum(np.arange(16) + 1, w)
    rc = np.tile(rc.reshape(1, 64), (128, 1))
    scanm = np.ones((128, 512), f32)
    scanm[:, ::32] = 0.0
    tri = (np.arange(32)[:, None] <= np.arange(32)[None, :]).astype(f32)
    glam = np.tile(tri, (1, 16))
    return dict(ident=np.eye(128, dtype=f32), rope=np.ascontiguousarray(rope, f32), rc=np.ascontiguousarray(rc),
                scanm=scanm, glam=np.ascontiguousarray(glam))


_PROG = {}


def run_cores(inputs, cores, n_layers=NL, tile_sel=None):
    I = {k_: np.asarray(v) for k_, v in inputs.items()}
    key = (n_layers, tuple(tile_sel) if tile_sel is not None else None)
    nc = build_program(n_layers, tile_sel)
    vec = np.zeros((128, NV), np.float32)
    for l in range(4):
        vec[:, V_GMIX + 16 * l:V_GMIX + 16 * l + 16] = _fm(I["g_mix"][l], 16)
        vec[:, V_GFFN + 16 * l:V_GFFN + 16 * l + 16] = _fm(I["g_ffn"][l], 16)
    vec[:, V_GFIN:V_GFIN + 16] = _fm(I["g_final"], 16)
    for i in range(2):
        vec[:, V_GQA + 4 * i:V_GQA + 4 * i + 4] = _fm(I["g_qa"][i], 4)
        vec[:, V_GKVA + 4 * i:V_GKVA + 4 * i + 4] = _fm(I["g_kva"][i], 4)
        vec[:, V_PSC + 8 * i:V_PSC + 8 * i + 8] = _fm(I["pool_scale"][i], 8)
        vec[:, V_LBP + 16 * i:V_LBP + 16 * i + 16] = _fm(I["lb_param"][i], 16)
        vec[:, V_GON + i] = I["g_onorm"][i]
    cwb = np.zeros((4, 128, 4 * 88), np.float32)
    for l in range(4):
        for j in range(3):
            cwb[l, :, 88 * j:88 * j + 88] = _fm(I["conv_w"][l, j], 88)
        cwb[l, :, 264:352] = _fm(I["conv_b"][l], 88)
    shared = dict(vec=vec, cwb=cwb, **_consts())
    for nm in ("w_in_a", "w_qb", "w_pool"):
        shared[nm] = np.ascontiguousarray(I[nm], np.float32)

    def pm(w, cols):
        K = w.shape[0]
        return np.ascontiguousarray(w[:, cols].reshape(K // 128, 128, len(cols)).transpose(1, 0, 2).reshape(128, -1))

    ar = np.arange
    shared["w_up"] = np.stack([np.stack([pm(I["w_up"][l], np.concatenate([ar(256 * p, 256 * p + 256), ar(DFF + 256 * p, DFF + 256 * p + 256)]))
                                         for p in range(22)]) for l in range(4)])
    shared["w_down"] = np.stack([np.stack([pm(I["w_down"][l], ar(128 * d, 128 * d + 128)) for d in range(16)]) for l in range(4)])
    shared["w_in_a4"] = np.stack([np.stack([pm(I["w_in_a"][i], ar(c0, c0 + 512)) for c0 in (0, 512, 1088, 1600)]) for i in range(2)])
    shared["w_out_a"] = np.stack([np.stack([pm(I["w_out_a"][i], ar(512 * p, 512 * p + 512)) for p in range(4)]) for i in range(2)])
    shared["w_out_c"] = np.stack([np.stack([pm(I["w_out_c"][i], ar(512 * p, 512 * p + 512)) for p in range(4)]) for i in range(2)])

    def pm_inc(w, h):
        cols = np.concatenate([ar(2048 * a + 128 * h, 2048 * a + 128 * h + 128) for a in range(4)])
        x = w[:, cols].reshape(16, 128, 4, 128).transpose(1, 2, 0, 3)
        return np.ascontiguousarray(x.reshape(128, -1))
    shared["w_in_c"] = np.stack([np.stack([pm_inc(I["w_in_c"][i], h) for h in range(16)]) for i in range(2)])
    shared["w_uk"] = np.ascontiguousarray(I["w_uk"].reshape(2, 512, 1024), np.float32)
    shared["w_uv"] = np.ascontiguousarray(I["w_uv"].reshape(2, 512, 1024), np.float32)
    in_maps = []
    for c in cores:
        m = dict(shared)
        s4 = slice(4 * c, 4 * c + 4)
        m["xp"] = np.ascontiguousarray(I["x_prompt"][c])
        m["xs"] = np.ascontiguousarray(I["x_sample"][s4].reshape(256, D))
        m["c_lat"] = np.ascontiguousarray(I["cache_mla_latent"][:, s4])
        m["c_kr"] = np.ascontiguousarray(I["cache_mla_krope"][:, s4])
        m["s_pool"] = np.ascontiguousarray(I["state_pool"][:, s4])
        m["s_hg"] = np.ascontiguousarray(I["state_hgrn"][:, s4])
        m["s_cv"] = np.ascontiguousarray(I["state_ffn_conv"][:, s4])
        in_maps.append(m)
    res = run_bass_kernel_spmd(nc, in_maps, core_ids=list(range(len(cores))))
    return res.results


def kernel(**inputs):
    R = run_cores(inputs, list(range(8)))
    f = np.float32
    y_prompt = np.stack([r["y_p"] for r in R], 0).astype(f)
    y_sample = np.concatenate([r["y_s"].reshape(4, 64, D) for r in R], 0).astype(f)
    lat_p = np.stack([r["lat_p"] for r in R], 1).astype(f)
    kr_p = np.stack([r["kr_p"] for r in R], 1).astype(f)
    pool_p = np.stack([r["pool_p"] for r in R], 1).astype(f)
    hg_p = np.stack([r["hg_p"] for r in R], 1).astype(f)
    cv_p = np.stack([r["cv_p"] for r in R], 1).astype(f)
    lat_s = np.concatenate([r["lat_s"].reshape(2, 4, 64, 512) for r in R], 1).astype(f)
    kr_s = np.concatenate([r["kr_s"].reshape(2, 4, 64, 64) for r in R], 1).astype(f)
    pool_s = np.concatenate([r["pool_s"] for r in R], 1).astype(f)
    hg_s = np.concatenate([r["hg_s"] for r in R], 1).astype(f)
    cv_s = np.concatenate([r["cv_s"] for r in R], 1).astype(f)
    return (y_prompt, y_sample, lat_p, kr_p, pool_p, hg_p, cv_p, lat_s, kr_s, pool_s, hg_s, cv_s)
```
